# Optimizing a Trainium2 kernel written in Bass

```python
import jax
import jax.numpy as jnp
from jax import lax
import numpy as np

D_MODEL = 2048
BATCH = 4
SEQ = 2048
DEPTH = 4

CHUNK = 64
Q_BLOCK = 128
ATTN_WIDTH = D_MODEL // 2
ATTN_HEAD_DIM = 128
ATTN_HEADS = ATTN_WIDTH // ATTN_HEAD_DIM
CONV_CH = D_MODEL // 4
DW_CONV_LEN = 31
LRU_WIDTH = D_MODEL // 4
LRU_BLOCKS = 4
LRU_BLOCK_DIM = LRU_WIDTH // LRU_BLOCKS
LRU_CONV_LEN = 4
LRU_C = 8.0
MIX_WIDTH = ATTN_WIDTH + CONV_CH + LRU_WIDTH
IN_COLS = 3 * ATTN_WIDTH + 2 * CONV_CH + 2 * LRU_WIDTH
SPLITS = [ATTN_WIDTH, 2 * ATTN_WIDTH, 3 * ATTN_WIDTH,
          3 * ATTN_WIDTH + CONV_CH, 3 * ATTN_WIDTH + 2 * CONV_CH,
          3 * ATTN_WIDTH + 2 * CONV_CH + LRU_WIDTH]
D_FF = -(-8 * D_MODEL // (3 * 256)) * 256
EPS = 1e-6

kernel_name = "hybrid_stickbreak_conformer_rglru_block"


def rms_norm(x, g):
    xf = x.astype(jnp.float32)
    y = xf * lax.rsqrt(jnp.mean(xf * xf, axis=-1, keepdims=True) + EPS)
    return (y * g.astype(jnp.float32)).astype(x.dtype)


def layer_norm(x, g, b):
    xf = x.astype(jnp.float32)
    mu = jnp.mean(xf, axis=-1, keepdims=True)
    xc = xf - mu
    y = xc * lax.rsqrt(jnp.mean(xc * xc, axis=-1, keepdims=True) + EPS)
    return (y * g.astype(jnp.float32) + b.astype(jnp.float32)).astype(x.dtype)


def causal_depthwise_conv(x, w, b):
    width, ch = w.shape
    y = lax.conv_general_dilated(x, w[:, None, :].astype(x.dtype), window_strides=(1,),
                                 padding=[(width - 1, 0)],
                                 dimension_numbers=('NWC', 'WIO', 'NWC'),
                                 feature_group_count=ch)
    return y + b.astype(x.dtype)


def stick_breaking_attention(q, k, v):
    _, s_len, _, dh = q.shape
    scale = dh ** -0.5
    outs = []
    for blk in range(s_len // Q_BLOCK):
        q0 = blk * Q_BLOCK
        kv_len = q0 + Q_BLOCK
        qb = q[:, q0:kv_len].astype(jnp.float32)
        kb = k[:, :kv_len].astype(jnp.float32)
        vb = v[:, :kv_len].astype(jnp.float32)
        z = jnp.einsum('bqhd,bkhd->bhqk', qb, kb) * scale
        q_pos = q0 + jnp.arange(Q_BLOCK)
        k_pos = jnp.arange(kv_len)
        mask = k_pos[None, :] < q_pos[:, None]
        log_keep = jnp.where(mask, jax.nn.log_sigmoid(-z), 0.0)
        log_later = lax.cumsum(log_keep, axis=3, reverse=True) - log_keep
        log_w = jax.nn.log_sigmoid(z) + log_later
        w = jnp.where(mask, jnp.exp(log_w), 0.0)
        outs.append(jnp.einsum('bhqk,bkhd->bqhd', w, vb))
    return jnp.concatenate(outs, axis=1).astype(v.dtype)


def conformer_conv(c_val, c_gate, dw_w, dw_b, ln_g, ln_b):
    u = c_val * jax.nn.sigmoid(c_gate)
    u = causal_depthwise_conv(u, dw_w, dw_b)
    u = layer_norm(u, ln_g, ln_b)
    return jax.nn.silu(u)


def griffin_recurrent(r_x, r_y, conv_w, conv_b, w_a, b_a, w_i, b_i, lam):
    xr = causal_depthwise_conv(r_x, conv_w, conv_b)
    bsz, s_len, width = xr.shape
    xb = xr.reshape(bsz, s_len, LRU_BLOCKS, LRU_BLOCK_DIM)
    gate_a = jnp.einsum('bsni,nio->bsno', xb, w_a).reshape(bsz, s_len, width) + b_a
    gate_i = jnp.einsum('bsni,nio->bsno', xb, w_i).reshape(bsz, s_len, width) + b_i
    r = jax.nn.sigmoid(gate_a.astype(jnp.float32))
    i = jax.nn.sigmoid(gate_i.astype(jnp.float32))
    log_a = -LRU_C * r * jax.nn.softplus(-lam.astype(jnp.float32))
    a = jnp.exp(log_a)
    b = jnp.sqrt(-jnp.expm1(2.0 * log_a)) * (i * xr.astype(jnp.float32))

    def combine(left, right):
        a1, b1 = left
        a2, b2 = right
        return a1 * a2, a2 * b1 + b2

    _, h = lax.associative_scan(combine, (a, b), axis=1)
    return (h * jax.nn.gelu(r_y.astype(jnp.float32))).astype(r_x.dtype)


def setup_inputs(seed: int = 0) -> dict:
    key = jax.random.key(seed)
    ks = jax.random.split(key, 25)
    f32 = jnp.float32

    def nrm(k, shape, scale):
        return jax.random.normal(k, shape, f32) * scale

    def gain(k, shape):
        return 1.0 + 0.02 * jax.random.normal(k, shape, f32)

    u = jax.random.uniform(ks[20], (DEPTH, LRU_WIDTH), f32, 0.9, 0.999)
    a_base = u ** (1.0 / LRU_C)
    lam = jnp.log(a_base) - jnp.log1p(-a_base)
    return {
        'x': jax.random.normal(ks[0], (BATCH, SEQ, D_MODEL), f32),
        'w_in': nrm(ks[1], (DEPTH, D_MODEL, IN_COLS), D_MODEL ** -0.5),
        'w_out': nrm(ks[2], (DEPTH, MIX_WIDTH, D_MODEL), MIX_WIDTH ** -0.5),
        'g_pre_mix': gain(ks[3], (DEPTH, D_MODEL)),
        'g_post_mix': gain(ks[4], (DEPTH, D_MODEL)),
        'g_pre_ffn': gain(ks[5], (DEPTH, D_MODEL)),
        'g_post_ffn': gain(ks[6], (DEPTH, D_MODEL)),
        'g_attn_grp': gain(ks[7], (DEPTH, ATTN_WIDTH)),
        'g_conv_grp': gain(ks[8], (DEPTH, CONV_CH)),
        'g_lru_grp': gain(ks[9], (DEPTH, LRU_WIDTH)),
        'dw_conv_w': nrm(ks[10], (DEPTH, DW_CONV_LEN, CONV_CH), DW_CONV_LEN ** -0.5),
        'dw_conv_b': nrm(ks[11], (DEPTH, CONV_CH), 0.01),
        'conv_ln_g': gain(ks[12], (DEPTH, CONV_CH)),
        'conv_ln_b': nrm(ks[13], (DEPTH, CONV_CH), 0.01),
        'lru_conv_w': nrm(ks[14], (DEPTH, LRU_CONV_LEN, LRU_WIDTH), LRU_CONV_LEN ** -0.5),
        'lru_conv_b': nrm(ks[15], (DEPTH, LRU_WIDTH), 0.01),
        'lru_w_a': nrm(ks[16], (DEPTH, LRU_BLOCKS, LRU_BLOCK_DIM, LRU_BLOCK_DIM), LRU_BLOCK_DIM ** -0.5),
        'lru_b_a': nrm(ks[17], (DEPTH, LRU_WIDTH), 0.01),
        'lru_w_i': nrm(ks[18], (DEPTH, LRU_BLOCKS, LRU_BLOCK_DIM, LRU_BLOCK_DIM), LRU_BLOCK_DIM ** -0.5),
        'lru_b_i': nrm(ks[19], (DEPTH, LRU_WIDTH), 0.01),
        'lru_lambda': lam,
        'w_gate': nrm(ks[21], (DEPTH, D_MODEL, D_FF), D_MODEL ** -0.5),
        'w_up': nrm(ks[22], (DEPTH, D_MODEL, D_FF), D_MODEL ** -0.5),
        'w_down': nrm(ks[23], (DEPTH, D_FF, D_MODEL), D_FF ** -0.5),
    }


def reference(x, w_in, w_out, g_pre_mix, g_post_mix, g_pre_ffn, g_post_ffn,
              g_attn_grp, g_conv_grp, g_lru_grp, dw_conv_w, dw_conv_b, conv_ln_g, conv_ln_b,
              lru_conv_w, lru_conv_b, lru_w_a, lru_b_a, lru_w_i, lru_b_i, lru_lambda,
              w_gate, w_up, w_down):
    bsz, s_len, _ = x.shape
    heads = (bsz, s_len, ATTN_HEADS, ATTN_HEAD_DIM)
    h = x
    for l in range(DEPTH):
        u = rms_norm(h, g_pre_mix[l])
        proj = jnp.einsum('bsd,dc->bsc', u, w_in[l])
        q, k, v, c_val, c_gate, r_x, r_y = jnp.split(proj, SPLITS, axis=-1)
        y_attn = stick_breaking_attention(q.reshape(heads), k.reshape(heads),
                                          v.reshape(heads)).reshape(bsz, s_len, ATTN_WIDTH)
        y_conv = conformer_conv(c_val, c_gate, dw_conv_w[l], dw_conv_b[l],
                                conv_ln_g[l], conv_ln_b[l])
        y_lru = griffin_recurrent(r_x, r_y, lru_conv_w[l], lru_conv_b[l], lru_w_a[l], lru_b_a[l],
                                  lru_w_i[l], lru_b_i[l], lru_lambda[l])
        mixed = jnp.concatenate([rms_norm(y_attn, g_attn_grp[l]),
                                 rms_norm(y_conv, g_conv_grp[l]),
                                 rms_norm(y_lru, g_lru_grp[l])], axis=-1)
        h = h + rms_norm(jnp.einsum('bsc,cd->bsd', mixed, w_out[l]), g_post_mix[l])
        u = rms_norm(h, g_pre_ffn[l])
        f = jax.nn.silu(jnp.einsum('bsd,df->bsf', u, w_gate[l])) * jnp.einsum('bsd,df->bsf', u, w_up[l])
        h = h + rms_norm(jnp.einsum('bsf,fd->bsd', f, w_down[l]), g_post_ffn[l])
    return h
```

```python
import numpy as np
import ml_dtypes
import concourse.bass as bass
import concourse.mybir as mybir
from concourse.bass_utils import run_bass_kernel_spmd
from contextlib import ExitStack

F32 = mybir.dt.float32
BF16 = mybir.dt.bfloat16
AF = mybir.ActivationFunctionType
ALU = mybir.AluOpType

D = 2048
DC = 16
ATT = 1024
NH = 8
CCH = 512
DFF = 5632
FC = 44
INC = 5120
EPS = 1e-6
VL = 248
NEG = -30000.0


class Rec:
    ENG = ("pe", "act", "dve", "pool", "sp")

    def __init__(self):
        self.q = {e: [] for e in self.ENG}
        self.cnt = {e: 0 for e in self.ENG}
        self.dma_cnt = []
        self.last_dma = {}
        self.bar = {e: [] for e in self.ENG}

    def new_dma_sem(self):
        c = getattr(self, "sem_cursor", len(self.dma_cnt))
        if c >= len(self.dma_cnt):
            self.dma_cnt.append(0)
        self.sem_cursor = c + 1
        return c

    def _deps(self, eng, deps):
        d = []
        for x in deps:
            if x is None:
                continue
            if isinstance(x, list):
                d.extend([y for y in x if y is not None])
            else:
                d.append(x)
        if self.bar[eng]:
            d = d + self.bar[eng]
            self.bar[eng] = []
        return d

    def op(self, eng, fn, deps=(), sig=True):
        tok = None
        if sig:
            self.cnt[eng] += 1
            tok = ("e", eng, self.cnt[eng])
        self.q[eng].append((fn, self._deps(eng, deps), tok))
        return tok

    def dma(self, fn, semidx, deps=(), eng="sp"):
        self.dma_cnt[semidx] += 16
        tok = ("d", semidx, self.dma_cnt[semidx])
        self.q[eng].append((fn, self._deps(eng, deps), tok))
        self.last_dma[semidx] = tok
        return tok

    def barrier(self):
        toks = [("e", e, self.cnt[e]) for e in self.ENG if self.cnt[e] > 0]
        toks += list(self.last_dma.values())
        for e in self.ENG:
            self.bar[e] = list(toks)

    def replay(self, nc):
        with ExitStack() as es:
            esem = {e: es.enter_context(nc.semaphore("sem_" + e)) for e in self.ENG}
            dsem = [es.enter_context(nc.semaphore("dsem%d" % i)) for i in range(len(self.dma_cnt))]
            block = es.enter_context(nc.Block())

            def semof(tok):
                return esem[tok[1]] if tok[0] == "e" else dsem[tok[1]]

            def run(eng, h):
                waited = {}
                for fn, deps, tok in self.q[eng]:
                    for d in deps:
                        key = (d[0], d[1])
                        if waited.get(key, 0) >= d[2]:
                            continue
                        if d[0] == "e" and d[1] == eng and eng == "pe":
                            continue
                        h.wait_ge(semof(d), d[2])
                        waited[key] = d[2]
                    ins = fn(h)
                    if tok is not None:
                        ins.then_inc(semof(tok), 1 if tok[0] == "e" else 16)
                if eng == "sp":
                    for t in self.last_dma.values():
                        if waited.get((t[0], t[1]), 0) < t[2]:
                            h.wait_ge(semof(t), t[2])

            @block.tensor
            def _(h):
                run("pe", h)

            @block.scalar
            def _(h):
                run("act", h)

            @block.vector
            def _(h):
                run("dve", h)

            @block.gpsimd
            def _(h):
                run("pool", h)

            @block.sync
            def _(h):
                run("sp", h)


class Slot:
    def __init__(self, R, ap):
        self.ap = ap
        self.sem = R.new_dma_sem()
        self.free = None
        self.ready = None


def build(T, NL, dbg=False):
    NT = T // 512
    NBK = T // 128
    HALF = min(1024, T)
    NHALF = T // HALF
    HT = HALF // 512
    nc = bass.Bass("TRN2", target_bir_lowering=False)
    R = Rec()
    din = lambda n, s, dt=F32: nc.dram_tensor(n, s, dt, kind="ExternalInput").ap()
    skind = "ExternalOutput" if dbg else "Internal"
    dsc = lambda n, s, dt: nc.dram_tensor(n, s, dt, kind=skind).ap()
    xT = din("xT", [D, T])
    w_in = din("w_in", [NL, D, INC])
    w_out = din("w_out", [NL, D, D])
    w_gate = din("w_gate", [NL, D, DFF])
    w_up = din("w_up", [NL, D, DFF])
    w_down = din("w_down", [NL, DFF, D])
    vecs_d = din("vecs", [128, NL * VL])
    lruw_d = din("lruw", [128, NL * 8 * 128])
    cst_d = din("cst", [128, 128 * 3 + 896])
    outT = nc.dram_tensor("outT", [D, T], F32, kind="ExternalOutput").ap()
    hT_d = dsc("hT_s", [D, T], F32)
    oT_d = dsc("oT_s", [D, T], F32)
    qT_d = dsc("qT_s", [ATT, T], BF16)
    kT_d = dsc("kT_s", [ATT, T], BF16)
    v_d = dsc("v_s", [T, ATT], BF16)
    glu_d = dsc("glu_s", [CCH, T], BF16)
    rx_d = dsc("rx_s", [CCH, T], BF16)
    gry_d = dsc("gry_s", [CCH, T], F32)
    yat_d = dsc("yat_s", [ATT, T], F32)
    fT_d = dsc("fT_s", [DFF, T], BF16)

    with ExitStack() as es:
        def sb(name, shape, dt):
            return es.enter_context(nc.sbuf_tensor(name, shape, dt))

        vecs = sb("vecs_sb", [128, NL * VL], F32)
        ident_f = sb("ident_f", [128, 128], F32)
        ident_b = sb("ident_b", [128, 128], BF16)
        tri_b = sb("tri_b", [128, 128], BF16)
        fix_b = sb("fix_b", [128, 128], BF16)
        mbig = sb("mbig", [128, 896], BF16)
        ones_f = sb("ones_f", [128, 128], F32)
        eps_t = sb("eps_t", [128, 1], F32)
        one_t = sb("one_t", [128, 1], F32)
        lruc = sb("lruc", [128, NL * 8], F32)
        lruw_b = sb("lruw_b", [128, 8 * 128], BF16)
        rstd = sb("rstd", [128, T], F32)
        A = sb("arenaA", [128, 16 * T], BF16)
        XA = sb("arenaX", [128, max(0, 44 * HALF - 16 * T) + 2], BF16)
        WA = sb("arenaW", [128, 66 * 256], F32)
        MI = sb("arenaM", [128, 6 * 1024], F32)
        ps = [es.enter_context(nc.psum_tensor("ps%d" % i, [128, 512], F32)) for i in range(8)]
        bank_free = [None] * 8

        uT = A[:, :].rearrange("p (c t) -> p c t", c=16)

        def wa_f32(off, n):
            return WA[:, off:off + n]

        def wa_bf16(off_f32, n_bf):
            return WA[:, off_f32:off_f32 + (n_bf + 1) // 2].bitcast(BF16)

        def mi_f32(off, n):
            return MI[:, off:off + n]

        def mi_bf16(off_f32, n_bf):
            return MI[:, off_f32:off_f32 + (n_bf + 1) // 2].bitcast(BF16)

        sem_misc = [R.new_dma_sem() for _ in range(4)]

        t_v = R.dma(lambda h: h.dma_start(out=vecs[:, :], in_=vecs_d[:, :]), sem_misc[0])
        cst = WA[:, 0:128 * 3 + 896]
        t_c = R.dma(lambda h: h.dma_start(out=cst, in_=cst_d[:, :]), sem_misc[1])
        R.op("dve", lambda h: h.tensor_copy(out=ident_b[:, :], in_=cst[:, 0:128]), deps=[t_c])
        R.op("dve", lambda h: h.tensor_copy(out=ident_f[:, :], in_=cst[:, 0:128]))
        R.op("dve", lambda h: h.tensor_copy(out=tri_b[:, :], in_=cst[:, 128:256]))
        R.op("dve", lambda h: h.tensor_copy(out=fix_b[:, :], in_=cst[:, 256:384]))
        R.op("dve", lambda h: h.tensor_copy(out=mbig[:, :], in_=cst[:, 384:384 + 896]))
        R.op("dve", lambda h: h.memset(ones_f[:, :], 1.0))
        R.op("dve", lambda h: h.memset(eps_t[:, :], EPS))
        R.op("dve", lambda h: h.memset(one_t[:, :], 1.0))
        for l in range(NL):
            lam = vecs[:, l * VL + 104:l * VL + 108]
            cc = lruc[:, l * 8:l * 8 + 4]
            c2 = lruc[:, l * 8 + 4:l * 8 + 8]
            ta_ = R.op("act", lambda h, lam=lam, cc=cc: h.activation(out=cc, in_=lam, func=AF.Exp, scale=-1.0), deps=[t_v])
            tb_ = R.op("act", lambda h, cc=cc: h.activation(out=cc, in_=cc, func=AF.Ln, bias=one_t[:, 0:1], scale=1.0),
                       deps=[("e", "dve", R.cnt["dve"]), ta_])
            t1 = R.op("act", lambda h, cc=cc, c2=c2: h.activation(out=c2, in_=cc, func=AF.Copy, scale=-16.0), deps=[tb_])
            R.op("act", lambda h, cc=cc: h.activation(out=cc, in_=cc, func=AF.Copy, scale=-8.0), deps=[t1])
        R.barrier()

        def get_bank(cands):
            b = cands[get_bank.i % len(cands)]
            get_bank.i += 1
            return b
        get_bank.i = 0

        def stats_pass(src_d, nchunks, hslots, sqslots):
            last = None
            for c in range(nchunks):
                hs = hslots[c % 2]
                ld = R.dma(lambda h, hs=hs, c=c: h.dma_start(out=hs.ap, in_=src_d[c * 128:(c + 1) * 128, :]), hs.sem, deps=[hs.free])
                sq = sqslots[c % 2]
                t_sq = R.op("act", lambda h, hs=hs, sq=sq: h.activation(out=sq.ap, in_=hs.ap, func=AF.Square), deps=[ld, sq.free])
                hs.free = t_sq
                for tt in range(NT):
                    last = R.op("pe", lambda h, tt=tt, sq=sq, c=c: h.matmul(ps[tt][:, :], lhsT=ones_f[:, :], rhs=sq.ap[:, tt * 512:(tt + 1) * 512], start=(c == 0), stop=(c == nchunks - 1)),
                                deps=[t_sq, bank_free[tt]] if True else [], sig=(tt == NT - 1))
                sq.free = last
            finish_rstd(last, nchunks * 128, range(NT), 0)

        def finish_rstd(tok, nfeat, banks, col0, width=512):
            t2 = None
            for i, b in enumerate(banks):
                dst = rstd[:, col0 + i * width:col0 + (i + 1) * width]
                t1 = R.op("act", lambda h, b=b, dst=dst: h.activation(out=dst, in_=ps[b][:, 0:width], func=AF.Ln, bias=eps_t[:, 0:1], scale=1.0 / nfeat), deps=[tok])
                bank_free[b] = t1
                t2 = R.op("act", lambda h, dst=dst: h.activation(out=dst, in_=dst, func=AF.Exp, scale=-0.5), deps=[t1])
            return t2

        def build_uT(src_d, gcol0, hslots, rstd_tok):
            toks = []
            for c in range(DC):
                hs = hslots[c % 2]
                ld = R.dma(lambda h, hs=hs, c=c: h.dma_start(out=hs.ap, in_=src_d[c * 128:(c + 1) * 128, :]), hs.sem, deps=[hs.free])
                t = R.op("dve", lambda h, hs=hs, c=c: h.scalar_tensor_tensor(out=uT[:, c, :], in0=hs.ap, scalar=vecs[:, gcol0 + c:gcol0 + c + 1], in1=rstd[:, :], op0=ALU.mult, op1=ALU.mult),
                         deps=[ld, rstd_tok])
                hs.free = t
                toks.append(t)
            return toks[-1]

        class WStream:
            def __init__(self, srcs, nrows, ncols, nstg, nwb=2):
                self.srcs = srcs
                self.n = len(srcs)
                self.nrows, self.ncols = nrows, ncols
                sz = nrows * ncols
                self.stg = [Slot(R, wa_f32(i * sz, sz).rearrange("p (r c) -> p r c", r=nrows)) for i in range(nstg)]
                off = nstg * sz
                self.wb = [Slot(R, wa_bf16(off + i * (sz // 2), sz).rearrange("p (r c) -> p r c", r=nrows)) for i in range(nwb)]
                assert off + nwb * (sz // 2) <= 66 * 256
                self.nstg = nstg
                self.nwb = nwb
                self.L, self.C = {}, {}
                self.used = {}

            def load(self, s):
                if s >= self.n or s in self.L:
                    return
                st = self.stg[s % self.nstg]
                self.L[s] = R.dma(lambda h, st=st, s=s: h.dma_start(out=st.ap, in_=self.srcs[s]), st.sem, deps=[st.free])

            def cast(self, s):
                if s >= self.n or s in self.C:
                    return
                st = self.stg[s % self.nstg]
                wb = self.wb[s % self.nwb]
                self.C[s] = R.op("pool", lambda h, st=st, wb=wb: h.tensor_copy(out=wb.ap, in_=st.ap), deps=[self.L[s], wb.free])
                st.free = self.C[s]

            def start(self):
                for s in range(min(self.nstg, self.n)):
                    self.load(s)
                self.cast(0)

            def begin(self, s):
                self.cast(s + 1)
                return self.wb[s % self.nwb].ap, self.C[s]

            def end(self, s, last_mm_tok):
                self.wb[s % self.nwb].free = last_mm_tok
                self.load(s + self.nstg)

        def colslabs(w2d, col0, ncols, width):
            src = w2d.rearrange("(c p) n -> p c n", p=128)
            return [src[:, :, c:c + width] for c in range(col0, col0 + ncols, width)]

        sem_base = len(R.dma_cnt)

        def layer(l, h_src):
            R.sem_cursor = sem_base
            V0 = l * VL
            last_layer = (l == NL - 1)
            h_dst = outT if last_layer else hT_d

            hsl = [Slot(R, mi_f32(i * T, T)) for i in range(2)]
            sqs = [Slot(R, wa_f32(i * T, T)) for i in range(2)]
            if l == 0:
                stats_pass(h_src, DC, hsl, sqs)
            rstd_tok = ("e", "act", R.cnt["act"])
            build_uT(h_src, V0 + 0, hsl, rstd_tok)
            R.barrier()

            wl = w_in[l]
            order = []
            for s in range(4):
                order.append(("q", s * 256))
            for s in range(4):
                order.append(("k", 1024 + s * 256))
            for s in range(4):
                order.append(("v", 2048 + s * 256))
            for s in range(2):
                order.append(("cg", 3584 + s * 256))
                order.append(("cv", 3072 + s * 256))
            for s in range(2):
                order.append(("rx", 4096 + s * 256))
            for s in range(2):
                order.append(("ry", 4608 + s * 256))
            srcw = wl.rearrange("(c p) n -> p c n", p=128)
            ws = WStream([srcw[:, :, c0:c0 + 256] for _, c0 in order], 16, 256, 3)
            obs = [Slot(R, mi_f32(i * T, T)) for i in range(2)]
            sig = [XA[:, i * 2 * T:(i + 1) * 2 * T].bitcast(F32) for i in range(2)]
            tmpA = mi_f32(2 * T, 512)
            tmpB = mi_f32(2 * T + 512, 512)
            vbuf = Slot(R, XA[:, 4 * T:4 * T + NBK * 256].rearrange("p (n c) -> p n c", n=NBK))
            oi = 0
            sig_tok = [None, None]
            tmp_free = None
            ws.start()
            for s, (kind, c0) in enumerate(order):
                wb, wtok = ws.begin(s)
                lastmm = None
                if kind == "v":
                    vb = vbuf
                    evs = []
                    for tk in range(NBK):
                        b = get_bank([4, 5, 6, 7])
                        for dc in range(DC):
                            lastmm = R.op("pe", lambda h, b=b, dc=dc, tk=tk, wb=wb: h.matmul(ps[b][:, 0:256], lhsT=uT[:, dc, tk * 128:(tk + 1) * 128], rhs=wb[:, dc, :], start=(dc == 0), stop=(dc == DC - 1)),
                                          deps=[wtok, bank_free[b]] if dc == 0 else [], sig=(dc == DC - 1))
                        ev = R.op("act", lambda h, b=b, tk=tk, vb=vb: h.activation(out=vb.ap[:, tk, :], in_=ps[b][:, 0:256], func=AF.Copy), deps=[lastmm, vb.free])
                        bank_free[b] = ev
                        evs.append(ev)
                    vc0 = c0 - 2048
                    vb.free = R.dma(lambda h, vb=vb, vc0=vc0: h.dma_start(out=v_d.rearrange("(n p) c -> p n c", p=128)[:, :, vc0:vc0 + 256], in_=vb.ap), vb.sem, deps=[evs[-1]])
                else:
                    for ch in range(2):
                        ob = obs[oi % 2]
                        oi += 1
                        ob_bf = ob.ap.bitcast(BF16)[:, 0:T]
                        evl = None
                        for tt in range(NT):
                            b = get_bank([4, 5, 6, 7])
                            for dc in range(DC):
                                lastmm = R.op("pe", lambda h, b=b, dc=dc, tt=tt, wb=wb, ch=ch: h.matmul(ps[b][:, :], lhsT=wb[:, dc, ch * 128:(ch + 1) * 128], rhs=uT[:, dc, tt * 512:(tt + 1) * 512], start=(dc == 0), stop=(dc == DC - 1)),
                                              deps=[wtok, bank_free[b]] if dc == 0 else [], sig=(dc == DC - 1))
                            sl = slice(tt * 512, (tt + 1) * 512)
                            P = ps[b]
                            if kind == "q":
                                evl = R.op("act", lambda h, P=P, ob_bf=ob_bf, sl=sl: h.activation(out=ob_bf[:, sl], in_=P[:, :], func=AF.Copy, scale=128.0 ** -0.5), deps=[lastmm, ob.free])
                            elif kind in ("k", "rx"):
                                evl = R.op("act", lambda h, P=P, ob_bf=ob_bf, sl=sl: h.activation(out=ob_bf[:, sl], in_=P[:, :], func=AF.Copy), deps=[lastmm, ob.free])
                            elif kind == "cg":
                                evl = R.op("act", lambda h, P=P, ch=ch, sl=sl: h.activation(out=sig[ch][:, sl], in_=P[:, :], func=AF.Sigmoid), deps=[lastmm, sig_tok[ch]])
                            elif kind == "cv":
                                evl = R.op("dve", lambda h, P=P, ch=ch, sl=sl, ob_bf=ob_bf: h.tensor_tensor(out=ob_bf[:, sl], in0=P[:, :], in1=sig[ch][:, sl], op=ALU.mult), deps=[lastmm, ob.free, sig_ready[ch]])
                                sig_tok[ch] = evl
                            elif kind == "ry":
                                t1 = R.op("act", lambda h, P=P: h.activation(out=tmpA, in_=P[:, :], func=AF.Square), deps=[lastmm, tmp_free])
                                t2 = R.op("dve", lambda h: h.tensor_scalar(out=tmpA, in0=tmpA, scalar1=0.044715, scalar2=1.0, op0=ALU.mult, op1=ALU.add), deps=[t1])
                                t3 = R.op("dve", lambda h, P=P: h.tensor_tensor(out=tmpB, in0=P[:, :], in1=tmpA, op=ALU.mult), deps=[t2])
                                t4 = R.op("act", lambda h: h.activation(out=tmpA, in_=tmpB, func=AF.Sigmoid, scale=1.5957691216), deps=[t3])
                                evl = R.op("dve", lambda h, P=P, ob=ob, sl=sl: h.tensor_tensor(out=ob.ap[:, sl], in0=P[:, :], in1=tmpA, op=ALU.mult), deps=[t4, ob.free])
                                tmp_free = evl
                            bank_free[b] = evl
                        if kind == "cg":
                            if ch == 0:
                                sig_ready = [None, None]
                            sig_ready[ch] = evl
                            oi -= 1
                            continue
                        r0 = (c0 + ch * 128)
                        if kind == "q":
                            dst = qT_d[r0:r0 + 128, :]
                        elif kind == "k":
                            dst = kT_d[r0 - 1024:r0 - 1024 + 128, :]
                        elif kind == "cv":
                            dst = glu_d[r0 - 3072:r0 - 3072 + 128, :]
                        elif kind == "rx":
                            dst = rx_d[r0 - 4096:r0 - 4096 + 128, :]
                        elif kind == "ry":
                            dst = gry_d[r0 - 4608:r0 - 4608 + 128, :]
                        src_ap = ob.ap if kind == "ry" else ob_bf
                        ob.free = R.dma(lambda h, dst=dst, src_ap=src_ap: h.dma_start(out=dst, in_=src_ap), ob.sem, deps=[evl])
                ws.end(s, lastmm)
            R.barrier()

            off = 0
            qs, ks, vs = [], [], []
            for i in range(2):
                qs.append(Slot(R, wa_bf16(off, T))); off += T // 2
                ks.append(Slot(R, wa_bf16(off, T))); off += T // 2
                vs.append(Slot(R, wa_bf16(off, NBK * 128).rearrange("p (n d) -> p n d", n=NBK))); off += NBK * 64
            ebuf = [wa_f32(off + i * 512, 512) for i in range(4)]; off += 4 * 512
            ecbuf = [wa_f32(off + i * 512, 512) for i in range(4)]; off += 4 * 512
            spbuf = [wa_bf16(off + i * 256, 512) for i in range(4)]; off += 4 * 256
            wbuf = [wa_bf16(off + i * 256, 512) for i in range(4)]; off += 4 * 256
            ybuf = [Slot(R, wa_f32(off + i * T, T)) for i in range(2)]; off += 2 * T
            assert off <= 66 * 256, off
            jobs = []
            for hh in range(NH):
                for c in range(NT):
                    nb = 4 * (c + 1)
                    for i in range(nb - 1, -1, -1):
                        jobs.append((hh, c, i, i == nb - 1, i == 0, (i - 4 * c) if i >= 4 * c else -1))
            NJ = len(jobs)
            Ztok, Etok, Ltok, Ttok, Xtok, Mtok, Otok = {}, {}, {}, {}, {}, {}, {}
            head_ld = {}
            PZ = [0, 1]
            PP = [2, 3]
            PO = [4, 5]
            stream_of = {}
            sidx = -1
            for n, jb in enumerate(jobs):
                if jb[3]:
                    sidx += 1
                stream_of[n] = sidx
            y_evac = {}

            def load_head(hh):
                if hh >= NH or hh in head_ld:
                    return
                i = hh % 2
                t1 = R.dma(lambda h, i=i, hh=hh: h.dma_start(out=qs[i].ap, in_=qT_d[hh * 128:(hh + 1) * 128, :]), qs[i].sem, deps=[qs[i].free])
                t2 = R.dma(lambda h, i=i, hh=hh: h.dma_start(out=ks[i].ap, in_=kT_d[hh * 128:(hh + 1) * 128, :]), ks[i].sem, deps=[ks[i].free])
                t3 = R.dma(lambda h, i=i, hh=hh: h.dma_start(out=vs[i].ap, in_=v_d.rearrange("(n p) c -> p n c", p=128)[:, :, hh * 128:(hh + 1) * 128]), vs[i].sem, deps=[vs[i].free])
                head_ld[hh] = [t1, t2, t3]

            def stageZ(n):
                if n >= NJ:
                    return
                hh, c, i, first, last, dg = jobs[n]
                hi = hh % 2
                b = PZ[n % 2]
                deps = [bank_free[b]] + head_ld[hh][0:2]
                tk = R.op("pe", lambda h, b=b, hi=hi, i=i, c=c: h.matmul(ps[b][:, :], lhsT=ks[hi].ap[:, i * 128:(i + 1) * 128], rhs=qs[hi].ap[:, c * 512:(c + 1) * 512], start=True, stop=(dg < 0)),
                          deps=deps, sig=(dg < 0))
                if dg >= 0:
                    o0 = 384 - 128 * dg
                    tk = R.op("pe", lambda h, b=b, o0=o0: h.matmul(ps[b][:, :], lhsT=ident_b[:, :], rhs=mbig[:, o0:o0 + 512], start=False, stop=True))
                Ztok[n] = tk
                eb, spb = ebuf[n % 4], spbuf[n % 4]
                Etok[n] = R.op("act", lambda h, b=b, eb=eb: h.activation(out=eb, in_=ps[b][:, :], func=AF.Exp), deps=[tk, Mtok.get(n - 4)])
                bank_free[b] = Etok[n]
                Ltok[n] = R.op("act", lambda h, eb=eb, spb=spb: h.activation(out=spb, in_=eb, func=AF.Ln, bias=one_t[:, 0:1], scale=1.0), deps=[Etok[n], Ttok.get(n - 4), Ftok.get(n - 4)])

            Ftok = {}

            def stageT(n):
                if n >= NJ:
                    return
                hh, c, i, first, last, dg = jobs[n]
                b = PP[stream_of[n] % 2]
                spb = spbuf[n % 4]
                Ttok[n] = R.op("pe", lambda h, b=b, spb=spb, first=first: h.matmul(ps[b][:, :], lhsT=tri_b[:, :], rhs=spb, start=first, stop=True, skip_group_check=True),
                               deps=[Ltok[n], Xtok.get(n - 1), bank_free[b] if first else None])

            def stageF(n):
                hh, c, i, first, last, dg = jobs[n]
                if last:
                    return
                b = PP[stream_of[n] % 2]
                spb = spbuf[n % 4]
                Ftok[n] = R.op("pe", lambda h, b=b, spb=spb: h.matmul(ps[b][:, :], lhsT=fix_b[:, :], rhs=spb, start=False, stop=True, skip_group_check=True),
                               deps=[Xtok[n]])

            def stageX(n):
                if n >= NJ:
                    return
                hh, c, i, first, last, dg = jobs[n]
                b = PP[stream_of[n] % 2]
                ecb = ecbuf[n % 4]
                Xtok[n] = R.op("act", lambda h, b=b, ecb=ecb: h.activation(out=ecb, in_=ps[b][:, :], func=AF.Exp, scale=-1.0), deps=[Ttok[n], Mtok.get(n - 4)])
                if last:
                    bank_free[b] = Xtok[n]

            def stageM(n):
                eb, ecb, wbf = ebuf[n % 4], ecbuf[n % 4], wbuf[n % 4]
                Mtok[n] = R.op("dve", lambda h, eb=eb, ecb=ecb, wbf=wbf: h.tensor_tensor(out=wbf, in0=eb, in1=ecb, op=ALU.mult), deps=[Xtok[n], Etok[n], Otok.get(n - 4)])

            def stageO(n):
                hh, c, i, first, last, dg = jobs[n]
                hi = hh % 2
                b = PO[stream_of[n] % 2]
                wbf = wbuf[n % 4]
                Otok[n] = R.op("pe", lambda h, b=b, hi=hi, i=i, wbf=wbf, first=first, last=last: h.matmul(ps[b][:, :], lhsT=vs[hi].ap[:, i, :], rhs=wbf, start=first, stop=last),
                               deps=[Mtok[n], head_ld[hh][2], bank_free[b] if first else None])
                if last:
                    yb = ybuf[hi]
                    ev = R.op("dve", lambda h, b=b, yb=yb, c=c: h.tensor_copy(out=yb.ap[:, c * 512:(c + 1) * 512], in_=ps[b][:, :]), deps=[Otok[n], yb.free if c == 0 else None])
                    bank_free[b] = ev
                    if c == NT - 1:
                        yb.free = R.dma(lambda h, yb=yb, hh=hh: h.dma_start(out=yat_d[hh * 128:(hh + 1) * 128, :], in_=yb.ap), yb.sem, deps=[ev])
                        qs[hi].free = Otok[n]
                        ks[hi].free = Otok[n]
                        vs[hi].free = Otok[n]
                        load_head(hh + 2)

            load_head(0)
            load_head(1)
            stageZ(0)
            stageZ(1)
            stageT(0)
            stageX(0)
            for n in range(NJ):
                stageZ(n + 2)
                stageF(n)
                stageT(n + 1)
                stageX(n + 1)
                stageM(n)
                stageO(n)
            R.barrier()

            mixT = uT
            off = 0
            glus = []
            for i in range(2):
                glus.append(Slot(R, wa_bf16(off, T + 32))); off += (T + 32) // 2
            diag = wa_bf16(off, 31 * 128).rearrange("p (k c) -> p k c", k=31); off += 31 * 64
            ycv = wa_f32(off, 4 * T).rearrange("p (j t) -> p j t", j=4); off += 4 * T
            mu = wa_f32(off, T); off += T
            sq2 = [wa_f32(off + i * 512, 512) for i in range(2)]; off += 1024
            assert off <= 66 * 256, off
            for i in range(2):
                glus[i].free = R.op("dve", lambda h, i=i: h.memset(glus[i].ap[:, 0:32], 0.0))
            diag_free = None
            last_cv = None
            for j in range(4):
                gs = glus[j % 2]
                ld = R.dma(lambda h, gs=gs, j=j: h.dma_start(out=gs.ap[:, 32:32 + T], in_=glu_d[j * 128:(j + 1) * 128, :]), gs.sem, deps=[gs.free])
                dts = None
                for k in range(31):
                    wc = vecs[:, V0 + 108 + j * 31 + k:V0 + 108 + j * 31 + k + 1]
                    dts = R.op("pool", lambda h, k=k, wc=wc: h.tensor_scalar(out=diag[:, k, :], in0=ident_f[:, :], scalar1=wc, scalar2=None, op0=ALU.mult), deps=[diag_free] if k == 0 else [])
                lm = None
                for tt in range(NT):
                    b = get_bank([0, 1, 2, 3])
                    for k in range(31):
                        s0 = 32 + tt * 512 - 30 + k
                        lm = R.op("pe", lambda h, b=b, k=k, s0=s0, gs=gs: h.matmul(ps[b][:, :], lhsT=diag[:, k, :], rhs=gs.ap[:, s0:s0 + 512], start=(k == 0), stop=(k == 30)),
                                  deps=[ld, dts, bank_free[b]] if k == 0 else [], sig=(k == 30))
                    ev = R.op("act", lambda h, b=b, j=j, tt=tt: h.activation(out=ycv[:, j, tt * 512:(tt + 1) * 512], in_=ps[b][:, :], func=AF.Identity, bias=vecs[:, V0 + 80 + j:V0 + 81 + j], scale=1.0), deps=[lm])
                    bank_free[b] = ev
                    last_cv = ev
                gs.free = lm
                diag_free = lm
            for tt in range(NT):
                sl = slice(tt * 512, (tt + 1) * 512)
                b1 = get_bank([4, 5, 6, 7])
                b2 = get_bank([0, 1, 2, 3])
                t_s1 = t_s2 = None
                for j in range(4):
                    t_s1 = R.op("pe", lambda h, b1=b1, j=j, sl=sl: h.matmul(ps[b1][:, :], lhsT=ones_f[:, :], rhs=ycv[:, j, sl], start=(j == 0), stop=(j == 3)), deps=[last_cv, bank_free[b1]] if j == 0 else [], sig=(j == 3))
                for j in range(4):
                    sq = sq2[j % 2]
                    tq = R.op("dve", lambda h, sq=sq, j=j, sl=sl: h.tensor_tensor(out=sq, in0=ycv[:, j, sl], in1=ycv[:, j, sl], op=ALU.mult), deps=[last_cv, t_s2, ("e", "pe", R.cnt["pe"])])
                    t_s2 = R.op("pe", lambda h, b2=b2, j=j, sq=sq: h.matmul(ps[b2][:, :], lhsT=ones_f[:, :], rhs=sq, start=(j == 0), stop=(j == 3)), deps=[tq, bank_free[b2]] if j == 0 else [tq])
                t_mu = R.op("act", lambda h, b1=b1, sl=sl: h.activation(out=mu[:, sl], in_=ps[b1][:, :], func=AF.Copy, scale=1.0 / 512), deps=[t_s1])
                bank_free[b1] = t_mu
                t_m2 = R.op("dve", lambda h, sl=sl: h.tensor_tensor(out=sq2[0], in0=mu[:, sl], in1=mu[:, sl], op=ALU.mult), deps=[t_mu, t_s2])
                t_var = R.op("dve", lambda h, b2=b2, sl=sl: h.scalar_tensor_tensor(out=rstd[:, sl], in0=ps[b2][:, :], scalar=1.0 / 512, in1=sq2[0], op0=ALU.mult, op1=ALU.subtract), deps=[t_m2])
                bank_free[b2] = t_var
                t_l = R.op("act", lambda h, sl=sl: h.activation(out=rstd[:, sl], in_=rstd[:, sl], func=AF.Ln, bias=eps_t[:, 0:1], scale=1.0), deps=[t_var])
                t_r = R.op("act", lambda h, sl=sl: h.activation(out=rstd[:, sl], in_=rstd[:, sl], func=AF.Exp, scale=-0.5), deps=[t_l])
                b3 = get_bank([0, 1, 2, 3])
                t_g = None
                for j in range(4):
                    ta = R.op("dve", lambda h, j=j, sl=sl: h.tensor_tensor(out=ycv[:, j, sl], in0=ycv[:, j, sl], in1=mu[:, sl], op=ALU.subtract), deps=[t_r, t_s1, t_s2])
                    tb = R.op("dve", lambda h, j=j, sl=sl: h.tensor_tensor(out=ycv[:, j, sl], in0=ycv[:, j, sl], in1=rstd[:, sl], op=ALU.mult), deps=[ta])
                    tc = R.op("act", lambda h, j=j, sl=sl: h.activation(out=ycv[:, j, sl], in_=ycv[:, j, sl], func=AF.Silu, bias=vecs[:, V0 + 88 + j:V0 + 89 + j], scale=vecs[:, V0 + 84 + j:V0 + 85 + j]), deps=[tb])
                    sq = sq2[j % 2]
                    tq = R.op("dve", lambda h, sq=sq, j=j, sl=sl: h.tensor_tensor(out=sq, in0=ycv[:, j, sl], in1=ycv[:, j, sl], op=ALU.mult), deps=[tc, t_g])
                    t_g = R.op("pe", lambda h, b3=b3, j=j, sq=sq: h.matmul(ps[b3][:, :], lhsT=ones_f[:, :], rhs=sq, start=(j == 0), stop=(j == 3)), deps=[tq, bank_free[b3]] if j == 0 else [tq])
                t_l = R.op("act", lambda h, b3=b3, sl=sl: h.activation(out=rstd[:, sl], in_=ps[b3][:, :], func=AF.Ln, bias=eps_t[:, 0:1], scale=1.0 / 512), deps=[t_g])
                bank_free[b3] = t_l
                t_r = R.op("act", lambda h, sl=sl: h.activation(out=rstd[:, sl], in_=rstd[:, sl], func=AF.Exp, scale=-0.5), deps=[t_l])
                for j in range(4):
                    R.op("dve", lambda h, j=j, sl=sl: h.scalar_tensor_tensor(out=mixT[:, 8 + j, sl], in0=ycv[:, j, sl], scalar=vecs[:, V0 + 72 + j:V0 + 73 + j], in1=rstd[:, sl], op0=ALU.mult, op1=ALU.mult), deps=[t_r])
            R.barrier()

            off = 0
            rxs = []
            for i in range(2):
                rxs.append(Slot(R, wa_bf16(off, T + 4))); off += (T + 4) // 2
            dg4 = wa_bf16(off, 4 * 128).rearrange("p (k c) -> p k c", k=4); off += 4 * 64
            xr = wa_f32(off, T); off += T
            xrb = wa_bf16(off, T); off += T // 2
            ga = wa_f32(off, T); off += T
            gi = wa_f32(off, T); off += T
            a2 = wa_f32(off, T); off += T
            ylr = A[:, 0:8 * T].bitcast(F32).rearrange("p (j t) -> p j t", j=4)
            lruw_f = XA[:, 0:2048].bitcast(F32)
            grs = [Slot(R, wa_f32(off + i * T, T)) for i in range(2)]; off += 2 * T
            assert off <= 66 * 256, off
            t_lw = R.dma(lambda h: h.dma_start(out=lruw_f[:, :], in_=lruw_d[:, l * 1024:(l + 1) * 1024]), sem_misc[2])
            t_lwb = R.op("dve", lambda h: h.tensor_copy(out=lruw_b[:, :], in_=lruw_f[:, :]), deps=[t_lw])
            for i in range(2):
                rxs[i].free = R.op("dve", lambda h, i=i: h.memset(rxs[i].ap[:, 0:4], 0.0))
            dfree = None
            grp_last = None
            for j in range(4):
                rs = rxs[j % 2]
                ld = R.dma(lambda h, rs=rs, j=j: h.dma_start(out=rs.ap[:, 4:4 + T], in_=rx_d[j * 128:(j + 1) * 128, :]), rs.sem, deps=[rs.free])
                gr = grs[j % 2]
                ldg = R.dma(lambda h, gr=gr, j=j: h.dma_start(out=gr.ap, in_=gry_d[j * 128:(j + 1) * 128, :]), gr.sem, deps=[gr.free])
                dts = None
                for k in range(4):
                    wc = vecs[:, V0 + 232 + j * 4 + k:V0 + 232 + j * 4 + k + 1]
                    dts = R.op("pool", lambda h, k=k, wc=wc: h.tensor_scalar(out=dg4[:, k, :], in0=ident_f[:, :], scalar1=wc, scalar2=None, op0=ALU.mult), deps=[dfree] if k == 0 else [])
                lm = None
                prev = grp_last
                for tt in range(NT):
                    sl = slice(tt * 512, (tt + 1) * 512)
                    b = get_bank([0, 1, 2, 3])
                    for k in range(4):
                        s0 = 4 + tt * 512 - 3 + k
                        lm = R.op("pe", lambda h, b=b, k=k, s0=s0, rs=rs: h.matmul(ps[b][:, :], lhsT=dg4[:, k, :], rhs=rs.ap[:, s0:s0 + 512], start=(k == 0), stop=(k == 3)),
                                  deps=[ld, dts, bank_free[b]] if k == 0 else [], sig=(k == 3))
                    t_x = R.op("act", lambda h, b=b, sl=sl, j=j: h.activation(out=xr[:, sl], in_=ps[b][:, :], func=AF.Identity, bias=vecs[:, V0 + 92 + j:V0 + 93 + j], scale=1.0), deps=[lm, prev])
                    bank_free[b] = t_x
                    t_xb = R.op("dve", lambda h, sl=sl: h.tensor_copy(out=xrb[:, sl], in_=xr[:, sl]), deps=[t_x, prev])
                    ba = get_bank([4, 5, 6, 7])
                    bi = get_bank([4, 5, 6, 7])
                    t_ga = R.op("pe", lambda h, ba=ba, sl=sl, j=j: h.matmul(ps[ba][:, :], lhsT=lruw_b[:, j * 128:(j + 1) * 128], rhs=xrb[:, sl], start=True, stop=True), deps=[t_xb, t_lwb, bank_free[ba]])
                    t_gi = R.op("pe", lambda h, bi=bi, sl=sl, j=j: h.matmul(ps[bi][:, :], lhsT=lruw_b[:, (4 + j) * 128:(5 + j) * 128], rhs=xrb[:, sl], start=True, stop=True), deps=[t_xb, bank_free[bi]])
                    t_r = R.op("act", lambda h, ba=ba, sl=sl, j=j: h.activation(out=ga[:, sl], in_=ps[ba][:, :], func=AF.Sigmoid, bias=vecs[:, V0 + 96 + j:V0 + 97 + j], scale=1.0), deps=[t_ga, prev])
                    bank_free[ba] = t_r
                    t_i = R.op("act", lambda h, bi=bi, sl=sl, j=j: h.activation(out=gi[:, sl], in_=ps[bi][:, :], func=AF.Sigmoid, bias=vecs[:, V0 + 100 + j:V0 + 101 + j], scale=1.0), deps=[t_gi, prev])
                    bank_free[bi] = t_i
                rs.free = lm
                dfree = lm
                t_a2 = R.op("act", lambda h, j=j: h.activation(out=a2[:, :], in_=ga[:, :], func=AF.Exp, scale=lruc[:, l * 8 + 4 + j:l * 8 + 5 + j]), deps=[t_r, t_i, prev])
                t_a = R.op("act", lambda h, j=j: h.activation(out=ga[:, :], in_=ga[:, :], func=AF.Exp, scale=lruc[:, l * 8 + j:l * 8 + j + 1]), deps=[t_a2])
                t_om = R.op("dve", lambda h: h.tensor_scalar(out=a2[:, :], in0=a2[:, :], scalar1=-1.0, scalar2=1.0, op0=ALU.mult, op1=ALU.add), deps=[t_a2])
                t_om = R.op("dve", lambda h: h.tensor_scalar(out=a2[:, :], in0=a2[:, :], scalar1=1e-30, scalar2=None, op0=ALU.max), deps=[t_om])
                t_ln = R.op("act", lambda h: h.activation(out=a2[:, :], in_=a2[:, :], func=AF.Ln), deps=[t_om, t_a])
                t_sq = R.op("act", lambda h: h.activation(out=a2[:, :], in_=a2[:, :], func=AF.Exp, scale=0.5), deps=[t_ln])
                t_b1 = R.op("dve", lambda h: h.tensor_tensor(out=gi[:, :], in0=gi[:, :], in1=xr[:, :], op=ALU.mult), deps=[t_i, t_x])
                t_b2 = R.op("dve", lambda h: h.tensor_tensor(out=gi[:, :], in0=gi[:, :], in1=a2[:, :], op=ALU.mult), deps=[t_b1, t_sq])
                t_h = R.op("dve", lambda h, j=j: h.tensor_tensor_scan(out=ylr[:, j, :], data0=ga[:, :], data1=gi[:, :], initial=0.0, op0=ALU.mult, op1=ALU.add), deps=[t_b2, t_a])
                t_y = R.op("dve", lambda h, j=j, gr=gr: h.tensor_tensor(out=ylr[:, j, :], in0=ylr[:, j, :], in1=gr.ap, op=ALU.mult), deps=[t_h, ldg])
                gr.free = t_y
                grp_last = t_y
            for tt in range(NT):
                sl = slice(tt * 512, (tt + 1) * 512)
                b3 = get_bank([0, 1, 2, 3])
                t_g = None
                for j in range(4):
                    sqb = (xr if j % 2 == 0 else a2)[:, 0:512]
                    tq = R.op("dve", lambda h, sqb=sqb, j=j, sl=sl: h.tensor_tensor(out=sqb, in0=ylr[:, j, sl], in1=ylr[:, j, sl], op=ALU.mult), deps=[grp_last, t_g])
                    t_g = R.op("pe", lambda h, b3=b3, j=j, sqb=sqb: h.matmul(ps[b3][:, :], lhsT=ones_f[:, :], rhs=sqb, start=(j == 0), stop=(j == 3)), deps=[tq, bank_free[b3]] if j == 0 else [tq])
                t_l = R.op("act", lambda h, b3=b3, sl=sl: h.activation(out=rstd[:, sl], in_=ps[b3][:, :], func=AF.Ln, bias=eps_t[:, 0:1], scale=1.0 / 512), deps=[t_g])
                bank_free[b3] = t_l
                t_r = R.op("act", lambda h, sl=sl: h.activation(out=rstd[:, sl], in_=rstd[:, sl], func=AF.Exp, scale=-0.5), deps=[t_l])
                for j in range(4):
                    R.op("dve", lambda h, j=j, sl=sl: h.scalar_tensor_tensor(out=mixT[:, 12 + j, sl], in0=ylr[:, j, sl], scalar=vecs[:, V0 + 76 + j:V0 + 77 + j], in1=rstd[:, sl], op0=ALU.mult, op1=ALU.mult), deps=[t_r])
            R.barrier()

            hsl = [Slot(R, mi_f32(i * T, T)) for i in range(2)]
            sqs = [Slot(R, wa_f32(i * T, T)) for i in range(2)]
            stats_pass(yat_d, NH, hsl, sqs)
            rstd_tok = ("e", "act", R.cnt["act"])
            for c in range(NH):
                hs = hsl[c % 2]
                ld = R.dma(lambda h, hs=hs, c=c: h.dma_start(out=hs.ap, in_=yat_d[c * 128:(c + 1) * 128, :]), hs.sem, deps=[hs.free])
                hs.free = R.op("dve", lambda h, hs=hs, c=c: h.scalar_tensor_tensor(out=mixT[:, c, :], in0=hs.ap, scalar=vecs[:, V0 + 64 + c:V0 + 65 + c], in1=rstd[:, :], op0=ALU.mult, op1=ALU.mult), deps=[ld, rstd_tok])
            R.barrier()

            def proj_postnorm(ws, nk, rhs_of, tok_tiles, col_of_tile):
                pass

            def out_proj(wmat, kchunks, rhs_fn, tiles, wslabs_builder):
                pass

            srcw = w_out[l].rearrange("(c p) n -> p c n", p=128)
            ws = WStream([srcw[:, :, c0:c0 + 256] for c0 in range(0, D, 256)], 16, 256, 3)
            obs = [Slot(R, mi_f32(i * T, T)) for i in range(2)]
            sqs = [Slot(R, mi_f32(2 * T + i * 512, 512)) for i in range(2)]
            oi = 0
            ws.start()
            ssq_last = None
            nsq = 0
            for s in range(8):
                wb, wtok = ws.begin(s)
                lastmm = None
                for ch in range(2):
                    ob = obs[oi % 2]
                    oi += 1
                    evl = None
                    for tt in range(NT):
                        b = get_bank([4, 5, 6, 7])
                        sl = slice(tt * 512, (tt + 1) * 512)
                        for kc in range(DC):
                            lastmm = R.op("pe", lambda h, b=b, kc=kc, sl=sl, wb=wb, ch=ch: h.matmul(ps[b][:, :], lhsT=wb[:, kc, ch * 128:(ch + 1) * 128], rhs=mixT[:, kc, sl], start=(kc == 0), stop=(kc == DC - 1)),
                                          deps=[wtok, bank_free[b]] if kc == 0 else [], sig=(kc == DC - 1))
                        evl = R.op("act", lambda h, b=b, ob=ob, sl=sl: h.activation(out=ob.ap[:, sl], in_=ps[b][:, :], func=AF.Copy), deps=[lastmm, ob.free])
                        bank_free[b] = evl
                        sq = sqs[nsq % 2]
                        nsq += 1
                        tq = R.op("dve", lambda h, sq=sq, ob=ob, sl=sl: h.tensor_tensor(out=sq.ap, in0=ob.ap[:, sl], in1=ob.ap[:, sl], op=ALU.mult), deps=[evl, sq.free])
                        first = (s == 0 and ch == 0)
                        lastc = (s == 7 and ch == 1)
                        ssq_last = R.op("pe", lambda h, tt=tt, sq=sq, first=first, lastc=lastc: h.matmul(ps[tt][:, :], lhsT=ones_f[:, :], rhs=sq.ap, start=first, stop=lastc, skip_group_check=True), deps=[tq, bank_free[tt]] if first else [tq])
                        sq.free = ssq_last
                    r0 = s * 256 + ch * 128
                    ob.free = R.dma(lambda h, ob=ob, r0=r0: h.dma_start(out=oT_d[r0:r0 + 128, :], in_=ob.ap), ob.sem, deps=[evl, tq])
                ws.end(s, lastmm)
            finish_rstd(ssq_last, D, range(NT), 0)
            R.barrier()

            def residual_pass(hsrc_d, hdst_d, gcol0, want_stats):
                hsl = [Slot(R, mi_f32(i * T, T)) for i in range(2)]
                osl = [Slot(R, wa_f32(i * T, T)) for i in range(2)]
                sqs = [Slot(R, wa_f32(2 * T + i * T, T)) for i in range(2)]
                nrm = wa_f32(4 * T, T)
                rstd_tok = ("e", "act", R.cnt["act"])
                last = None
                t_add_prev = None
                for c in range(DC):
                    hs, os_ = hsl[c % 2], osl[c % 2]
                    ld1 = R.dma(lambda h, hs=hs, c=c: h.dma_start(out=hs.ap, in_=hsrc_d[c * 128:(c + 1) * 128, :]), hs.sem, deps=[hs.free])
                    ld2 = R.dma(lambda h, os_=os_, c=c: h.dma_start(out=os_.ap, in_=oT_d[c * 128:(c + 1) * 128, :]), os_.sem, deps=[os_.free])
                    t_n = R.op("dve", lambda h, os_=os_, c=c: h.scalar_tensor_tensor(out=os_.ap, in0=os_.ap, scalar=vecs[:, gcol0 + c:gcol0 + c + 1], in1=rstd[:, :], op0=ALU.mult, op1=ALU.mult), deps=[ld2, rstd_tok])
                    t_add = R.op("dve", lambda h, os_=os_, hs=hs: h.tensor_tensor(out=hs.ap, in0=hs.ap, in1=os_.ap, op=ALU.add), deps=[t_n, ld1])
                    os_.free = t_add
                    st = R.dma(lambda h, hs=hs, c=c: h.dma_start(out=hdst_d[c * 128:(c + 1) * 128, :], in_=hs.ap), hs.sem, deps=[t_add])
                    hs.free = st
                    if want_stats:
                        sq = sqs[c % 2]
                        t_sq = R.op("act", lambda h, hs=hs, sq=sq: h.activation(out=sq.ap, in_=hs.ap, func=AF.Square), deps=[t_add, sq.free])
                        for tt in range(NT):
                            last = R.op("pe", lambda h, tt=tt, sq=sq, c=c: h.matmul(ps[tt][:, :], lhsT=ones_f[:, :], rhs=sq.ap[:, tt * 512:(tt + 1) * 512], start=(c == 0), stop=(c == DC - 1)),
                                        deps=[t_sq, bank_free[tt], rstd_tok] if c == 0 else [t_sq], sig=(tt == NT - 1))
                        sq.free = last
                        hs.free = [st, t_sq]
                if want_stats:
                    t_fin = finish_rstd_after(last, t_add)
                R.barrier()

            def finish_rstd_after(tok, guard):
                t2 = None
                for tt in range(NT):
                    dst = rstd[:, tt * 512:(tt + 1) * 512]
                    t1 = R.op("act", lambda h, tt=tt, dst=dst: h.activation(out=dst, in_=ps[tt][:, :], func=AF.Ln, bias=eps_t[:, 0:1], scale=1.0 / D), deps=[tok, guard])
                    bank_free[tt] = t1
                    t2 = R.op("act", lambda h, dst=dst: h.activation(out=dst, in_=dst, func=AF.Exp, scale=-0.5), deps=[t1])
                return t2

            residual_pass(h_src, hT_d, V0 + 16, True)

            hsl = [Slot(R, mi_f32(i * T, T)) for i in range(2)]
            rstd_tok = ("e", "act", R.cnt["act"])
            build_uT(hT_d, V0 + 32, hsl, rstd_tok)
            R.barrier()

            srcg = w_gate[l].rearrange("(c p) n -> p c n", p=128)
            srcu = w_up[l].rearrange("(c p) n -> p c n", p=128)
            srcs = []
            for s in range(DFF // 256):
                srcs.append(srcg[:, :, s * 256:(s + 1) * 256])
                srcs.append(srcu[:, :, s * 256:(s + 1) * 256])
            ws = WStream(srcs, 16, 256, 2, nwb=4)
            obs = [Slot(R, mi_bf16(i * (T // 2), T)) for i in range(4)]
            sgb = [mi_f32(2 * T + i * 512, 512) for i in range(4)]
            sg_free = [None] * 4
            ws.load(0)
            ws.load(1)
            ws.cast(0)
            ws.cast(1)
            oi = 0
            nsg = 0
            for s2 in range(DFF // 256):
                sg_, su_ = 2 * s2, 2 * s2 + 1
                wg, gtok = ws.wb[sg_ % 4].ap, ws.C[sg_]
                wu, utok = ws.wb[su_ % 4].ap, ws.C[su_]
                ws.load(sg_ + 2)
                ws.load(su_ + 2)
                ws.cast(sg_ + 2)
                ws.cast(su_ + 2)
                lastmm = None
                for ch in range(2):
                    ob = obs[oi % 4]
                    oi += 1
                    evl = None
                    for tt in range(NT):
                        sl = slice(tt * 512, (tt + 1) * 512)
                        bg = get_bank([4, 5, 6, 7])
                        bu = get_bank([0, 1, 2, 3])
                        for kc in range(DC):
                            tg = R.op("pe", lambda h, bg=bg, kc=kc, sl=sl, wg=wg, ch=ch: h.matmul(ps[bg][:, :], lhsT=wg[:, kc, ch * 128:(ch + 1) * 128], rhs=uT[:, kc, sl], start=(kc == 0), stop=(kc == DC - 1)),
                                      deps=[gtok, bank_free[bg]] if kc == 0 else [], sig=(kc == DC - 1))
                        for kc in range(DC):
                            lastmm = R.op("pe", lambda h, bu=bu, kc=kc, sl=sl, wu=wu, ch=ch: h.matmul(ps[bu][:, :], lhsT=wu[:, kc, ch * 128:(ch + 1) * 128], rhs=uT[:, kc, sl], start=(kc == 0), stop=(kc == DC - 1)),
                                          deps=[utok, bank_free[bu]] if kc == 0 else [], sig=(kc == DC - 1))
                        sg = nsg % 4
                        nsg += 1
                        t_s = R.op("act", lambda h, bg=bg, sg=sg: h.activation(out=sgb[sg], in_=ps[bg][:, :], func=AF.Silu), deps=[tg, sg_free[sg]])
                        bank_free[bg] = t_s
                        evl = R.op("dve", lambda h, bu=bu, sg=sg, ob=ob, sl=sl: h.tensor_tensor(out=ob.ap[:, sl], in0=ps[bu][:, :], in1=sgb[sg], op=ALU.mult), deps=[lastmm, t_s, ob.free])
                        bank_free[bu] = evl
                        sg_free[sg] = evl
                    r0 = s2 * 256 + ch * 128
                    ob.free = R.dma(lambda h, ob=ob, r0=r0: h.dma_start(out=fT_d[r0:r0 + 128, :], in_=ob.ap), ob.sem, deps=[evl])
                ws.wb[sg_ % 4].free = lastmm
                ws.wb[su_ % 4].free = lastmm
            R.barrier()

            fT = None
            srcd = w_down[l].rearrange("(c p) n -> p c n", p=128)
            for hf in range(NHALF):
                t0 = hf * HALF
                nA = (16 * T) // HALF
                def fT_ap(j, lo, hi, nA=nA):
                    if j < nA:
                        return A[:, j * HALF + lo:j * HALF + hi]
                    return XA[:, (j - nA) * HALF + lo:(j - nA) * HALF + hi]
                fsl = Slot(R, None)
                fl = []
                for j in range(FC):
                    fl.append(R.dma(lambda h, j=j, t0=t0: h.dma_start(out=fT_ap(j, 0, HALF), in_=fT_d[j * 128:(j + 1) * 128, t0:t0 + HALF]), fsl.sem if j % 2 == 0 else sem_misc[3]))
                ws = WStream([srcd[:, :, c0:c0 + 128] for c0 in range(0, D, 128)], FC, 128, 2)
                obs = [Slot(R, mi_f32(i * HALF, HALF)) for i in range(2)]
                sqs = [Slot(R, mi_f32(2 * T + i * 512, 512)) for i in range(2)]
                ws.start()
                nsq = 0
                ssq_last = None
                for s in range(DC):
                    wb, wtok = ws.begin(s)
                    ob = obs[s % 2]
                    evl = None
                    lastmm = None
                    for t2 in range(HT):
                        b = get_bank([4, 5, 6, 7])
                        for j in range(FC):
                            lastmm = R.op("pe", lambda h, b=b, j=j, t2=t2, wb=wb: h.matmul(ps[b][:, :], lhsT=wb[:, j, :], rhs=fT_ap(j, t2 * 512, (t2 + 1) * 512), start=(j == 0), stop=(j == FC - 1)),
                                          deps=([wtok, bank_free[b]] + fl) if j == 0 else [], sig=(j == FC - 1))
                        sl = slice(t2 * 512, (t2 + 1) * 512)
                        evl = R.op("act", lambda h, b=b, ob=ob, sl=sl: h.activation(out=ob.ap[:, sl], in_=ps[b][:, :], func=AF.Copy), deps=[lastmm, ob.free])
                        bank_free[b] = evl
                        sq = sqs[nsq % 2]
                        nsq += 1
                        tq = R.op("dve", lambda h, sq=sq, ob=ob, sl=sl: h.tensor_tensor(out=sq.ap, in0=ob.ap[:, sl], in1=ob.ap[:, sl], op=ALU.mult), deps=[evl, sq.free])
                        first = (s == 0)
                        lastc = (s == DC - 1)
                        bs = hf * HT + t2
                        ssq_last = R.op("pe", lambda h, bs=bs, sq=sq, first=first, lastc=lastc: h.matmul(ps[bs][:, :], lhsT=ones_f[:, :], rhs=sq.ap, start=first, stop=lastc, skip_group_check=True), deps=[tq, bank_free[bs]] if first else [tq])
                        sq.free = ssq_last
                    ob.free = R.dma(lambda h, ob=ob, s=s, t0=t0: h.dma_start(out=oT_d[s * 128:(s + 1) * 128, t0:t0 + HALF], in_=ob.ap), ob.sem, deps=[evl, tq])
                    ws.end(s, lastmm)
                finish_rstd(ssq_last, D, [hf * HT + t2 for t2 in range(HT)], t0)
                R.barrier()

            residual_pass(hT_d, h_dst, V0 + 48, not last_layer)

        for l in range(NL):
            layer(l, xT if l == 0 else hT_d)

        R.replay(nc)
    return nc


def _fm(v, n):
    return np.ascontiguousarray(v.reshape(n, 128).T)


def _pack_vecs(inp, layers):
    cols = []
    for l in layers:
        cols += [_fm(inp["g_pre_mix"][l], 16), _fm(inp["g_post_mix"][l], 16), _fm(inp["g_pre_ffn"][l], 16), _fm(inp["g_post_ffn"][l], 16),
                 _fm(inp["g_attn_grp"][l], 8), _fm(inp["g_conv_grp"][l], 4), _fm(inp["g_lru_grp"][l], 4),
                 _fm(inp["dw_conv_b"][l], 4), _fm(inp["conv_ln_g"][l], 4), _fm(inp["conv_ln_b"][l], 4),
                 _fm(inp["lru_conv_b"][l], 4), _fm(inp["lru_b_a"][l], 4), _fm(inp["lru_b_i"][l], 4), _fm(inp["lru_lambda"][l], 4)]
        dw = inp["dw_conv_w"][l]
        cols.append(np.ascontiguousarray(dw.reshape(31, 4, 128).transpose(2, 1, 0).reshape(128, 124)))
        lw = inp["lru_conv_w"][l]
        cols.append(np.ascontiguousarray(lw.reshape(4, 4, 128).transpose(2, 1, 0).reshape(128, 16)))
    return np.ascontiguousarray(np.concatenate(cols, axis=1).astype(np.float32))


def _pack_lruw(inp, layers):
    out = []
    for l in layers:
        wa = inp["lru_w_a"][l]
        wi = inp["lru_w_i"][l]
        both = np.concatenate([wa, wi], axis=0)
        out.append(both.transpose(1, 0, 2).reshape(128, 8 * 128))
    return np.ascontiguousarray(np.concatenate(out, axis=1).astype(np.float32))


def _consts():
    k = np.arange(128)
    ident = np.eye(128, dtype=np.float32)
    tri = (k[:, None] >= k[None, :]).astype(np.float32)
    fix = (k[:, None] < k[None, :]).astype(np.float32)
    c = np.arange(896)
    mb = np.where(k[:, None] >= (c[None, :] - 384), NEG, 0.0).astype(np.float32)
    return np.ascontiguousarray(np.concatenate([ident, tri, fix, mb], axis=1))


_NC_CACHE = {}


def run_layers(hT_list, inp, layers, T, dbg=False):
    key = (T, len(layers), dbg)
    if key not in _NC_CACHE:
        _NC_CACHE[key] = build(T, len(layers), dbg)
    nc = _NC_CACHE[key]
    shared = {
        "w_in": np.ascontiguousarray(inp["w_in"][layers]),
        "w_out": np.ascontiguousarray(inp["w_out"][layers]),
        "w_gate": np.ascontiguousarray(inp["w_gate"][layers]),
        "w_up": np.ascontiguousarray(inp["w_up"][layers]),
        "w_down": np.ascontiguousarray(inp["w_down"][layers]),
        "vecs": _pack_vecs(inp, layers),
        "lruw": _pack_lruw(inp, layers),
        "cst": _consts(),
    }
    in_maps = [dict(shared, xT=np.ascontiguousarray(h)) for h in hT_list]
    res = run_bass_kernel_spmd(nc, in_maps, core_ids=list(range(len(in_maps))))
    return res.results


FUSED = True


def kernel(**inputs):
    inp = {k: np.asarray(v) for k, v in inputs.items()}
    x = inp["x"]
    B, S, _ = x.shape
    NCORE = B
    hT = [np.ascontiguousarray(x[b % B].T) for b in range(NCORE)]
    if FUSED:
        res = run_layers(hT, inp, list(range(4)), S)
    else:
        for l in range(4):
            res = run_layers(hT, inp, [l], S)
            hT = [res[b]["outT"] for b in range(NCORE)]
    out = np.stack([np.ascontiguousarray(res[b]["outT"].T) for b in range(B)], axis=0)
    return out.astype(np.float32)
```

```python
import numpy as np
import ml_dtypes
import concourse.bass as bass
import concourse.mybir as mybir
from concourse.bass_utils import run_bass_kernel_spmd
from contextlib import ExitStack

F32 = mybir.dt.float32
BF16 = mybir.dt.bfloat16
AF = mybir.ActivationFunctionType
ALU = mybir.AluOpType

D = 2048
DC = 16
ATT = 1024
NH = 8
CCH = 512
DFF = 5632
FC = 44
INC = 5120
EPS = 1e-6
VL = 248
NEG = -30000.0


class Rec:
    ENG = ("pe", "act", "dve", "pool", "sp")

    def __init__(self):
        self.q = {e: [] for e in self.ENG}
        self.cnt = {e: 0 for e in self.ENG}
        self.dma_cnt = []
        self.last_dma = {}
        self.bar = {e: [] for e in self.ENG}

    def new_dma_sem(self):
        c = getattr(self, "sem_cursor", len(self.dma_cnt))
        if c >= len(self.dma_cnt):
            self.dma_cnt.append(0)
        self.sem_cursor = c + 1
        return c

    def _deps(self, eng, deps):
        d = []
        for x in deps:
            if x is None:
                continue
            if isinstance(x, list):
                d.extend([y for y in x if y is not None])
            else:
                d.append(x)
        if self.bar[eng]:
            d = d + self.bar[eng]
            self.bar[eng] = []
        return d

    def op(self, eng, fn, deps=(), sig=True):
        tok = None
        if sig:
            self.cnt[eng] += 1
            tok = ("e", eng, self.cnt[eng])
        self.q[eng].append((fn, self._deps(eng, deps), tok))
        return tok

    def dma(self, fn, semidx, deps=(), eng="sp"):
        self.dma_cnt[semidx] += 16
        tok = ("d", semidx, self.dma_cnt[semidx])
        self.q[eng].append((fn, self._deps(eng, deps), tok))
        self.last_dma[semidx] = tok
        return tok

    def barrier(self):
        toks = [("e", e, self.cnt[e]) for e in self.ENG if self.cnt[e] > 0]
        toks += list(self.last_dma.values())
        for e in self.ENG:
            self.bar[e] = list(toks)

    def replay(self, nc):
        with ExitStack() as es:
            esem = {e: es.enter_context(nc.semaphore("sem_" + e)) for e in self.ENG}
            dsem = [es.enter_context(nc.semaphore("dsem%d" % i)) for i in range(len(self.dma_cnt))]
            block = es.enter_context(nc.Block())

            def semof(tok):
                return esem[tok[1]] if tok[0] == "e" else dsem[tok[1]]

            def run(eng, h):
                waited = {}
                for fn, deps, tok in self.q[eng]:
                    for d in deps:
                        key = (d[0], d[1])
                        if waited.get(key, 0) >= d[2]:
                            continue
                        if d[0] == "e" and d[1] == eng and eng == "pe":
                            continue
                        h.wait_ge(semof(d), d[2])
                        waited[key] = d[2]
                    ins = fn(h)
                    if tok is not None:
                        ins.then_inc(semof(tok), 1 if tok[0] == "e" else 16)
                if eng == "sp":
                    for t in self.last_dma.values():
                        if waited.get((t[0], t[1]), 0) < t[2]:
                            h.wait_ge(semof(t), t[2])

            @block.tensor
            def _(h):
                run("pe", h)

            @block.scalar
            def _(h):
                run("act", h)

            @block.vector
            def _(h):
                run("dve", h)

            @block.gpsimd
            def _(h):
                run("pool", h)

            @block.sync
            def _(h):
                run("sp", h)


class Slot:
    def __init__(self, R, ap):
        self.ap = ap
        self.sem = R.new_dma_sem()
        self.free = None
        self.ready = None


def build(T, NL, dbg=False):
    NT = T // 512
    NBK = T // 128
    HALF = min(1024, T)
    NHALF = T // HALF
    HT = HALF // 512
    nc = bass.Bass("TRN2", target_bir_lowering=False)
    R = Rec()
    din = lambda n, s, dt=F32: nc.dram_tensor(n, s, dt, kind="ExternalInput").ap()
    skind = "ExternalOutput" if dbg else "Internal"
    dsc = lambda n, s, dt: nc.dram_tensor(n, s, dt, kind=skind).ap()
    xT = din("xT", [D, T])
    w_in = din("w_in", [NL, D, INC])
    w_out = din("w_out", [NL, D, D])
    w_gate = din("w_gate", [NL, D, DFF])
    w_up = din("w_up", [NL, D, DFF])
    w_down = din("w_down", [NL, DC, 128, FC * 128])
    vecs_d = din("vecs", [128, NL * VL])
    lruw_d = din("lruw", [128, NL * 8 * 128])
    cst_d = din("cst", [128, 128 * 3 + 896])
    outT = nc.dram_tensor("outT", [D, T], F32, kind="ExternalOutput").ap()
    hT_d = dsc("hT_s", [D, T], F32)
    oT_d = dsc("oT_s", [D, T], F32)
    qT_d = dsc("qT_s", [ATT, T], BF16)
    kT_d = dsc("kT_s", [ATT, T], BF16)
    v_d = dsc("v_s", [T, ATT], BF16)
    glu_d = dsc("glu_s", [CCH, T], BF16)
    rx_d = dsc("rx_s", [CCH, T], BF16)
    gry_d = dsc("gry_s", [CCH, T], F32)
    yat_d = dsc("yat_s", [ATT, T], F32)
    fT_d = dsc("fT_s", [DFF, T], BF16)

    with ExitStack() as es:
        def sb(name, shape, dt):
            return es.enter_context(nc.sbuf_tensor(name, shape, dt))

        vecs = sb("vecs_sb", [128, NL * VL], F32)
        ident_f = sb("ident_f", [128, 128], F32)
        ident_b = sb("ident_b", [128, 128], BF16)
        tri_b = sb("tri_b", [128, 128], BF16)
        fix_b = sb("fix_b", [128, 128], BF16)
        mbig = sb("mbig", [128, 896], BF16)
        ones_f = sb("ones_f", [128, 128], F32)
        eps_t = sb("eps_t", [128, 1], F32)
        one_t = sb("one_t", [128, 1], F32)
        lruc = sb("lruc", [128, NL * 8], F32)
        lruw_b = sb("lruw_b", [128, 8 * 128], BF16)
        rstd = sb("rstd", [128, T], F32)
        A = sb("arenaA", [128, 16 * T], BF16)
        XA = sb("arenaX", [128, max(0, 44 * HALF - 16 * T) + 2], BF16)
        WA = sb("arenaW", [128, 66 * 256], F32)
        MI = sb("arenaM", [128, 6 * 1024], F32)
        ps = [es.enter_context(nc.psum_tensor("ps%d" % i, [128, 512], F32)) for i in range(8)]
        bank_free = [None] * 8

        uT = A[:, :].rearrange("p (c t) -> p c t", c=16)

        def wa_f32(off, n):
            return WA[:, off:off + n]

        def wa_bf16(off_f32, n_bf):
            return WA[:, off_f32:off_f32 + (n_bf + 1) // 2].bitcast(BF16)

        def mi_f32(off, n):
            return MI[:, off:off + n]

        def mi_bf16(off_f32, n_bf):
            return MI[:, off_f32:off_f32 + (n_bf + 1) // 2].bitcast(BF16)

        sem_misc = [R.new_dma_sem() for _ in range(4)]

        t_v = R.dma(lambda h: h.dma_start(out=vecs[:, :], in_=vecs_d[:, :]), sem_misc[0])
        cst = WA[:, 0:128 * 3 + 896]
        t_c = R.dma(lambda h: h.dma_start(out=cst, in_=cst_d[:, :]), sem_misc[1])
        R.op("dve", lambda h: h.tensor_copy(out=ident_b[:, :], in_=cst[:, 0:128]), deps=[t_c])
        R.op("dve", lambda h: h.tensor_copy(out=ident_f[:, :], in_=cst[:, 0:128]))
        R.op("dve", lambda h: h.tensor_copy(out=tri_b[:, :], in_=cst[:, 128:256]))
        R.op("dve", lambda h: h.tensor_copy(out=fix_b[:, :], in_=cst[:, 256:384]))
        R.op("dve", lambda h: h.tensor_copy(out=mbig[:, :], in_=cst[:, 384:384 + 896]))
        R.op("dve", lambda h: h.memset(ones_f[:, :], 1.0))
        R.op("dve", lambda h: h.memset(eps_t[:, :], EPS))
        R.op("dve", lambda h: h.memset(one_t[:, :], 1.0))
        for l in range(NL):
            lam = vecs[:, l * VL + 104:l * VL + 108]
            cc = lruc[:, l * 8:l * 8 + 4]
            c2 = lruc[:, l * 8 + 4:l * 8 + 8]
            ta_ = R.op("act", lambda h, lam=lam, cc=cc: h.activation(out=cc, in_=lam, func=AF.Exp, scale=-1.0), deps=[t_v])
            tb_ = R.op("act", lambda h, cc=cc: h.activation(out=cc, in_=cc, func=AF.Ln, bias=one_t[:, 0:1], scale=1.0),
                       deps=[("e", "dve", R.cnt["dve"]), ta_])
            t1 = R.op("act", lambda h, cc=cc, c2=c2: h.activation(out=c2, in_=cc, func=AF.Copy, scale=-16.0), deps=[tb_])
            R.op("act", lambda h, cc=cc: h.activation(out=cc, in_=cc, func=AF.Copy, scale=-8.0), deps=[t1])
        R.barrier()

        def get_bank(cands):
            b = cands[get_bank.i % len(cands)]
            get_bank.i += 1
            return b
        get_bank.i = 0

        def stats_pass(src_d, nchunks, hslots, sqslots):
            last = None
            for c in range(nchunks):
                hs = hslots[c % 2]
                ld = R.dma(lambda h, hs=hs, c=c: h.dma_start(out=hs.ap, in_=src_d[c * 128:(c + 1) * 128, :]), hs.sem, deps=[hs.free])
                sq = sqslots[c % 2]
                t_sq = R.op("act", lambda h, hs=hs, sq=sq: h.activation(out=sq.ap, in_=hs.ap, func=AF.Square), deps=[ld, sq.free])
                hs.free = t_sq
                for tt in range(NT):
                    last = R.op("pe", lambda h, tt=tt, sq=sq, c=c: h.matmul(ps[tt][:, :], lhsT=ones_f[:, :], rhs=sq.ap[:, tt * 512:(tt + 1) * 512], start=(c == 0), stop=(c == nchunks - 1)),
                                deps=[t_sq, bank_free[tt]] if True else [], sig=(tt == NT - 1))
                sq.free = last
            finish_rstd(last, nchunks * 128, range(NT), 0)

        def finish_rstd(tok, nfeat, banks, col0, width=512):
            t2 = None
            for i, b in enumerate(banks):
                dst = rstd[:, col0 + i * width:col0 + (i + 1) * width]
                t1 = R.op("act", lambda h, b=b, dst=dst: h.activation(out=dst, in_=ps[b][:, 0:width], func=AF.Ln, bias=eps_t[:, 0:1], scale=1.0 / nfeat), deps=[tok])
                bank_free[b] = t1
                t2 = R.op("act", lambda h, dst=dst: h.activation(out=dst, in_=dst, func=AF.Exp, scale=-0.5), deps=[t1])
            return t2

        def build_uT(src_d, gcol0, hslots, rstd_tok):
            toks = []
            for c in range(DC):
                hs = hslots[c % 2]
                ld = R.dma(lambda h, hs=hs, c=c: h.dma_start(out=hs.ap, in_=src_d[c * 128:(c + 1) * 128, :]), hs.sem, deps=[hs.free])
                t = R.op("dve", lambda h, hs=hs, c=c: h.scalar_tensor_tensor(out=uT[:, c, :], in0=hs.ap, scalar=vecs[:, gcol0 + c:gcol0 + c + 1], in1=rstd[:, :], op0=ALU.mult, op1=ALU.mult),
                         deps=[ld, rstd_tok])
                hs.free = t
                toks.append(t)
            return toks[-1]

        class WStream:
            def __init__(self, srcs, nrows, ncols, nstg, nwb=2):
                self.srcs = srcs
                self.n = len(srcs)
                self.nrows, self.ncols = nrows, ncols
                sz = nrows * ncols
                self.stg = [Slot(R, wa_f32(i * sz, sz).rearrange("p (r c) -> p r c", r=nrows)) for i in range(nstg)]
                off = nstg * sz
                self.wb = [Slot(R, wa_bf16(off + i * (sz // 2), sz).rearrange("p (r c) -> p r c", r=nrows)) for i in range(nwb)]
                assert off + nwb * (sz // 2) <= 66 * 256
                self.nstg = nstg
                self.nwb = nwb
                self.L, self.C = {}, {}
                self.used = {}

            def load(self, s):
                if s >= self.n or s in self.L:
                    return
                st = self.stg[s % self.nstg]
                self.L[s] = R.dma(lambda h, st=st, s=s: h.dma_start(out=st.ap, in_=self.srcs[s]), st.sem, deps=[st.free])

            def cast(self, s):
                if s >= self.n or s in self.C:
                    return
                st = self.stg[s % self.nstg]
                wb = self.wb[s % self.nwb]
                self.C[s] = R.op("pool", lambda h, st=st, wb=wb: h.tensor_copy(out=wb.ap, in_=st.ap), deps=[self.L[s], wb.free])
                st.free = self.C[s]

            def start(self):
                for s in range(min(self.nstg, self.n)):
                    self.load(s)
                self.cast(0)

            def begin(self, s):
                self.cast(s + 1)
                return self.wb[s % self.nwb].ap, self.C[s]

            def end(self, s, last_mm_tok):
                self.wb[s % self.nwb].free = last_mm_tok
                self.load(s + self.nstg)

        def colslabs(w2d, col0, ncols, width):
            src = w2d.rearrange("(c p) n -> p c n", p=128)
            return [src[:, :, c:c + width] for c in range(col0, col0 + ncols, width)]

        sem_base = len(R.dma_cnt)

        def layer(l, h_src):
            R.sem_cursor = sem_base
            V0 = l * VL
            last_layer = (l == NL - 1)
            h_dst = outT if last_layer else hT_d

            hsl = [Slot(R, mi_f32(i * T, T)) for i in range(2)]
            sqs = [Slot(R, wa_f32(i * T, T)) for i in range(2)]
            if l == 0:
                stats_pass(h_src, DC, hsl, sqs)
            rstd_tok = ("e", "act", R.cnt["act"])
            build_uT(h_src, V0 + 0, hsl, rstd_tok)
            R.barrier()

            wl = w_in[l]
            order = []
            for s in range(4):
                order.append(("q", s * 256))
            for s in range(4):
                order.append(("k", 1024 + s * 256))
            for s in range(4):
                order.append(("v", 2048 + s * 256))
            for s in range(2):
                order.append(("cg", 3584 + s * 256))
                order.append(("cv", 3072 + s * 256))
            for s in range(2):
                order.append(("rx", 4096 + s * 256))
            for s in range(2):
                order.append(("ry", 4608 + s * 256))
            srcw = wl.rearrange("(c p) n -> p c n", p=128)
            ws = WStream([srcw[:, :, c0:c0 + 256] for _, c0 in order], 16, 256, 3)
            obs = [Slot(R, mi_f32(i * T, T)) for i in range(2)]
            sig = [XA[:, i * 2 * T:(i + 1) * 2 * T].bitcast(F32) for i in range(2)]
            tmpA = mi_f32(2 * T, 512)
            tmpB = mi_f32(2 * T + 512, 512)
            vbuf = Slot(R, XA[:, 4 * T:4 * T + NBK * 256].rearrange("p (n c) -> p n c", n=NBK))
            oi = 0
            sig_tok = [None, None]
            tmp_free = None
            ws.start()
            for s, (kind, c0) in enumerate(order):
                wb, wtok = ws.begin(s)
                lastmm = None
                if kind == "v":
                    vb = vbuf
                    evs = []
                    for tk in range(NBK):
                        b = get_bank([4, 5, 6, 7])
                        for dc in range(DC):
                            lastmm = R.op("pe", lambda h, b=b, dc=dc, tk=tk, wb=wb: h.matmul(ps[b][:, 0:256], lhsT=uT[:, dc, tk * 128:(tk + 1) * 128], rhs=wb[:, dc, :], start=(dc == 0), stop=(dc == DC - 1)),
                                          deps=[wtok, bank_free[b]] if dc == 0 else [], sig=(dc == DC - 1))
                        ev = R.op("act", lambda h, b=b, tk=tk, vb=vb: h.activation(out=vb.ap[:, tk, :], in_=ps[b][:, 0:256], func=AF.Copy), deps=[lastmm, vb.free])
                        bank_free[b] = ev
                        evs.append(ev)
                    vc0 = c0 - 2048
                    vb.free = R.dma(lambda h, vb=vb, vc0=vc0: h.dma_start(out=v_d.rearrange("(n p) c -> p n c", p=128)[:, :, vc0:vc0 + 256], in_=vb.ap), vb.sem, deps=[evs[-1]])
                else:
                    for ch in range(2):
                        ob = obs[oi % 2]
                        oi += 1
                        ob_bf = ob.ap.bitcast(BF16)[:, 0:T]
                        evl = None
                        for tt in range(NT):
                            b = get_bank([4, 5, 6, 7])
                            for dc in range(DC):
                                lastmm = R.op("pe", lambda h, b=b, dc=dc, tt=tt, wb=wb, ch=ch: h.matmul(ps[b][:, :], lhsT=wb[:, dc, ch * 128:(ch + 1) * 128], rhs=uT[:, dc, tt * 512:(tt + 1) * 512], start=(dc == 0), stop=(dc == DC - 1)),
                                              deps=[wtok, bank_free[b]] if dc == 0 else [], sig=(dc == DC - 1))
                            sl = slice(tt * 512, (tt + 1) * 512)
                            P = ps[b]
                            if kind == "q":
                                evl = R.op("act", lambda h, P=P, ob_bf=ob_bf, sl=sl: h.activation(out=ob_bf[:, sl], in_=P[:, :], func=AF.Copy, scale=128.0 ** -0.5), deps=[lastmm, ob.free])
                            elif kind in ("k", "rx"):
                                evl = R.op("act", lambda h, P=P, ob_bf=ob_bf, sl=sl: h.activation(out=ob_bf[:, sl], in_=P[:, :], func=AF.Copy), deps=[lastmm, ob.free])
                            elif kind == "cg":
                                evl = R.op("act", lambda h, P=P, ch=ch, sl=sl: h.activation(out=sig[ch][:, sl], in_=P[:, :], func=AF.Sigmoid), deps=[lastmm, sig_tok[ch]])
                            elif kind == "cv":
                                evl = R.op("dve", lambda h, P=P, ch=ch, sl=sl, ob_bf=ob_bf: h.tensor_tensor(out=ob_bf[:, sl], in0=P[:, :], in1=sig[ch][:, sl], op=ALU.mult), deps=[lastmm, ob.free, sig_ready[ch]])
                                sig_tok[ch] = evl
                            elif kind == "ry":
                                t1 = R.op("act", lambda h, P=P: h.activation(out=tmpA, in_=P[:, :], func=AF.Square), deps=[lastmm, tmp_free])
                                t2 = R.op("dve", lambda h: h.tensor_scalar(out=tmpA, in0=tmpA, scalar1=0.044715, scalar2=1.0, op0=ALU.mult, op1=ALU.add), deps=[t1])
                                t3 = R.op("dve", lambda h, P=P: h.tensor_tensor(out=tmpB, in0=P[:, :], in1=tmpA, op=ALU.mult), deps=[t2])
                                t4 = R.op("act", lambda h: h.activation(out=tmpA, in_=tmpB, func=AF.Sigmoid, scale=1.5957691216), deps=[t3])
                                evl = R.op("dve", lambda h, P=P, ob=ob, sl=sl: h.tensor_tensor(out=ob.ap[:, sl], in0=P[:, :], in1=tmpA, op=ALU.mult), deps=[t4, ob.free])
                                tmp_free = evl
                            bank_free[b] = evl
                        if kind == "cg":
                            if ch == 0:
                                sig_ready = [None, None]
                            sig_ready[ch] = evl
                            oi -= 1
                            continue
                        r0 = (c0 + ch * 128)
                        if kind == "q":
                            dst = qT_d[r0:r0 + 128, :]
                        elif kind == "k":
                            dst = kT_d[r0 - 1024:r0 - 1024 + 128, :]
                        elif kind == "cv":
                            dst = glu_d[r0 - 3072:r0 - 3072 + 128, :]
                        elif kind == "rx":
                            dst = rx_d[r0 - 4096:r0 - 4096 + 128, :]
                        elif kind == "ry":
                            dst = gry_d[r0 - 4608:r0 - 4608 + 128, :]
                        src_ap = ob.ap if kind == "ry" else ob_bf
                        ob.free = R.dma(lambda h, dst=dst, src_ap=src_ap: h.dma_start(out=dst, in_=src_ap), ob.sem, deps=[evl])
                ws.end(s, lastmm)
            R.barrier()

            off = 0
            qs, ks, vs = [], [], []
            for i in range(2):
                qs.append(Slot(R, wa_bf16(off, T))); off += T // 2
                ks.append(Slot(R, wa_bf16(off, T))); off += T // 2
                vs.append(Slot(R, wa_bf16(off, NBK * 128).rearrange("p (n d) -> p n d", n=NBK))); off += NBK * 64
            ebuf = [wa_f32(off + i * 512, 512) for i in range(4)]; off += 4 * 512
            ecbuf = [wa_f32(off + i * 512, 512) for i in range(4)]; off += 4 * 512
            spbuf = [wa_bf16(off + i * 256, 512) for i in range(4)]; off += 4 * 256
            wbuf = [wa_bf16(off + i * 256, 512) for i in range(4)]; off += 4 * 256
            ybuf = [Slot(R, wa_f32(off + i * T, T)) for i in range(2)]; off += 2 * T
            assert off <= 66 * 256, off
            jobs = []
            for hh in range(NH):
                for c in range(NT):
                    nb = 4 * (c + 1)
                    for i in range(nb - 1, -1, -1):
                        jobs.append((hh, c, i, i == nb - 1, i == 0, (i - 4 * c) if i >= 4 * c else -1))
            NJ = len(jobs)
            Ztok, Etok, Ltok, Ttok, Xtok, Mtok, Otok = {}, {}, {}, {}, {}, {}, {}
            head_ld = {}
            PZ = [0, 1]
            PP = [2, 3]
            PO = [4, 5]
            stream_of = {}
            sidx = -1
            for n, jb in enumerate(jobs):
                if jb[3]:
                    sidx += 1
                stream_of[n] = sidx
            y_evac = {}

            def load_head(hh):
                if hh >= NH or hh in head_ld:
                    return
                i = hh % 2
                t1 = R.dma(lambda h, i=i, hh=hh: h.dma_start(out=qs[i].ap, in_=qT_d[hh * 128:(hh + 1) * 128, :]), qs[i].sem, deps=[qs[i].free])
                t2 = R.dma(lambda h, i=i, hh=hh: h.dma_start(out=ks[i].ap, in_=kT_d[hh * 128:(hh + 1) * 128, :]), ks[i].sem, deps=[ks[i].free])
                t3 = R.dma(lambda h, i=i, hh=hh: h.dma_start(out=vs[i].ap, in_=v_d.rearrange("(n p) c -> p n c", p=128)[:, :, hh * 128:(hh + 1) * 128]), vs[i].sem, deps=[vs[i].free])
                head_ld[hh] = [t1, t2, t3]

            def stageZ(n):
                if n >= NJ:
                    return
                hh, c, i, first, last, dg = jobs[n]
                hi = hh % 2
                b = PZ[n % 2]
                deps = [bank_free[b]] + head_ld[hh][0:2]
                tk = R.op("pe", lambda h, b=b, hi=hi, i=i, c=c: h.matmul(ps[b][:, :], lhsT=ks[hi].ap[:, i * 128:(i + 1) * 128], rhs=qs[hi].ap[:, c * 512:(c + 1) * 512], start=True, stop=(dg < 0)),
                          deps=deps, sig=(dg < 0))
                if dg >= 0:
                    o0 = 384 - 128 * dg
                    tk = R.op("pe", lambda h, b=b, o0=o0: h.matmul(ps[b][:, :], lhsT=ident_b[:, :], rhs=mbig[:, o0:o0 + 512], start=False, stop=True))
                Ztok[n] = tk
                eb, spb = ebuf[n % 4], spbuf[n % 4]
                Etok[n] = R.op("act", lambda h, b=b, eb=eb: h.activation(out=eb, in_=ps[b][:, :], func=AF.Exp), deps=[tk, Mtok.get(n - 4)])
                bank_free[b] = Etok[n]
                Ltok[n] = R.op("act", lambda h, eb=eb, spb=spb: h.activation(out=spb, in_=eb, func=AF.Ln, bias=one_t[:, 0:1], scale=1.0), deps=[Etok[n], Ttok.get(n - 4), Ftok.get(n - 4)])

            Ftok = {}

            def stageT(n):
                if n >= NJ:
                    return
                hh, c, i, first, last, dg = jobs[n]
                b = PP[stream_of[n] % 2]
                spb = spbuf[n % 4]
                Ttok[n] = R.op("pe", lambda h, b=b, spb=spb, first=first: h.matmul(ps[b][:, :], lhsT=tri_b[:, :], rhs=spb, start=first, stop=True, skip_group_check=True),
                               deps=[Ltok[n], Xtok.get(n - 1), bank_free[b] if first else None])

            def stageF(n):
                hh, c, i, first, last, dg = jobs[n]
                if last:
                    return
                b = PP[stream_of[n] % 2]
                spb = spbuf[n % 4]
                Ftok[n] = R.op("pe", lambda h, b=b, spb=spb: h.matmul(ps[b][:, :], lhsT=fix_b[:, :], rhs=spb, start=False, stop=True, skip_group_check=True),
                               deps=[Xtok[n]])

            def stageX(n):
                if n >= NJ:
                    return
                hh, c, i, first, last, dg = jobs[n]
                b = PP[stream_of[n] % 2]
                ecb = ecbuf[n % 4]
                Xtok[n] = R.op("act", lambda h, b=b, ecb=ecb: h.activation(out=ecb, in_=ps[b][:, :], func=AF.Exp, scale=-1.0), deps=[Ttok[n], Mtok.get(n - 4)])
                if last:
                    bank_free[b] = Xtok[n]

            def stageM(n):
                eb, ecb, wbf = ebuf[n % 4], ecbuf[n % 4], wbuf[n % 4]
                Mtok[n] = R.op("dve", lambda h, eb=eb, ecb=ecb, wbf=wbf: h.tensor_tensor(out=wbf, in0=eb, in1=ecb, op=ALU.mult), deps=[Xtok[n], Etok[n], Otok.get(n - 4)])

            def stageO(n):
                hh, c, i, first, last, dg = jobs[n]
                hi = hh % 2
                b = PO[stream_of[n] % 2]
                wbf = wbuf[n % 4]
                Otok[n] = R.op("pe", lambda h, b=b, hi=hi, i=i, wbf=wbf, first=first, last=last: h.matmul(ps[b][:, :], lhsT=vs[hi].ap[:, i, :], rhs=wbf, start=first, stop=last),
                               deps=[Mtok[n], head_ld[hh][2], bank_free[b] if first else None])
                if last:
                    yb = ybuf[hi]
                    ev = R.op("dve", lambda h, b=b, yb=yb, c=c: h.tensor_copy(out=yb.ap[:, c * 512:(c + 1) * 512], in_=ps[b][:, :]), deps=[Otok[n], yb.free if c == 0 else None])
                    bank_free[b] = ev
                    if c == NT - 1:
                        yb.free = R.dma(lambda h, yb=yb, hh=hh: h.dma_start(out=yat_d[hh * 128:(hh + 1) * 128, :], in_=yb.ap), yb.sem, deps=[ev])
                        qs[hi].free = Otok[n]
                        ks[hi].free = Otok[n]
                        vs[hi].free = Otok[n]
                        load_head(hh + 2)

            load_head(0)
            load_head(1)
            stageZ(0)
            stageZ(1)
            stageT(0)
            stageX(0)
            for n in range(NJ):
                stageZ(n + 2)
                stageF(n)
                stageT(n + 1)
                stageX(n + 1)
                stageM(n)
                stageO(n)
            R.barrier()

            mixT = uT
            off = 0
            glus = []
            for i in range(2):
                glus.append(Slot(R, wa_bf16(off, T + 32))); off += (T + 32) // 2
            diags = [wa_bf16(off + i * 31 * 64, 31 * 128).rearrange("p (k c) -> p k c", k=31) for i in range(2)]; off += 2 * 31 * 64
            ycv = wa_f32(off, 4 * T).rearrange("p (j t) -> p j t", j=4); off += 4 * T
            mu = mi_f32(0, T)
            sq2 = [wa_f32(off + i * 512, 512) for i in range(2)]; off += 1024
            assert off <= 66 * 256, off
            for i in range(2):
                glus[i].free = R.op("dve", lambda h, i=i: h.memset(glus[i].ap[:, 0:32], 0.0))
            diag_free = [None, None]
            last_cv = None
            dts_of = {}
            ld_of = {}

            def conv_prep(j):
                if j >= 4:
                    return
                gs_ = glus[j % 2]
                ld_of[j] = R.dma(lambda h, gs_=gs_, j=j: h.dma_start(out=gs_.ap[:, 32:32 + T], in_=glu_d[j * 128:(j + 1) * 128, :]), gs_.sem, deps=[gs_.free])
                dg_ = diags[j % 2]
                t_ = None
                for k in range(31):
                    wc = vecs[:, V0 + 108 + j * 31 + k:V0 + 108 + j * 31 + k + 1]
                    t_ = R.op("act", lambda h, k=k, wc=wc, dg_=dg_: h.activation(out=dg_[:, k, :], in_=ident_f[:, :], func=AF.Identity, scale=wc), deps=[diag_free[j % 2]] if k == 0 else [], sig=(k == 30))
                dts_of[j] = t_

            conv_prep(0)
            for j in range(4):
                diag = diags[j % 2]
                gs = glus[j % 2]
                ld, dts = ld_of[j], dts_of[j]
                lm = None
                for tt in range(NT):
                    b = get_bank([0, 1, 2, 3])
                    for k in range(31):
                        s0 = 32 + tt * 512 - 30 + k
                        lm = R.op("pe", lambda h, b=b, k=k, s0=s0, gs=gs, diag=diag: h.matmul(ps[b][:, :], lhsT=diag[:, k, :], rhs=gs.ap[:, s0:s0 + 512], start=(k == 0), stop=(k == 30)),
                                  deps=[ld, dts, bank_free[b]] if k == 0 else [], sig=(k == 30))
                    if tt == 0:
                        conv_prep(j + 1)
                    ev = R.op("act", lambda h, b=b, j=j, tt=tt: h.activation(out=ycv[:, j, tt * 512:(tt + 1) * 512], in_=ps[b][:, :], func=AF.Identity, bias=vecs[:, V0 + 80 + j:V0 + 81 + j], scale=1.0), deps=[lm])
                    bank_free[b] = ev
                    last_cv = ev
                gs.free = lm
                diag_free[j % 2] = lm
            for tt in range(NT):
                sl = slice(tt * 512, (tt + 1) * 512)
                b1 = get_bank([4, 5, 6, 7])
                b2 = get_bank([0, 1, 2, 3])
                t_s1 = t_s2 = None
                for j in range(4):
                    t_s1 = R.op("pe", lambda h, b1=b1, j=j, sl=sl: h.matmul(ps[b1][:, :], lhsT=ones_f[:, :], rhs=ycv[:, j, sl], start=(j == 0), stop=(j == 3)), deps=[last_cv, bank_free[b1]] if j == 0 else [], sig=(j == 3))
                for j in range(4):
                    sq = sq2[j % 2]
                    tq = R.op("dve", lambda h, sq=sq, j=j, sl=sl: h.tensor_tensor(out=sq, in0=ycv[:, j, sl], in1=ycv[:, j, sl], op=ALU.mult), deps=[last_cv, t_s2, ("e", "pe", R.cnt["pe"])])
                    t_s2 = R.op("pe", lambda h, b2=b2, j=j, sq=sq: h.matmul(ps[b2][:, :], lhsT=ones_f[:, :], rhs=sq, start=(j == 0), stop=(j == 3)), deps=[tq, bank_free[b2]] if j == 0 else [tq])
                t_mu = R.op("act", lambda h, b1=b1, sl=sl: h.activation(out=mu[:, sl], in_=ps[b1][:, :], func=AF.Copy, scale=1.0 / 512), deps=[t_s1])
                bank_free[b1] = t_mu
                t_m2 = R.op("dve", lambda h, sl=sl: h.tensor_tensor(out=sq2[0], in0=mu[:, sl], in1=mu[:, sl], op=ALU.mult), deps=[t_mu, t_s2])
                t_var = R.op("dve", lambda h, b2=b2, sl=sl: h.scalar_tensor_tensor(out=rstd[:, sl], in0=ps[b2][:, :], scalar=1.0 / 512, in1=sq2[0], op0=ALU.mult, op1=ALU.subtract), deps=[t_m2])
                bank_free[b2] = t_var
                t_l = R.op("act", lambda h, sl=sl: h.activation(out=rstd[:, sl], in_=rstd[:, sl], func=AF.Ln, bias=eps_t[:, 0:1], scale=1.0), deps=[t_var])
                t_r = R.op("act", lambda h, sl=sl: h.activation(out=rstd[:, sl], in_=rstd[:, sl], func=AF.Exp, scale=-0.5), deps=[t_l])
                b3 = get_bank([0, 1, 2, 3])
                t_g = None
                for j in range(4):
                    ta = R.op("dve", lambda h, j=j, sl=sl: h.tensor_tensor(out=ycv[:, j, sl], in0=ycv[:, j, sl], in1=mu[:, sl], op=ALU.subtract), deps=[t_r, t_s1, t_s2])
                    tb = R.op("dve", lambda h, j=j, sl=sl: h.tensor_tensor(out=ycv[:, j, sl], in0=ycv[:, j, sl], in1=rstd[:, sl], op=ALU.mult), deps=[ta])
                    tc = R.op("act", lambda h, j=j, sl=sl: h.activation(out=ycv[:, j, sl], in_=ycv[:, j, sl], func=AF.Silu, bias=vecs[:, V0 + 88 + j:V0 + 89 + j], scale=vecs[:, V0 + 84 + j:V0 + 85 + j]), deps=[tb])
                    sq = sq2[j % 2]
                    tq = R.op("dve", lambda h, sq=sq, j=j, sl=sl: h.tensor_tensor(out=sq, in0=ycv[:, j, sl], in1=ycv[:, j, sl], op=ALU.mult), deps=[tc, t_g])
                    t_g = R.op("pe", lambda h, b3=b3, j=j, sq=sq: h.matmul(ps[b3][:, :], lhsT=ones_f[:, :], rhs=sq, start=(j == 0), stop=(j == 3)), deps=[tq, bank_free[b3]] if j == 0 else [tq])
                t_l = R.op("act", lambda h, b3=b3, sl=sl: h.activation(out=rstd[:, sl], in_=ps[b3][:, :], func=AF.Ln, bias=eps_t[:, 0:1], scale=1.0 / 512), deps=[t_g])
                bank_free[b3] = t_l
                t_r = R.op("act", lambda h, sl=sl: h.activation(out=rstd[:, sl], in_=rstd[:, sl], func=AF.Exp, scale=-0.5), deps=[t_l])
                for j in range(4):
                    R.op("dve", lambda h, j=j, sl=sl: h.scalar_tensor_tensor(out=mixT[:, 8 + j, sl], in0=ycv[:, j, sl], scalar=vecs[:, V0 + 72 + j:V0 + 73 + j], in1=rstd[:, sl], op0=ALU.mult, op1=ALU.mult), deps=[t_r])
            R.barrier()

            off = 0
            rxs = []
            for i in range(2):
                rxs.append(Slot(R, wa_bf16(off, T + 4))); off += (T + 4) // 2
            dg4 = wa_bf16(off, 4 * 128).rearrange("p (k c) -> p k c", k=4); off += 4 * 64
            xr = wa_f32(off, T); off += T
            xrb = wa_bf16(off, T); off += T // 2
            ga = wa_f32(off, T); off += T
            gi = wa_f32(off, T); off += T
            a2 = wa_f32(off, T); off += T
            ylr = A[:, 0:8 * T].bitcast(F32).rearrange("p (j t) -> p j t", j=4)
            lruw_f = XA[:, 0:2048].bitcast(F32)
            grs = [Slot(R, wa_f32(off + i * T, T)) for i in range(2)]; off += 2 * T
            assert off <= 66 * 256, off
            t_lw = R.dma(lambda h: h.dma_start(out=lruw_f[:, :], in_=lruw_d[:, l * 1024:(l + 1) * 1024]), sem_misc[2])
            t_lwb = R.op("dve", lambda h: h.tensor_copy(out=lruw_b[:, :], in_=lruw_f[:, :]), deps=[t_lw])
            for i in range(2):
                rxs[i].free = R.op("dve", lambda h, i=i: h.memset(rxs[i].ap[:, 0:4], 0.0))
            dfree = None
            grp_last = None
            for j in range(4):
                rs = rxs[j % 2]
                ld = R.dma(lambda h, rs=rs, j=j: h.dma_start(out=rs.ap[:, 4:4 + T], in_=rx_d[j * 128:(j + 1) * 128, :]), rs.sem, deps=[rs.free])
                gr = grs[j % 2]
                ldg = R.dma(lambda h, gr=gr, j=j: h.dma_start(out=gr.ap, in_=gry_d[j * 128:(j + 1) * 128, :]), gr.sem, deps=[gr.free])
                dts = None
                for k in range(4):
                    wc = vecs[:, V0 + 232 + j * 4 + k:V0 + 232 + j * 4 + k + 1]
                    dts = R.op("act", lambda h, k=k, wc=wc: h.activation(out=dg4[:, k, :], in_=ident_f[:, :], func=AF.Identity, scale=wc), deps=[dfree] if k == 0 else [], sig=(k == 3))
                lm = None
                prev = grp_last
                for tt in range(NT):
                    sl = slice(tt * 512, (tt + 1) * 512)
                    b = get_bank([0, 1, 2, 3])
                    for k in range(4):
                        s0 = 4 + tt * 512 - 3 + k
                        lm = R.op("pe", lambda h, b=b, k=k, s0=s0, rs=rs: h.matmul(ps[b][:, :], lhsT=dg4[:, k, :], rhs=rs.ap[:, s0:s0 + 512], start=(k == 0), stop=(k == 3)),
                                  deps=[ld, dts, bank_free[b]] if k == 0 else [], sig=(k == 3))
                    t_x = R.op("act", lambda h, b=b, sl=sl, j=j: h.activation(out=xr[:, sl], in_=ps[b][:, :], func=AF.Identity, bias=vecs[:, V0 + 92 + j:V0 + 93 + j], scale=1.0), deps=[lm, prev])
                    bank_free[b] = t_x
                    t_xb = R.op("dve", lambda h, sl=sl: h.tensor_copy(out=xrb[:, sl], in_=xr[:, sl]), deps=[t_x, prev])
                    ba = get_bank([4, 5, 6, 7])
                    bi = get_bank([4, 5, 6, 7])
                    t_ga = R.op("pe", lambda h, ba=ba, sl=sl, j=j: h.matmul(ps[ba][:, :], lhsT=lruw_b[:, j * 128:(j + 1) * 128], rhs=xrb[:, sl], start=True, stop=True), deps=[t_xb, t_lwb, bank_free[ba]])
                    t_gi = R.op("pe", lambda h, bi=bi, sl=sl, j=j: h.matmul(ps[bi][:, :], lhsT=lruw_b[:, (4 + j) * 128:(5 + j) * 128], rhs=xrb[:, sl], start=True, stop=True), deps=[t_xb, bank_free[bi]])
                    t_r = R.op("act", lambda h, ba=ba, sl=sl, j=j: h.activation(out=ga[:, sl], in_=ps[ba][:, :], func=AF.Sigmoid, bias=vecs[:, V0 + 96 + j:V0 + 97 + j], scale=1.0), deps=[t_ga, prev])
                    bank_free[ba] = t_r
                    t_i = R.op("act", lambda h, bi=bi, sl=sl, j=j: h.activation(out=gi[:, sl], in_=ps[bi][:, :], func=AF.Sigmoid, bias=vecs[:, V0 + 100 + j:V0 + 101 + j], scale=1.0), deps=[t_gi, prev])
                    bank_free[bi] = t_i
                rs.free = lm
                dfree = lm
                t_a2 = R.op("act", lambda h, j=j: h.activation(out=a2[:, :], in_=ga[:, :], func=AF.Exp, scale=lruc[:, l * 8 + 4 + j:l * 8 + 5 + j]), deps=[t_r, t_i, prev])
                t_a = R.op("act", lambda h, j=j: h.activation(out=ga[:, :], in_=ga[:, :], func=AF.Exp, scale=lruc[:, l * 8 + j:l * 8 + j + 1]), deps=[t_a2])
                t_om = R.op("dve", lambda h: h.tensor_scalar(out=a2[:, :], in0=a2[:, :], scalar1=-1.0, scalar2=1.0, op0=ALU.mult, op1=ALU.add), deps=[t_a2])
                t_om = R.op("dve", lambda h: h.tensor_scalar(out=a2[:, :], in0=a2[:, :], scalar1=1e-30, scalar2=None, op0=ALU.max), deps=[t_om])
                t_ln = R.op("act", lambda h: h.activation(out=a2[:, :], in_=a2[:, :], func=AF.Ln), deps=[t_om, t_a])
                t_sq = R.op("act", lambda h: h.activation(out=a2[:, :], in_=a2[:, :], func=AF.Exp, scale=0.5), deps=[t_ln])
                t_b1 = R.op("dve", lambda h: h.tensor_tensor(out=gi[:, :], in0=gi[:, :], in1=xr[:, :], op=ALU.mult), deps=[t_i, t_x])
                t_b2 = R.op("dve", lambda h: h.tensor_tensor(out=gi[:, :], in0=gi[:, :], in1=a2[:, :], op=ALU.mult), deps=[t_b1, t_sq])
                t_h = R.op("dve", lambda h, j=j: h.tensor_tensor_scan(out=ylr[:, j, :], data0=ga[:, :], data1=gi[:, :], initial=0.0, op0=ALU.mult, op1=ALU.add), deps=[t_b2, t_a])
                t_y = R.op("dve", lambda h, j=j, gr=gr: h.tensor_tensor(out=ylr[:, j, :], in0=ylr[:, j, :], in1=gr.ap, op=ALU.mult), deps=[t_h, ldg])
                gr.free = t_y
                grp_last = t_y
            for tt in range(NT):
                sl = slice(tt * 512, (tt + 1) * 512)
                b3 = get_bank([0, 1, 2, 3])
                t_g = None
                for j in range(4):
                    sqb = (xr if j % 2 == 0 else a2)[:, 0:512]
                    tq = R.op("dve", lambda h, sqb=sqb, j=j, sl=sl: h.tensor_tensor(out=sqb, in0=ylr[:, j, sl], in1=ylr[:, j, sl], op=ALU.mult), deps=[grp_last, t_g])
                    t_g = R.op("pe", lambda h, b3=b3, j=j, sqb=sqb: h.matmul(ps[b3][:, :], lhsT=ones_f[:, :], rhs=sqb, start=(j == 0), stop=(j == 3)), deps=[tq, bank_free[b3]] if j == 0 else [tq])
                t_l = R.op("act", lambda h, b3=b3, sl=sl: h.activation(out=rstd[:, sl], in_=ps[b3][:, :], func=AF.Ln, bias=eps_t[:, 0:1], scale=1.0 / 512), deps=[t_g])
                bank_free[b3] = t_l
                t_r = R.op("act", lambda h, sl=sl: h.activation(out=rstd[:, sl], in_=rstd[:, sl], func=AF.Exp, scale=-0.5), deps=[t_l])
                for j in range(4):
                    R.op("dve", lambda h, j=j, sl=sl: h.scalar_tensor_tensor(out=mixT[:, 12 + j, sl], in0=ylr[:, j, sl], scalar=vecs[:, V0 + 76 + j:V0 + 77 + j], in1=rstd[:, sl], op0=ALU.mult, op1=ALU.mult), deps=[t_r])
            R.barrier()

            hsl = [Slot(R, mi_f32(i * T, T)) for i in range(2)]
            sqs = [Slot(R, wa_f32(i * T, T)) for i in range(2)]
            stats_pass(yat_d, NH, hsl, sqs)
            rstd_tok = ("e", "act", R.cnt["act"])
            for c in range(NH):
                hs = hsl[c % 2]
                ld = R.dma(lambda h, hs=hs, c=c: h.dma_start(out=hs.ap, in_=yat_d[c * 128:(c + 1) * 128, :]), hs.sem, deps=[hs.free])
                hs.free = R.op("dve", lambda h, hs=hs, c=c: h.scalar_tensor_tensor(out=mixT[:, c, :], in0=hs.ap, scalar=vecs[:, V0 + 64 + c:V0 + 65 + c], in1=rstd[:, :], op0=ALU.mult, op1=ALU.mult), deps=[ld, rstd_tok])
            R.barrier()

            def proj_postnorm(ws, nk, rhs_of, tok_tiles, col_of_tile):
                pass

            def out_proj(wmat, kchunks, rhs_fn, tiles, wslabs_builder):
                pass

            srcw = w_out[l].rearrange("(c p) n -> p c n", p=128)
            ws = WStream([srcw[:, :, c0:c0 + 256] for c0 in range(0, D, 256)], 16, 256, 3)
            obs = [Slot(R, mi_f32(i * T, T)) for i in range(2)]
            sqs = [Slot(R, mi_f32(2 * T + i * 512, 512)) for i in range(2)]
            oi = 0
            ws.start()
            ssq_st = {"last": None}
            pend = []
            nsq = 0
            for s in range(8):
                wb, wtok = ws.begin(s)
                lastmm = None
                for ch in range(2):
                    ob = obs[oi % 2]
                    oi += 1
                    evl = None
                    for tt in range(NT):
                        b = get_bank([4, 5, 6, 7])
                        sl = slice(tt * 512, (tt + 1) * 512)
                        for kc in range(DC):
                            lastmm = R.op("pe", lambda h, b=b, kc=kc, sl=sl, wb=wb, ch=ch: h.matmul(ps[b][:, :], lhsT=wb[:, kc, ch * 128:(ch + 1) * 128], rhs=mixT[:, kc, sl], start=(kc == 0), stop=(kc == DC - 1)),
                                          deps=[wtok, bank_free[b]] if kc == 0 else [], sig=(kc == DC - 1))
                        for fn_ in pend:
                            fn_()
                        pend = []
                        evl = R.op("act", lambda h, b=b, ob=ob, sl=sl: h.activation(out=ob.ap[:, sl], in_=ps[b][:, :], func=AF.Copy), deps=[lastmm, ob.free])
                        bank_free[b] = evl
                        sq = sqs[nsq % 2]
                        nsq += 1
                        tq = R.op("dve", lambda h, sq=sq, ob=ob, sl=sl: h.tensor_tensor(out=sq.ap, in0=ob.ap[:, sl], in1=ob.ap[:, sl], op=ALU.mult), deps=[evl, sq.free])
                        first = (s == 0 and ch == 0)
                        lastc = (s == 7 and ch == 1)

                        def ssq_(tt=tt, sq=sq, first=first, lastc=lastc, tq=tq):
                            t_ = R.op("pe", lambda h: h.matmul(ps[tt][:, :], lhsT=ones_f[:, :], rhs=sq.ap, start=first, stop=lastc, skip_group_check=True), deps=[tq, bank_free[tt]] if first else [tq])
                            sq.free = t_
                            ssq_st["last"] = t_
                        pend.append(ssq_)
                    r0 = s * 256 + ch * 128
                    ob.free = R.dma(lambda h, ob=ob, r0=r0: h.dma_start(out=oT_d[r0:r0 + 128, :], in_=ob.ap), ob.sem, deps=[evl, tq])
                ws.end(s, lastmm)
            for fn_ in pend:
                fn_()
            finish_rstd(ssq_st["last"], D, range(NT), 0)
            R.barrier()

            def residual_pass(hsrc_d, hdst_d, gcol0, want_stats):
                hsl = [Slot(R, mi_f32(i * T, T)) for i in range(2)]
                osl = [Slot(R, wa_f32(i * T, T)) for i in range(2)]
                sqs = [Slot(R, wa_f32(2 * T + i * T, T)) for i in range(2)]
                nrm = wa_f32(4 * T, T)
                rstd_tok = ("e", "act", R.cnt["act"])
                last = None
                t_add_prev = None
                for c in range(DC):
                    hs, os_ = hsl[c % 2], osl[c % 2]
                    ld1 = R.dma(lambda h, hs=hs, c=c: h.dma_start(out=hs.ap, in_=hsrc_d[c * 128:(c + 1) * 128, :]), hs.sem, deps=[hs.free])
                    ld2 = R.dma(lambda h, os_=os_, c=c: h.dma_start(out=os_.ap, in_=oT_d[c * 128:(c + 1) * 128, :]), os_.sem, deps=[os_.free])
                    t_n = R.op("dve", lambda h, os_=os_, c=c: h.scalar_tensor_tensor(out=os_.ap, in0=os_.ap, scalar=vecs[:, gcol0 + c:gcol0 + c + 1], in1=rstd[:, :], op0=ALU.mult, op1=ALU.mult), deps=[ld2, rstd_tok])
                    t_add = R.op("dve", lambda h, os_=os_, hs=hs: h.tensor_tensor(out=hs.ap, in0=hs.ap, in1=os_.ap, op=ALU.add), deps=[t_n, ld1])
                    os_.free = t_add
                    st = R.dma(lambda h, hs=hs, c=c: h.dma_start(out=hdst_d[c * 128:(c + 1) * 128, :], in_=hs.ap), hs.sem, deps=[t_add])
                    hs.free = st
                    if want_stats:
                        sq = sqs[c % 2]
                        t_sq = R.op("act", lambda h, hs=hs, sq=sq: h.activation(out=sq.ap, in_=hs.ap, func=AF.Square), deps=[t_add, sq.free])
                        for tt in range(NT):
                            last = R.op("pe", lambda h, tt=tt, sq=sq, c=c: h.matmul(ps[tt][:, :], lhsT=ones_f[:, :], rhs=sq.ap[:, tt * 512:(tt + 1) * 512], start=(c == 0), stop=(c == DC - 1)),
                                        deps=[t_sq, bank_free[tt], rstd_tok] if c == 0 else [t_sq], sig=(tt == NT - 1))
                        sq.free = last
                        hs.free = [st, t_sq]
                if want_stats:
                    t_fin = finish_rstd_after(last, t_add)
                R.barrier()

            def finish_rstd_after(tok, guard):
                t2 = None
                for tt in range(NT):
                    dst = rstd[:, tt * 512:(tt + 1) * 512]
                    t1 = R.op("act", lambda h, tt=tt, dst=dst: h.activation(out=dst, in_=ps[tt][:, :], func=AF.Ln, bias=eps_t[:, 0:1], scale=1.0 / D), deps=[tok, guard])
                    bank_free[tt] = t1
                    t2 = R.op("act", lambda h, dst=dst: h.activation(out=dst, in_=dst, func=AF.Exp, scale=-0.5), deps=[t1])
                return t2

            residual_pass(h_src, hT_d, V0 + 16, True)

            hsl = [Slot(R, mi_f32(i * T, T)) for i in range(2)]
            rstd_tok = ("e", "act", R.cnt["act"])
            build_uT(hT_d, V0 + 32, hsl, rstd_tok)
            R.barrier()

            srcg = w_gate[l].rearrange("(c p) n -> p c n", p=128)
            srcu = w_up[l].rearrange("(c p) n -> p c n", p=128)
            srcs = []
            for s in range(DFF // 256):
                srcs.append(srcg[:, :, s * 256:(s + 1) * 256])
                srcs.append(srcu[:, :, s * 256:(s + 1) * 256])
            ws = WStream(srcs, 16, 256, 2, nwb=4)
            obs = [Slot(R, mi_bf16(i * (T // 2), T)) for i in range(4)]
            sgb = [mi_f32(2 * T + i * 512, 512) for i in range(4)]
            sg_free = [None] * 4
            ws.load(0)
            ws.load(1)
            ws.cast(0)
            ws.cast(1)
            oi = 0
            nsg = 0
            for s2 in range(DFF // 256):
                sg_, su_ = 2 * s2, 2 * s2 + 1
                wg, gtok = ws.wb[sg_ % 4].ap, ws.C[sg_]
                wu, utok = ws.wb[su_ % 4].ap, ws.C[su_]
                ws.load(sg_ + 2)
                ws.load(su_ + 2)
                ws.cast(sg_ + 2)
                ws.cast(su_ + 2)
                lastmm = None
                for ch in range(2):
                    ob = obs[oi % 4]
                    oi += 1
                    evl = None
                    for tt in range(NT):
                        sl = slice(tt * 512, (tt + 1) * 512)
                        bg = get_bank([4, 5, 6, 7])
                        bu = get_bank([0, 1, 2, 3])
                        for kc in range(DC):
                            tg = R.op("pe", lambda h, bg=bg, kc=kc, sl=sl, wg=wg, ch=ch: h.matmul(ps[bg][:, :], lhsT=wg[:, kc, ch * 128:(ch + 1) * 128], rhs=uT[:, kc, sl], start=(kc == 0), stop=(kc == DC - 1)),
                                      deps=[gtok, bank_free[bg]] if kc == 0 else [], sig=(kc == DC - 1))
                        for kc in range(DC):
                            lastmm = R.op("pe", lambda h, bu=bu, kc=kc, sl=sl, wu=wu, ch=ch: h.matmul(ps[bu][:, :], lhsT=wu[:, kc, ch * 128:(ch + 1) * 128], rhs=uT[:, kc, sl], start=(kc == 0), stop=(kc == DC - 1)),
                                          deps=[utok, bank_free[bu]] if kc == 0 else [], sig=(kc == DC - 1))
                        sg = nsg % 4
                        nsg += 1
                        t_s = R.op("act", lambda h, bg=bg, sg=sg: h.activation(out=sgb[sg], in_=ps[bg][:, :], func=AF.Silu), deps=[tg, sg_free[sg]])
                        bank_free[bg] = t_s
                        evl = R.op("dve", lambda h, bu=bu, sg=sg, ob=ob, sl=sl: h.tensor_tensor(out=ob.ap[:, sl], in0=ps[bu][:, :], in1=sgb[sg], op=ALU.mult), deps=[lastmm, t_s, ob.free])
                        bank_free[bu] = evl
                        sg_free[sg] = evl
                    r0 = s2 * 256 + ch * 128
                    ob.free = R.dma(lambda h, ob=ob, r0=r0: h.dma_start(out=fT_d[r0:r0 + 128, :], in_=ob.ap), ob.sem, deps=[evl])
                ws.wb[sg_ % 4].free = lastmm
                ws.wb[su_ % 4].free = lastmm
            R.barrier()

            fT = None
            for hf in range(NHALF):
                t0 = hf * HALF
                nA = (16 * T) // HALF
                def fT_ap(j, lo, hi, nA=nA):
                    if j < nA:
                        return A[:, j * HALF + lo:j * HALF + hi]
                    return XA[:, (j - nA) * HALF + lo:(j - nA) * HALF + hi]
                fsl = Slot(R, None)
                fl = []
                for j in range(FC):
                    fl.append(R.dma(lambda h, j=j, t0=t0: h.dma_start(out=fT_ap(j, 0, HALF), in_=fT_d[j * 128:(j + 1) * 128, t0:t0 + HALF]), fsl.sem if j % 2 == 0 else sem_misc[3]))
                ws = WStream([w_down[l, s_].rearrange("p (j c) -> p j c", j=FC) for s_ in range(DC)], FC, 128, 2)
                obs = [Slot(R, mi_f32(i * HALF, HALF)) for i in range(2)]
                sqs = [Slot(R, mi_f32(2 * T + i * 512, 512)) for i in range(2)]
                ws.start()
                nsq = 0
                ssq_st = {"last": None}
                pend = []
                for s in range(DC):
                    wb, wtok = ws.begin(s)
                    ob = obs[s % 2]
                    evl = None
                    lastmm = None
                    for t2 in range(HT):
                        b = get_bank([4, 5, 6, 7])
                        for j in range(FC):
                            lastmm = R.op("pe", lambda h, b=b, j=j, t2=t2, wb=wb: h.matmul(ps[b][:, :], lhsT=wb[:, j, :], rhs=fT_ap(j, t2 * 512, (t2 + 1) * 512), start=(j == 0), stop=(j == FC - 1)),
                                          deps=([wtok, bank_free[b]] + fl) if j == 0 else [], sig=(j == FC - 1))
                        for fn_ in pend:
                            fn_()
                        pend = []
                        sl = slice(t2 * 512, (t2 + 1) * 512)
                        evl = R.op("act", lambda h, b=b, ob=ob, sl=sl: h.activation(out=ob.ap[:, sl], in_=ps[b][:, :], func=AF.Copy), deps=[lastmm, ob.free])
                        bank_free[b] = evl
                        sq = sqs[nsq % 2]
                        nsq += 1
                        tq = R.op("dve", lambda h, sq=sq, ob=ob, sl=sl: h.tensor_tensor(out=sq.ap, in0=ob.ap[:, sl], in1=ob.ap[:, sl], op=ALU.mult), deps=[evl, sq.free])
                        first = (s == 0)
                        lastc = (s == DC - 1)
                        bs = hf * HT + t2

                        def ssq_(bs=bs, sq=sq, first=first, lastc=lastc, tq=tq):
                            t_ = R.op("pe", lambda h: h.matmul(ps[bs][:, :], lhsT=ones_f[:, :], rhs=sq.ap, start=first, stop=lastc, skip_group_check=True), deps=[tq, bank_free[bs]] if first else [tq])
                            sq.free = t_
                            ssq_st["last"] = t_
                        pend.append(ssq_)
                    ob.free = R.dma(lambda h, ob=ob, s=s, t0=t0: h.dma_start(out=oT_d[s * 128:(s + 1) * 128, t0:t0 + HALF], in_=ob.ap), ob.sem, deps=[evl, tq])
                    ws.end(s, lastmm)
                for fn_ in pend:
                    fn_()
                finish_rstd(ssq_st["last"], D, [hf * HT + t2 for t2 in range(HT)], t0)
                R.barrier()

            residual_pass(hT_d, h_dst, V0 + 48, not last_layer)

        for l in range(NL):
            layer(l, xT if l == 0 else hT_d)

        R.replay(nc)
    return nc


def _fm(v, n):
    return np.ascontiguousarray(v.reshape(n, 128).T)


def _pack_vecs(inp, layers):
    cols = []
    for l in layers:
        cols += [_fm(inp["g_pre_mix"][l], 16), _fm(inp["g_post_mix"][l], 16), _fm(inp["g_pre_ffn"][l], 16), _fm(inp["g_post_ffn"][l], 16),
                 _fm(inp["g_attn_grp"][l], 8), _fm(inp["g_conv_grp"][l], 4), _fm(inp["g_lru_grp"][l], 4),
                 _fm(inp["dw_conv_b"][l], 4), _fm(inp["conv_ln_g"][l], 4), _fm(inp["conv_ln_b"][l], 4),
                 _fm(inp["lru_conv_b"][l], 4), _fm(inp["lru_b_a"][l], 4), _fm(inp["lru_b_i"][l], 4), _fm(inp["lru_lambda"][l], 4)]
        dw = inp["dw_conv_w"][l]
        cols.append(np.ascontiguousarray(dw.reshape(31, 4, 128).transpose(2, 1, 0).reshape(128, 124)))
        lw = inp["lru_conv_w"][l]
        cols.append(np.ascontiguousarray(lw.reshape(4, 4, 128).transpose(2, 1, 0).reshape(128, 16)))
    return np.ascontiguousarray(np.concatenate(cols, axis=1).astype(np.float32))


def _pack_lruw(inp, layers):
    out = []
    for l in layers:
        wa = inp["lru_w_a"][l]
        wi = inp["lru_w_i"][l]
        both = np.concatenate([wa, wi], axis=0)
        out.append(both.transpose(1, 0, 2).reshape(128, 8 * 128))
    return np.ascontiguousarray(np.concatenate(out, axis=1).astype(np.float32))


def _consts():
    k = np.arange(128)
    ident = np.eye(128, dtype=np.float32)
    tri = (k[:, None] >= k[None, :]).astype(np.float32)
    fix = (k[:, None] < k[None, :]).astype(np.float32)
    c = np.arange(896)
    mb = np.where(k[:, None] >= (c[None, :] - 384), NEG, 0.0).astype(np.float32)
    return np.ascontiguousarray(np.concatenate([ident, tri, fix, mb], axis=1))


_NC_CACHE = {}


def run_layers(hT_list, inp, layers, T, dbg=False):
    key = (T, len(layers), dbg)
    if key not in _NC_CACHE:
        _NC_CACHE[key] = build(T, len(layers), dbg)
    nc = _NC_CACHE[key]
    shared = {
        "w_in": np.ascontiguousarray(inp["w_in"][layers]),
        "w_out": np.ascontiguousarray(inp["w_out"][layers]),
        "w_gate": np.ascontiguousarray(inp["w_gate"][layers]),
        "w_up": np.ascontiguousarray(inp["w_up"][layers]),
        "w_down": np.ascontiguousarray(inp["w_down"][layers].reshape(len(layers), FC, 128, DC, 128).transpose(0, 3, 2, 1, 4)).reshape(len(layers), DC, 128, FC * 128),
        "vecs": _pack_vecs(inp, layers),
        "lruw": _pack_lruw(inp, layers),
        "cst": _consts(),
    }
    in_maps = [dict(shared, xT=np.ascontiguousarray(h)) for h in hT_list]
    res = run_bass_kernel_spmd(nc, in_maps, core_ids=list(range(len(in_maps))))
    return res.results


FUSED = True


def kernel(**inputs):
    inp = {k: np.asarray(v) for k, v in inputs.items()}
    x = inp["x"]
    B, S, _ = x.shape
    NCORE = B
    hT = [np.ascontiguousarray(x[b % B].T) for b in range(NCORE)]
    if FUSED:
        res = run_layers(hT, inp, list(range(4)), S)
    else:
        for l in range(4):
            res = run_layers(hT, inp, [l], S)
            hT = [res[b]["outT"] for b in range(NCORE)]
    out = np.stack([np.ascontiguousarray(res[b]["outT"].T) for b in range(B)], axis=0)
    return out.astype(np.float32)
```

```python
import numpy as np
import ml_dtypes
import concourse.bass as bass
import concourse.mybir as mybir
from concourse.bass_utils import run_bass_kernel_spmd
from contextlib import ExitStack

F32 = mybir.dt.float32
BF16 = mybir.dt.bfloat16
AF = mybir.ActivationFunctionType
ALU = mybir.AluOpType

D = 2048
DC = 16
ATT = 1024
NH = 8
CCH = 512
DFF = 5632
FC = 44
INC = 5120
EPS = 1e-6
VL = 248
NEG = -30000.0


class Rec:
    ENG = ("pe", "act", "dve", "pool", "sp")

    def __init__(self):
        self.q = {e: [] for e in self.ENG}
        self.cnt = {e: 0 for e in self.ENG}
        self.dma_cnt = []
        self.last_dma = {}
        self.bar = {e: [] for e in self.ENG}

    def new_dma_sem(self):
        c = getattr(self, "sem_cursor", len(self.dma_cnt))
        if c >= len(self.dma_cnt):
            self.dma_cnt.append(0)
        self.sem_cursor = c + 1
        return c

    def _deps(self, eng, deps):
        d = []
        for x in deps:
            if x is None:
                continue
            if isinstance(x, list):
                d.extend([y for y in x if y is not None])
            else:
                d.append(x)
        if self.bar[eng]:
            d = d + self.bar[eng]
            self.bar[eng] = []
        return d

    def op(self, eng, fn, deps=(), sig=True):
        tok = None
        if sig:
            self.cnt[eng] += 1
            tok = ("e", eng, self.cnt[eng])
        self.q[eng].append((fn, self._deps(eng, deps), tok))
        return tok

    def dma(self, fn, semidx, deps=(), eng="sp"):
        self.dma_cnt[semidx] += 16
        tok = ("d", semidx, self.dma_cnt[semidx])
        self.q[eng].append((fn, self._deps(eng, deps), tok))
        self.last_dma[semidx] = tok
        return tok

    def barrier(self):
        toks = [("e", e, self.cnt[e]) for e in self.ENG if self.cnt[e] > 0]
        toks += list(self.last_dma.values())
        for e in self.ENG:
            self.bar[e] = list(toks)

    def replay(self, nc):
        with ExitStack() as es:
            esem = {e: es.enter_context(nc.semaphore("sem_" + e)) for e in self.ENG}
            dsem = [es.enter_context(nc.semaphore("dsem%d" % i)) for i in range(len(self.dma_cnt))]
            block = es.enter_context(nc.Block())

            def semof(tok):
                return esem[tok[1]] if tok[0] == "e" else dsem[tok[1]]

            def run(eng, h):
                waited = {}
                for fn, deps, tok in self.q[eng]:
                    for d in deps:
                        key = (d[0], d[1])
                        if waited.get(key, 0) >= d[2]:
                            continue
                        if d[0] == "e" and d[1] == eng and eng == "pe":
                            continue
                        h.wait_ge(semof(d), d[2])
                        waited[key] = d[2]
                    ins = fn(h)
                    if tok is not None:
                        ins.then_inc(semof(tok), 1 if tok[0] == "e" else 16)
                if eng == "sp":
                    for t in self.last_dma.values():
                        if waited.get((t[0], t[1]), 0) < t[2]:
                            h.wait_ge(semof(t), t[2])

            @block.tensor
            def _(h):
                run("pe", h)

            @block.scalar
            def _(h):
                run("act", h)

            @block.vector
            def _(h):
                run("dve", h)

            @block.gpsimd
            def _(h):
                run("pool", h)

            @block.sync
            def _(h):
                run("sp", h)


class Slot:
    def __init__(self, R, ap):
        self.ap = ap
        self.sem = R.new_dma_sem()
        self.free = None
        self.ready = None


def build(T, NL, dbg=False):
    NT = T // 512
    NBK = T // 128
    HALF = min(1024, T)
    NHALF = T // HALF
    HT = HALF // 512
    nc = bass.Bass("TRN2", target_bir_lowering=False)
    R = Rec()
    din = lambda n, s, dt=F32: nc.dram_tensor(n, s, dt, kind="ExternalInput").ap()
    skind = "ExternalOutput" if dbg else "Internal"
    dsc = lambda n, s, dt: nc.dram_tensor(n, s, dt, kind=skind).ap()
    xT = din("xT", [D, T])
    w_in = din("w_in", [NL, D, INC])
    w_out = din("w_out", [NL, D, D])
    w_gate = din("w_gate", [NL, D, DFF])
    w_up = din("w_up", [NL, D, DFF])
    w_down = din("w_down", [NL, DC, 128, FC * 128])
    vecs_d = din("vecs", [128, NL * VL])
    lruw_d = din("lruw", [128, NL * 8 * 128])
    cst_d = din("cst", [128, 128 * 3 + 896])
    outT = nc.dram_tensor("outT", [D, T], F32, kind="ExternalOutput").ap()
    hT_d = dsc("hT_s", [D, T], F32)
    oT_d = dsc("oT_s", [D, T], F32)
    qT_d = dsc("qT_s", [ATT, T], BF16)
    kT_d = dsc("kT_s", [ATT, T], BF16)
    v_d = dsc("v_s", [T, ATT], BF16)
    glu_d = dsc("glu_s", [CCH, T], BF16)
    rx_d = dsc("rx_s", [CCH, T], BF16)
    gry_d = dsc("gry_s", [CCH, T], F32)
    yat_d = dsc("yat_s", [ATT, T], F32)
    fT_d = dsc("fT_s", [DFF, T], BF16)

    with ExitStack() as es:
        def sb(name, shape, dt):
            return es.enter_context(nc.sbuf_tensor(name, shape, dt))

        vecs = sb("vecs_sb", [128, NL * VL], F32)
        ident_f = sb("ident_f", [128, 128], F32)
        ident_b = sb("ident_b", [128, 128], BF16)
        tri_b = sb("tri_b", [128, 128], BF16)
        fix_b = sb("fix_b", [128, 128], BF16)
        mbig = sb("mbig", [128, 896], BF16)
        ones_f = sb("ones_f", [128, 128], F32)
        eps_t = sb("eps_t", [128, 1], F32)
        one_t = sb("one_t", [128, 1], F32)
        lruc = sb("lruc", [128, NL * 8], F32)
        lruw_b = sb("lruw_b", [128, 8 * 128], BF16)
        rstd = sb("rstd", [128, T], F32)
        A = sb("arenaA", [128, 16 * T], BF16)
        XA = sb("arenaX", [128, max(0, 44 * HALF - 16 * T) + 2], BF16)
        WA = sb("arenaW", [128, 66 * 256], F32)
        MI = sb("arenaM", [128, 6 * 1024], F32)
        ps = [es.enter_context(nc.psum_tensor("ps%d" % i, [128, 512], F32)) for i in range(8)]
        bank_free = [None] * 8

        uT = A[:, :].rearrange("p (c t) -> p c t", c=16)

        def wa_f32(off, n):
            return WA[:, off:off + n]

        def wa_bf16(off_f32, n_bf):
            return WA[:, off_f32:off_f32 + (n_bf + 1) // 2].bitcast(BF16)

        def mi_f32(off, n):
            return MI[:, off:off + n]

        def mi_bf16(off_f32, n_bf):
            return MI[:, off_f32:off_f32 + (n_bf + 1) // 2].bitcast(BF16)

        sem_misc = [R.new_dma_sem() for _ in range(4)]

        t_v = R.dma(lambda h: h.dma_start(out=vecs[:, :], in_=vecs_d[:, :]), sem_misc[0])
        cst = WA[:, 0:128 * 3 + 896]
        t_c = R.dma(lambda h: h.dma_start(out=cst, in_=cst_d[:, :]), sem_misc[1])
        R.op("dve", lambda h: h.tensor_copy(out=ident_b[:, :], in_=cst[:, 0:128]), deps=[t_c])
        R.op("dve", lambda h: h.tensor_copy(out=ident_f[:, :], in_=cst[:, 0:128]))
        R.op("dve", lambda h: h.tensor_copy(out=tri_b[:, :], in_=cst[:, 128:256]))
        R.op("dve", lambda h: h.tensor_copy(out=fix_b[:, :], in_=cst[:, 256:384]))
        R.op("dve", lambda h: h.tensor_copy(out=mbig[:, :], in_=cst[:, 384:384 + 896]))
        R.op("dve", lambda h: h.memset(ones_f[:, :], 1.0))
        R.op("dve", lambda h: h.memset(eps_t[:, :], EPS))
        R.op("dve", lambda h: h.memset(one_t[:, :], 1.0))
        for l in range(NL):
            lam = vecs[:, l * VL + 104:l * VL + 108]
            cc = lruc[:, l * 8:l * 8 + 4]
            c2 = lruc[:, l * 8 + 4:l * 8 + 8]
            ta_ = R.op("act", lambda h, lam=lam, cc=cc: h.activation(out=cc, in_=lam, func=AF.Exp, scale=-1.0), deps=[t_v])
            tb_ = R.op("act", lambda h, cc=cc: h.activation(out=cc, in_=cc, func=AF.Ln, bias=one_t[:, 0:1], scale=1.0),
                       deps=[("e", "dve", R.cnt["dve"]), ta_])
            t1 = R.op("act", lambda h, cc=cc, c2=c2: h.activation(out=c2, in_=cc, func=AF.Copy, scale=-16.0), deps=[tb_])
            R.op("act", lambda h, cc=cc: h.activation(out=cc, in_=cc, func=AF.Copy, scale=-8.0), deps=[t1])
        R.barrier()

        def get_bank(cands):
            b = cands[get_bank.i % len(cands)]
            get_bank.i += 1
            return b
        get_bank.i = 0

        def stats_pass(src_d, nchunks, hslots, sqslots):
            last = None
            for c in range(nchunks):
                hs = hslots[c % 2]
                ld = R.dma(lambda h, hs=hs, c=c: h.dma_start(out=hs.ap, in_=src_d[c * 128:(c + 1) * 128, :]), hs.sem, deps=[hs.free])
                sq = sqslots[c % 2]
                t_sq = R.op("act", lambda h, hs=hs, sq=sq: h.activation(out=sq.ap, in_=hs.ap, func=AF.Square), deps=[ld, sq.free])
                hs.free = t_sq
                for tt in range(NT):
                    last = R.op("pe", lambda h, tt=tt, sq=sq, c=c: h.matmul(ps[tt][:, :], lhsT=ones_f[:, :], rhs=sq.ap[:, tt * 512:(tt + 1) * 512], start=(c == 0), stop=(c == nchunks - 1)),
                                deps=[t_sq, bank_free[tt]] if True else [], sig=(tt == NT - 1))
                sq.free = last
            finish_rstd(last, nchunks * 128, range(NT), 0)

        def finish_rstd(tok, nfeat, banks, col0, width=512):
            t2 = None
            for i, b in enumerate(banks):
                dst = rstd[:, col0 + i * width:col0 + (i + 1) * width]
                t1 = R.op("act", lambda h, b=b, dst=dst: h.activation(out=dst, in_=ps[b][:, 0:width], func=AF.Ln, bias=eps_t[:, 0:1], scale=1.0 / nfeat), deps=[tok])
                bank_free[b] = t1
                t2 = R.op("act", lambda h, dst=dst: h.activation(out=dst, in_=dst, func=AF.Exp, scale=-0.5), deps=[t1])
            return t2

        def build_uT(src_d, gcol0, hslots, rstd_tok):
            toks = []
            for c in range(DC):
                hs = hslots[c % 2]
                ld = R.dma(lambda h, hs=hs, c=c: h.dma_start(out=hs.ap, in_=src_d[c * 128:(c + 1) * 128, :]), hs.sem, deps=[hs.free])
                t = R.op("dve", lambda h, hs=hs, c=c: h.scalar_tensor_tensor(out=uT[:, c, :], in0=hs.ap, scalar=vecs[:, gcol0 + c:gcol0 + c + 1], in1=rstd[:, :], op0=ALU.mult, op1=ALU.mult),
                         deps=[ld, rstd_tok])
                hs.free = t
                toks.append(t)
            return toks[-1]

        class WStream:
            def __init__(self, srcs, nrows, ncols, nstg, nwb=2, cast_eng="pool"):
                self.cast_eng = cast_eng
                self.srcs = srcs
                self.n = len(srcs)
                self.nrows, self.ncols = nrows, ncols
                sz = nrows * ncols
                self.stg = [Slot(R, wa_f32(i * sz, sz).rearrange("p (r c) -> p r c", r=nrows)) for i in range(nstg)]
                off = nstg * sz
                self.wb = [Slot(R, wa_bf16(off + i * (sz // 2), sz).rearrange("p (r c) -> p r c", r=nrows)) for i in range(nwb)]
                assert off + nwb * (sz // 2) <= 66 * 256
                self.nstg = nstg
                self.nwb = nwb
                self.L, self.C = {}, {}
                self.used = {}

            def load(self, s):
                if s >= self.n or s in self.L:
                    return
                st = self.stg[s % self.nstg]
                self.L[s] = R.dma(lambda h, st=st, s=s: h.dma_start(out=st.ap, in_=self.srcs[s]), st.sem, deps=[st.free])

            def cast(self, s):
                if s >= self.n or s in self.C:
                    return
                st = self.stg[s % self.nstg]
                wb = self.wb[s % self.nwb]
                if self.cast_eng == "act":
                    self.C[s] = R.op("act", lambda h, st=st, wb=wb: h.activation(out=wb.ap, in_=st.ap, func=AF.Copy), deps=[self.L[s], wb.free])
                else:
                    self.C[s] = R.op("pool", lambda h, st=st, wb=wb: h.tensor_copy(out=wb.ap, in_=st.ap), deps=[self.L[s], wb.free])
                st.free = self.C[s]

            def start(self):
                for s in range(min(self.nstg, self.n)):
                    self.load(s)
                self.cast(0)

            def begin(self, s):
                self.cast(s + 1)
                return self.wb[s % self.nwb].ap, self.C[s]

            def end(self, s, last_mm_tok):
                self.wb[s % self.nwb].free = last_mm_tok
                self.load(s + self.nstg)

        def colslabs(w2d, col0, ncols, width):
            src = w2d.rearrange("(c p) n -> p c n", p=128)
            return [src[:, :, c:c + width] for c in range(col0, col0 + ncols, width)]

        sem_base = len(R.dma_cnt)

        def layer(l, h_src):
            R.sem_cursor = sem_base
            V0 = l * VL
            last_layer = (l == NL - 1)
            h_dst = outT if last_layer else hT_d

            hsl = [Slot(R, mi_f32(i * T, T)) for i in range(2)]
            sqs = [Slot(R, wa_f32(i * T, T)) for i in range(2)]
            if l == 0:
                stats_pass(h_src, DC, hsl, sqs)
                rstd_tok = ("e", "act", R.cnt["act"])
                build_uT(h_src, V0 + 0, hsl, rstd_tok)
                R.barrier()

            wl = w_in[l]
            order = []
            for s in range(4):
                order.append(("q", s * 256))
            for s in range(4):
                order.append(("k", 1024 + s * 256))
            for s in range(4):
                order.append(("v", 2048 + s * 256))
            for s in range(2):
                order.append(("cg", 3584 + s * 256))
                order.append(("cv", 3072 + s * 256))
            for s in range(2):
                order.append(("rx", 4096 + s * 256))
            for s in range(2):
                order.append(("ry", 4608 + s * 256))
            srcw = wl.rearrange("(c p) n -> p c n", p=128)
            ws = WStream([srcw[:, :, c0:c0 + 256] for _, c0 in order], 16, 256, 3)
            obs = [Slot(R, mi_f32(i * T, T)) for i in range(2)]
            sig = [XA[:, i * 2 * T:(i + 1) * 2 * T].bitcast(F32) for i in range(2)]
            tmpA = mi_f32(2 * T, 512)
            tmpB = mi_f32(2 * T + 512, 512)
            vbuf = Slot(R, XA[:, 4 * T:4 * T + NBK * 256].rearrange("p (n c) -> p n c", n=NBK))
            oi = 0
            sig_tok = [None, None]
            tmp_free = None
            ws.start()
            for s, (kind, c0) in enumerate(order):
                wb, wtok = ws.begin(s)
                lastmm = None
                if kind == "v":
                    vb = vbuf
                    evs = []
                    for tk in range(NBK):
                        b = get_bank([4, 5, 6, 7])
                        for dc in range(DC):
                            lastmm = R.op("pe", lambda h, b=b, dc=dc, tk=tk, wb=wb: h.matmul(ps[b][:, 0:256], lhsT=uT[:, dc, tk * 128:(tk + 1) * 128], rhs=wb[:, dc, :], start=(dc == 0), stop=(dc == DC - 1)),
                                          deps=[wtok, bank_free[b]] if dc == 0 else [], sig=(dc == DC - 1))
                        ev = R.op("act", lambda h, b=b, tk=tk, vb=vb: h.activation(out=vb.ap[:, tk, :], in_=ps[b][:, 0:256], func=AF.Copy), deps=[lastmm, vb.free])
                        bank_free[b] = ev
                        evs.append(ev)
                    vc0 = c0 - 2048
                    vb.free = R.dma(lambda h, vb=vb, vc0=vc0: h.dma_start(out=v_d.rearrange("(n p) c -> p n c", p=128)[:, :, vc0:vc0 + 256], in_=vb.ap), vb.sem, deps=[evs[-1]])
                else:
                    for ch in range(2):
                        ob = obs[oi % 2]
                        oi += 1
                        ob_bf = ob.ap.bitcast(BF16)[:, 0:T]
                        evl = None
                        for tt in range(NT):
                            b = get_bank([4, 5, 6, 7])
                            for dc in range(DC):
                                lastmm = R.op("pe", lambda h, b=b, dc=dc, tt=tt, wb=wb, ch=ch: h.matmul(ps[b][:, :], lhsT=wb[:, dc, ch * 128:(ch + 1) * 128], rhs=uT[:, dc, tt * 512:(tt + 1) * 512], start=(dc == 0), stop=(dc == DC - 1)),
                                              deps=[wtok, bank_free[b]] if dc == 0 else [], sig=(dc == DC - 1))
                            sl = slice(tt * 512, (tt + 1) * 512)
                            P = ps[b]
                            if kind == "q":
                                evl = R.op("act", lambda h, P=P, ob_bf=ob_bf, sl=sl: h.activation(out=ob_bf[:, sl], in_=P[:, :], func=AF.Copy, scale=128.0 ** -0.5), deps=[lastmm, ob.free])
                            elif kind in ("k", "rx"):
                                evl = R.op("act", lambda h, P=P, ob_bf=ob_bf, sl=sl: h.activation(out=ob_bf[:, sl], in_=P[:, :], func=AF.Copy), deps=[lastmm, ob.free])
                            elif kind == "cg":
                                evl = R.op("act", lambda h, P=P, ch=ch, sl=sl: h.activation(out=sig[ch][:, sl], in_=P[:, :], func=AF.Sigmoid), deps=[lastmm, sig_tok[ch]])
                            elif kind == "cv":
                                evl = R.op("dve", lambda h, P=P, ch=ch, sl=sl, ob_bf=ob_bf: h.tensor_tensor(out=ob_bf[:, sl], in0=P[:, :], in1=sig[ch][:, sl], op=ALU.mult), deps=[lastmm, ob.free, sig_ready[ch]])
                                sig_tok[ch] = evl
                            elif kind == "ry":
                                t1 = R.op("act", lambda h, P=P: h.activation(out=tmpA, in_=P[:, :], func=AF.Square), deps=[lastmm, tmp_free])
                                t2 = R.op("dve", lambda h: h.tensor_scalar(out=tmpA, in0=tmpA, scalar1=0.044715, scalar2=1.0, op0=ALU.mult, op1=ALU.add), deps=[t1])
                                t3 = R.op("dve", lambda h, P=P: h.tensor_tensor(out=tmpB, in0=P[:, :], in1=tmpA, op=ALU.mult), deps=[t2])
                                t4 = R.op("act", lambda h: h.activation(out=tmpA, in_=tmpB, func=AF.Sigmoid, scale=1.5957691216), deps=[t3])
                                evl = R.op("dve", lambda h, P=P, ob=ob, sl=sl: h.tensor_tensor(out=ob.ap[:, sl], in0=P[:, :], in1=tmpA, op=ALU.mult), deps=[t4, ob.free])
                                tmp_free = evl
                            bank_free[b] = evl
                        if kind == "cg":
                            if ch == 0:
                                sig_ready = [None, None]
                            sig_ready[ch] = evl
                            oi -= 1
                            continue
                        r0 = (c0 + ch * 128)
                        if kind == "q":
                            dst = qT_d[r0:r0 + 128, :]
                        elif kind == "k":
                            dst = kT_d[r0 - 1024:r0 - 1024 + 128, :]
                        elif kind == "cv":
                            dst = glu_d[r0 - 3072:r0 - 3072 + 128, :]
                        elif kind == "rx":
                            dst = rx_d[r0 - 4096:r0 - 4096 + 128, :]
                        elif kind == "ry":
                            dst = gry_d[r0 - 4608:r0 - 4608 + 128, :]
                        src_ap = ob.ap if kind == "ry" else ob_bf
                        ob.free = R.dma(lambda h, dst=dst, src_ap=src_ap: h.dma_start(out=dst, in_=src_ap), ob.sem, deps=[evl])
                ws.end(s, lastmm)
            R.barrier()

            off = 0
            qs, ks, vs = [], [], []
            for i in range(2):
                qs.append(Slot(R, wa_bf16(off, T))); off += T // 2
                ks.append(Slot(R, wa_bf16(off, T))); off += T // 2
                vs.append(Slot(R, wa_bf16(off, NBK * 128).rearrange("p (n d) -> p n d", n=NBK))); off += NBK * 64
            ebuf = [wa_f32(off + i * 512, 512) for i in range(4)]; off += 4 * 512
            ecbuf = [wa_f32(off + i * 512, 512) for i in range(4)]; off += 4 * 512
            spbuf = [wa_bf16(off + i * 256, 512) for i in range(4)]; off += 4 * 256
            wbuf = [wa_bf16(off + i * 256, 512) for i in range(4)]; off += 4 * 256
            ybuf = [Slot(R, wa_f32(off + i * T, T)) for i in range(2)]; off += 2 * T
            assert off <= 66 * 256, off
            jobs = []
            for hh in range(NH):
                for c in range(NT):
                    nb = 4 * (c + 1)
                    for i in range(nb - 1, -1, -1):
                        jobs.append((hh, c, i, i == nb - 1, i == 0, (i - 4 * c) if i >= 4 * c else -1))
            NJ = len(jobs)
            Ztok, Etok, Ltok, Ttok, Xtok, Mtok, Otok = {}, {}, {}, {}, {}, {}, {}
            head_ld = {}
            PZ = [0, 1]
            PP = [2, 3]
            PO = [4, 5]
            stream_of = {}
            sidx = -1
            for n, jb in enumerate(jobs):
                if jb[3]:
                    sidx += 1
                stream_of[n] = sidx
            y_evac = {}

            def load_head(hh):
                if hh >= NH or hh in head_ld:
                    return
                i = hh % 2
                t1 = R.dma(lambda h, i=i, hh=hh: h.dma_start(out=qs[i].ap, in_=qT_d[hh * 128:(hh + 1) * 128, :]), qs[i].sem, deps=[qs[i].free])
                t2 = R.dma(lambda h, i=i, hh=hh: h.dma_start(out=ks[i].ap, in_=kT_d[hh * 128:(hh + 1) * 128, :]), ks[i].sem, deps=[ks[i].free])
                t3 = R.dma(lambda h, i=i, hh=hh: h.dma_start(out=vs[i].ap, in_=v_d.rearrange("(n p) c -> p n c", p=128)[:, :, hh * 128:(hh + 1) * 128]), vs[i].sem, deps=[vs[i].free])
                head_ld[hh] = [t1, t2, t3]

            def stageZ(n):
                if n >= NJ:
                    return
                hh, c, i, first, last, dg = jobs[n]
                hi = hh % 2
                b = PZ[n % 2]
                deps = [bank_free[b]] + head_ld[hh][0:2]
                tk = R.op("pe", lambda h, b=b, hi=hi, i=i, c=c: h.matmul(ps[b][:, :], lhsT=ks[hi].ap[:, i * 128:(i + 1) * 128], rhs=qs[hi].ap[:, c * 512:(c + 1) * 512], start=True, stop=(dg < 0)),
                          deps=deps, sig=(dg < 0))
                if dg >= 0:
                    o0 = 384 - 128 * dg
                    tk = R.op("pe", lambda h, b=b, o0=o0: h.matmul(ps[b][:, :], lhsT=ident_b[:, :], rhs=mbig[:, o0:o0 + 512], start=False, stop=True))
                Ztok[n] = tk
                eb, spb = ebuf[n % 4], spbuf[n % 4]
                Etok[n] = R.op("act", lambda h, b=b, eb=eb: h.activation(out=eb, in_=ps[b][:, :], func=AF.Exp), deps=[tk, Mtok.get(n - 4)])
                bank_free[b] = Etok[n]
                Ltok[n] = R.op("act", lambda h, eb=eb, spb=spb: h.activation(out=spb, in_=eb, func=AF.Ln, bias=one_t[:, 0:1], scale=1.0), deps=[Etok[n], Ttok.get(n - 4), Ftok.get(n - 4)])

            Ftok = {}

            def stageT(n):
                if n >= NJ:
                    return
                hh, c, i, first, last, dg = jobs[n]
                b = PP[stream_of[n] % 2]
                spb = spbuf[n % 4]
                Ttok[n] = R.op("pe", lambda h, b=b, spb=spb, first=first: h.matmul(ps[b][:, :], lhsT=tri_b[:, :], rhs=spb, start=first, stop=True, skip_group_check=True),
                               deps=[Ltok[n], Xtok.get(n - 1), bank_free[b] if first else None])

            def stageF(n):
                hh, c, i, first, last, dg = jobs[n]
                if last:
                    return
                b = PP[stream_of[n] % 2]
                spb = spbuf[n % 4]
                Ftok[n] = R.op("pe", lambda h, b=b, spb=spb: h.matmul(ps[b][:, :], lhsT=fix_b[:, :], rhs=spb, start=False, stop=True, skip_group_check=True),
                               deps=[Xtok[n]])

            def stageX(n):
                if n >= NJ:
                    return
                hh, c, i, first, last, dg = jobs[n]
                b = PP[stream_of[n] % 2]
                ecb = ecbuf[n % 4]
                Xtok[n] = R.op("act", lambda h, b=b, ecb=ecb: h.activation(out=ecb, in_=ps[b][:, :], func=AF.Exp, scale=-1.0), deps=[Ttok[n], Mtok.get(n - 4)])
                if last:
                    bank_free[b] = Xtok[n]

            def stageM(n):
                eb, ecb, wbf = ebuf[n % 4], ecbuf[n % 4], wbuf[n % 4]
                Mtok[n] = R.op("dve", lambda h, eb=eb, ecb=ecb, wbf=wbf: h.tensor_tensor(out=wbf, in0=eb, in1=ecb, op=ALU.mult), deps=[Xtok[n], Etok[n], Otok.get(n - 4)])

            def stageO(n):
                hh, c, i, first, last, dg = jobs[n]
                hi = hh % 2
                b = PO[stream_of[n] % 2]
                wbf = wbuf[n % 4]
                Otok[n] = R.op("pe", lambda h, b=b, hi=hi, i=i, wbf=wbf, first=first, last=last: h.matmul(ps[b][:, :], lhsT=vs[hi].ap[:, i, :], rhs=wbf, start=first, stop=last),
                               deps=[Mtok[n], head_ld[hh][2], bank_free[b] if first else None])
                if last:
                    yb = ybuf[hi]
                    ev = R.op("dve", lambda h, b=b, yb=yb, c=c: h.tensor_copy(out=yb.ap[:, c * 512:(c + 1) * 512], in_=ps[b][:, :]), deps=[Otok[n], yb.free if c == 0 else None])
                    bank_free[b] = ev
                    if c == NT - 1:
                        yb.free = R.dma(lambda h, yb=yb, hh=hh: h.dma_start(out=yat_d[hh * 128:(hh + 1) * 128, :], in_=yb.ap), yb.sem, deps=[ev])
                        qs[hi].free = Otok[n]
                        ks[hi].free = Otok[n]
                        vs[hi].free = Otok[n]
                        load_head(hh + 2)

            load_head(0)
            load_head(1)
            stageZ(0)
            stageZ(1)
            stageT(0)
            stageX(0)
            for n in range(NJ):
                stageZ(n + 2)
                stageF(n)
                stageT(n + 1)
                stageX(n + 1)
                stageM(n)
                stageO(n)
            R.barrier()

            mixT = uT
            off = 0
            glus = []
            for i in range(2):
                glus.append(Slot(R, wa_bf16(off, T + 32))); off += (T + 32) // 2
            diags = [wa_bf16(off + i * 31 * 64, 31 * 128).rearrange("p (k c) -> p k c", k=31) for i in range(2)]; off += 2 * 31 * 64
            ycv = wa_f32(off, 4 * T).rearrange("p (j t) -> p j t", j=4); off += 4 * T
            mu = mi_f32(0, T)
            sq2 = [wa_f32(off + i * 512, 512) for i in range(2)]; off += 1024
            assert off <= 66 * 256, off
            for i in range(2):
                glus[i].free = R.op("dve", lambda h, i=i: h.memset(glus[i].ap[:, 0:32], 0.0))
            diag_free = [None, None]
            last_cv = None
            dts_of = {}
            ld_of = {}

            def conv_prep(j):
                if j >= 4:
                    return
                gs_ = glus[j % 2]
                ld_of[j] = R.dma(lambda h, gs_=gs_, j=j: h.dma_start(out=gs_.ap[:, 32:32 + T], in_=glu_d[j * 128:(j + 1) * 128, :]), gs_.sem, deps=[gs_.free])
                dg_ = diags[j % 2]
                t_ = None
                for k in range(31):
                    wc = vecs[:, V0 + 108 + j * 31 + k:V0 + 108 + j * 31 + k + 1]
                    t_ = R.op("act", lambda h, k=k, wc=wc, dg_=dg_: h.activation(out=dg_[:, k, :], in_=ident_f[:, :], func=AF.Identity, scale=wc), deps=[diag_free[j % 2]] if k == 0 else [], sig=(k == 30))
                dts_of[j] = t_

            conv_prep(0)
            for j in range(4):
                diag = diags[j % 2]
                gs = glus[j % 2]
                ld, dts = ld_of[j], dts_of[j]
                lm = None
                for tt in range(NT):
                    b = get_bank([0, 1, 2, 3])
                    for k in range(31):
                        s0 = 32 + tt * 512 - 30 + k
                        lm = R.op("pe", lambda h, b=b, k=k, s0=s0, gs=gs, diag=diag: h.matmul(ps[b][:, :], lhsT=diag[:, k, :], rhs=gs.ap[:, s0:s0 + 512], start=(k == 0), stop=(k == 30)),
                                  deps=[ld, dts, bank_free[b]] if k == 0 else [], sig=(k == 30))
                    if tt == 0:
                        conv_prep(j + 1)
                    ev = R.op("act", lambda h, b=b, j=j, tt=tt: h.activation(out=ycv[:, j, tt * 512:(tt + 1) * 512], in_=ps[b][:, :], func=AF.Identity, bias=vecs[:, V0 + 80 + j:V0 + 81 + j], scale=1.0), deps=[lm])
                    bank_free[b] = ev
                    last_cv = ev
                gs.free = lm
                diag_free[j % 2] = lm
            for tt in range(NT):
                sl = slice(tt * 512, (tt + 1) * 512)
                b1 = get_bank([4, 5, 6, 7])
                b2 = get_bank([0, 1, 2, 3])
                t_s1 = t_s2 = None
                for j in range(4):
                    t_s1 = R.op("pe", lambda h, b1=b1, j=j, sl=sl: h.matmul(ps[b1][:, :], lhsT=ones_f[:, :], rhs=ycv[:, j, sl], start=(j == 0), stop=(j == 3)), deps=[last_cv, bank_free[b1]] if j == 0 else [], sig=(j == 3))
                for j in range(4):
                    sq = sq2[j % 2]
                    tq = R.op("dve", lambda h, sq=sq, j=j, sl=sl: h.tensor_tensor(out=sq, in0=ycv[:, j, sl], in1=ycv[:, j, sl], op=ALU.mult), deps=[last_cv, t_s2, ("e", "pe", R.cnt["pe"])])
                    t_s2 = R.op("pe", lambda h, b2=b2, j=j, sq=sq: h.matmul(ps[b2][:, :], lhsT=ones_f[:, :], rhs=sq, start=(j == 0), stop=(j == 3)), deps=[tq, bank_free[b2]] if j == 0 else [tq])
                t_mu = R.op("act", lambda h, b1=b1, sl=sl: h.activation(out=mu[:, sl], in_=ps[b1][:, :], func=AF.Copy, scale=1.0 / 512), deps=[t_s1])
                bank_free[b1] = t_mu
                t_m2 = R.op("dve", lambda h, sl=sl: h.tensor_tensor(out=sq2[0], in0=mu[:, sl], in1=mu[:, sl], op=ALU.mult), deps=[t_mu, t_s2])
                t_var = R.op("dve", lambda h, b2=b2, sl=sl: h.scalar_tensor_tensor(out=rstd[:, sl], in0=ps[b2][:, :], scalar=1.0 / 512, in1=sq2[0], op0=ALU.mult, op1=ALU.subtract), deps=[t_m2])
                bank_free[b2] = t_var
                t_l = R.op("act", lambda h, sl=sl: h.activation(out=rstd[:, sl], in_=rstd[:, sl], func=AF.Ln, bias=eps_t[:, 0:1], scale=1.0), deps=[t_var])
                t_r = R.op("act", lambda h, sl=sl: h.activation(out=rstd[:, sl], in_=rstd[:, sl], func=AF.Exp, scale=-0.5), deps=[t_l])
                b3 = get_bank([0, 1, 2, 3])
                t_g = None
                for j in range(4):
                    ta = R.op("dve", lambda h, j=j, sl=sl: h.tensor_tensor(out=ycv[:, j, sl], in0=ycv[:, j, sl], in1=mu[:, sl], op=ALU.subtract), deps=[t_r, t_s1, t_s2])
                    tb = R.op("dve", lambda h, j=j, sl=sl: h.tensor_tensor(out=ycv[:, j, sl], in0=ycv[:, j, sl], in1=rstd[:, sl], op=ALU.mult), deps=[ta])
                    tc = R.op("act", lambda h, j=j, sl=sl: h.activation(out=ycv[:, j, sl], in_=ycv[:, j, sl], func=AF.Silu, bias=vecs[:, V0 + 88 + j:V0 + 89 + j], scale=vecs[:, V0 + 84 + j:V0 + 85 + j]), deps=[tb])
                    sq = sq2[j % 2]
                    tq = R.op("dve", lambda h, sq=sq, j=j, sl=sl: h.tensor_tensor(out=sq, in0=ycv[:, j, sl], in1=ycv[:, j, sl], op=ALU.mult), deps=[tc, t_g])
                    t_g = R.op("pe", lambda h, b3=b3, j=j, sq=sq: h.matmul(ps[b3][:, :], lhsT=ones_f[:, :], rhs=sq, start=(j == 0), stop=(j == 3)), deps=[tq, bank_free[b3]] if j == 0 else [tq])
                t_l = R.op("act", lambda h, b3=b3, sl=sl: h.activation(out=rstd[:, sl], in_=ps[b3][:, :], func=AF.Ln, bias=eps_t[:, 0:1], scale=1.0 / 512), deps=[t_g])
                bank_free[b3] = t_l
                t_r = R.op("act", lambda h, sl=sl: h.activation(out=rstd[:, sl], in_=rstd[:, sl], func=AF.Exp, scale=-0.5), deps=[t_l])
                for j in range(4):
                    R.op("dve", lambda h, j=j, sl=sl: h.scalar_tensor_tensor(out=mixT[:, 8 + j, sl], in0=ycv[:, j, sl], scalar=vecs[:, V0 + 72 + j:V0 + 73 + j], in1=rstd[:, sl], op0=ALU.mult, op1=ALU.mult), deps=[t_r])
            R.barrier()

            off = 0
            rxs = []
            for i in range(2):
                rxs.append(Slot(R, wa_bf16(off, T + 4))); off += (T + 4) // 2
            dg4 = wa_bf16(off, 4 * 128).rearrange("p (k c) -> p k c", k=4); off += 4 * 64
            xr = wa_f32(off, T); off += T
            xrb = wa_bf16(off, T); off += T // 2
            ga = wa_f32(off, T); off += T
            gi = wa_f32(off, T); off += T
            a2 = wa_f32(off, T); off += T
            ylr = A[:, 0:8 * T].bitcast(F32).rearrange("p (j t) -> p j t", j=4)
            lruw_f = XA[:, 0:2048].bitcast(F32)
            grs = [Slot(R, wa_f32(off + i * T, T)) for i in range(2)]; off += 2 * T
            assert off <= 66 * 256, off
            t_lw = R.dma(lambda h: h.dma_start(out=lruw_f[:, :], in_=lruw_d[:, l * 1024:(l + 1) * 1024]), sem_misc[2])
            t_lwb = R.op("dve", lambda h: h.tensor_copy(out=lruw_b[:, :], in_=lruw_f[:, :]), deps=[t_lw])
            for i in range(2):
                rxs[i].free = R.op("dve", lambda h, i=i: h.memset(rxs[i].ap[:, 0:4], 0.0))
            dfree = None
            grp_last = None
            for j in range(4):
                rs = rxs[j % 2]
                ld = R.dma(lambda h, rs=rs, j=j: h.dma_start(out=rs.ap[:, 4:4 + T], in_=rx_d[j * 128:(j + 1) * 128, :]), rs.sem, deps=[rs.free])
                gr = grs[j % 2]
                ldg = R.dma(lambda h, gr=gr, j=j: h.dma_start(out=gr.ap, in_=gry_d[j * 128:(j + 1) * 128, :]), gr.sem, deps=[gr.free])
                dts = None
                for k in range(4):
                    wc = vecs[:, V0 + 232 + j * 4 + k:V0 + 232 + j * 4 + k + 1]
                    dts = R.op("act", lambda h, k=k, wc=wc: h.activation(out=dg4[:, k, :], in_=ident_f[:, :], func=AF.Identity, scale=wc), deps=[dfree] if k == 0 else [], sig=(k == 3))
                lm = None
                prev = grp_last
                for tt in range(NT):
                    sl = slice(tt * 512, (tt + 1) * 512)
                    b = get_bank([0, 1, 2, 3])
                    for k in range(4):
                        s0 = 4 + tt * 512 - 3 + k
                        lm = R.op("pe", lambda h, b=b, k=k, s0=s0, rs=rs: h.matmul(ps[b][:, :], lhsT=dg4[:, k, :], rhs=rs.ap[:, s0:s0 + 512], start=(k == 0), stop=(k == 3)),
                                  deps=[ld, dts, bank_free[b]] if k == 0 else [], sig=(k == 3))
                    t_x = R.op("act", lambda h, b=b, sl=sl, j=j: h.activation(out=xr[:, sl], in_=ps[b][:, :], func=AF.Identity, bias=vecs[:, V0 + 92 + j:V0 + 93 + j], scale=1.0), deps=[lm, prev])
                    bank_free[b] = t_x
                    t_xb = R.op("dve", lambda h, sl=sl: h.tensor_copy(out=xrb[:, sl], in_=xr[:, sl]), deps=[t_x, prev])
                    ba = get_bank([4, 5, 6, 7])
                    bi = get_bank([4, 5, 6, 7])
                    t_ga = R.op("pe", lambda h, ba=ba, sl=sl, j=j: h.matmul(ps[ba][:, :], lhsT=lruw_b[:, j * 128:(j + 1) * 128], rhs=xrb[:, sl], start=True, stop=True), deps=[t_xb, t_lwb, bank_free[ba]])
                    t_gi = R.op("pe", lambda h, bi=bi, sl=sl, j=j: h.matmul(ps[bi][:, :], lhsT=lruw_b[:, (4 + j) * 128:(5 + j) * 128], rhs=xrb[:, sl], start=True, stop=True), deps=[t_xb, bank_free[bi]])
                    t_r = R.op("act", lambda h, ba=ba, sl=sl, j=j: h.activation(out=ga[:, sl], in_=ps[ba][:, :], func=AF.Sigmoid, bias=vecs[:, V0 + 96 + j:V0 + 97 + j], scale=1.0), deps=[t_ga, prev])
                    bank_free[ba] = t_r
                    t_i = R.op("act", lambda h, bi=bi, sl=sl, j=j: h.activation(out=gi[:, sl], in_=ps[bi][:, :], func=AF.Sigmoid, bias=vecs[:, V0 + 100 + j:V0 + 101 + j], scale=1.0), deps=[t_gi, prev])
                    bank_free[bi] = t_i
                rs.free = lm
                dfree = lm
                t_a2 = R.op("act", lambda h, j=j: h.activation(out=a2[:, :], in_=ga[:, :], func=AF.Exp, scale=lruc[:, l * 8 + 4 + j:l * 8 + 5 + j]), deps=[t_r, t_i, prev])
                t_a = R.op("act", lambda h, j=j: h.activation(out=ga[:, :], in_=ga[:, :], func=AF.Exp, scale=lruc[:, l * 8 + j:l * 8 + j + 1]), deps=[t_a2])
                t_om = R.op("dve", lambda h: h.tensor_scalar(out=a2[:, :], in0=a2[:, :], scalar1=-1.0, scalar2=1.0, op0=ALU.mult, op1=ALU.add), deps=[t_a2])
                t_om = R.op("dve", lambda h: h.tensor_scalar(out=a2[:, :], in0=a2[:, :], scalar1=1e-30, scalar2=None, op0=ALU.max), deps=[t_om])
                t_ln = R.op("act", lambda h: h.activation(out=a2[:, :], in_=a2[:, :], func=AF.Ln), deps=[t_om, t_a])
                t_sq = R.op("act", lambda h: h.activation(out=a2[:, :], in_=a2[:, :], func=AF.Exp, scale=0.5), deps=[t_ln])
                t_b1 = R.op("dve", lambda h: h.tensor_tensor(out=gi[:, :], in0=gi[:, :], in1=xr[:, :], op=ALU.mult), deps=[t_i, t_x])
                t_b2 = R.op("dve", lambda h: h.tensor_tensor(out=gi[:, :], in0=gi[:, :], in1=a2[:, :], op=ALU.mult), deps=[t_b1, t_sq])
                t_h = R.op("dve", lambda h, j=j: h.tensor_tensor_scan(out=ylr[:, j, :], data0=ga[:, :], data1=gi[:, :], initial=0.0, op0=ALU.mult, op1=ALU.add), deps=[t_b2, t_a])
                t_y = R.op("dve", lambda h, j=j, gr=gr: h.tensor_tensor(out=ylr[:, j, :], in0=ylr[:, j, :], in1=gr.ap, op=ALU.mult), deps=[t_h, ldg])
                gr.free = t_y
                grp_last = t_y
            for tt in range(NT):
                sl = slice(tt * 512, (tt + 1) * 512)
                b3 = get_bank([0, 1, 2, 3])
                t_g = None
                for j in range(4):
                    sqb = (xr if j % 2 == 0 else a2)[:, 0:512]
                    tq = R.op("dve", lambda h, sqb=sqb, j=j, sl=sl: h.tensor_tensor(out=sqb, in0=ylr[:, j, sl], in1=ylr[:, j, sl], op=ALU.mult), deps=[grp_last, t_g])
                    t_g = R.op("pe", lambda h, b3=b3, j=j, sqb=sqb: h.matmul(ps[b3][:, :], lhsT=ones_f[:, :], rhs=sqb, start=(j == 0), stop=(j == 3)), deps=[tq, bank_free[b3]] if j == 0 else [tq])
                t_l = R.op("act", lambda h, b3=b3, sl=sl: h.activation(out=rstd[:, sl], in_=ps[b3][:, :], func=AF.Ln, bias=eps_t[:, 0:1], scale=1.0 / 512), deps=[t_g])
                bank_free[b3] = t_l
                t_r = R.op("act", lambda h, sl=sl: h.activation(out=rstd[:, sl], in_=rstd[:, sl], func=AF.Exp, scale=-0.5), deps=[t_l])
                for j in range(4):
                    R.op("dve", lambda h, j=j, sl=sl: h.scalar_tensor_tensor(out=mixT[:, 12 + j, sl], in0=ylr[:, j, sl], scalar=vecs[:, V0 + 76 + j:V0 + 77 + j], in1=rstd[:, sl], op0=ALU.mult, op1=ALU.mult), deps=[t_r])
            R.barrier()

            hsl = [Slot(R, mi_f32(i * T, T)) for i in range(2)]
            sqs = [Slot(R, wa_f32(i * T, T)) for i in range(2)]
            stats_pass(yat_d, NH, hsl, sqs)
            rstd_tok = ("e", "act", R.cnt["act"])
            for c in range(NH):
                hs = hsl[c % 2]
                ld = R.dma(lambda h, hs=hs, c=c: h.dma_start(out=hs.ap, in_=yat_d[c * 128:(c + 1) * 128, :]), hs.sem, deps=[hs.free])
                hs.free = R.op("dve", lambda h, hs=hs, c=c: h.scalar_tensor_tensor(out=mixT[:, c, :], in0=hs.ap, scalar=vecs[:, V0 + 64 + c:V0 + 65 + c], in1=rstd[:, :], op0=ALU.mult, op1=ALU.mult), deps=[ld, rstd_tok])
            R.barrier()

            def proj_postnorm(ws, nk, rhs_of, tok_tiles, col_of_tile):
                pass

            def out_proj(wmat, kchunks, rhs_fn, tiles, wslabs_builder):
                pass

            srcw = w_out[l].rearrange("(c p) n -> p c n", p=128)
            ws = WStream([srcw[:, :, c0:c0 + 256] for c0 in range(0, D, 256)], 16, 256, 3)
            obs = [Slot(R, mi_f32(i * T, T)) for i in range(2)]
            sqs = [Slot(R, mi_f32(2 * T + i * 512, 512)) for i in range(2)]
            oi = 0
            ws.start()
            ssq_st = {"last": None}
            pend = []
            nsq = 0
            for s in range(8):
                wb, wtok = ws.begin(s)
                lastmm = None
                for ch in range(2):
                    ob = obs[oi % 2]
                    oi += 1
                    evl = None
                    for tt in range(NT):
                        b = get_bank([4, 5, 6, 7])
                        sl = slice(tt * 512, (tt + 1) * 512)
                        for kc in range(DC):
                            lastmm = R.op("pe", lambda h, b=b, kc=kc, sl=sl, wb=wb, ch=ch: h.matmul(ps[b][:, :], lhsT=wb[:, kc, ch * 128:(ch + 1) * 128], rhs=mixT[:, kc, sl], start=(kc == 0), stop=(kc == DC - 1)),
                                          deps=[wtok, bank_free[b]] if kc == 0 else [], sig=(kc == DC - 1))
                        for fn_ in pend:
                            fn_()
                        pend = []
                        evl = R.op("act", lambda h, b=b, ob=ob, sl=sl: h.activation(out=ob.ap[:, sl], in_=ps[b][:, :], func=AF.Copy), deps=[lastmm, ob.free])
                        bank_free[b] = evl
                        sq = sqs[nsq % 2]
                        nsq += 1
                        tq = R.op("dve", lambda h, sq=sq, ob=ob, sl=sl: h.tensor_tensor(out=sq.ap, in0=ob.ap[:, sl], in1=ob.ap[:, sl], op=ALU.mult), deps=[evl, sq.free])
                        first = (s == 0 and ch == 0)
                        lastc = (s == 7 and ch == 1)

                        def ssq_(tt=tt, sq=sq, first=first, lastc=lastc, tq=tq):
                            t_ = R.op("pe", lambda h: h.matmul(ps[tt][:, :], lhsT=ones_f[:, :], rhs=sq.ap, start=first, stop=lastc, skip_group_check=True), deps=[tq, bank_free[tt]] if first else [tq])
                            sq.free = t_
                            ssq_st["last"] = t_
                        pend.append(ssq_)
                    r0 = s * 256 + ch * 128
                    ob.free = R.dma(lambda h, ob=ob, r0=r0: h.dma_start(out=oT_d[r0:r0 + 128, :], in_=ob.ap), ob.sem, deps=[evl, tq])
                ws.end(s, lastmm)
            for fn_ in pend:
                fn_()
            finish_rstd(ssq_st["last"], D, range(NT), 0)
            R.barrier()

            def residual_pass(hsrc_d, hdst_d, gcol0, want_stats, gnext=None):
                hsl = [Slot(R, mi_f32(i * T, T)) for i in range(2)]
                osl = [Slot(R, wa_f32(i * T, T)) for i in range(2)]
                sqs = [Slot(R, wa_f32(2 * T + i * T, T)) for i in range(2)]
                nrm = wa_f32(4 * T, T)
                rstd_tok = ("e", "act", R.cnt["act"])
                last = None
                t_add_prev = None
                for c in range(DC):
                    hs, os_ = hsl[c % 2], osl[c % 2]
                    ld1 = R.dma(lambda h, hs=hs, c=c: h.dma_start(out=hs.ap, in_=hsrc_d[c * 128:(c + 1) * 128, :]), hs.sem, deps=[hs.free])
                    ld2 = R.dma(lambda h, os_=os_, c=c: h.dma_start(out=os_.ap, in_=oT_d[c * 128:(c + 1) * 128, :]), os_.sem, deps=[os_.free])
                    t_n = R.op("dve", lambda h, os_=os_, c=c: h.scalar_tensor_tensor(out=os_.ap, in0=os_.ap, scalar=vecs[:, gcol0 + c:gcol0 + c + 1], in1=rstd[:, :], op0=ALU.mult, op1=ALU.mult), deps=[ld2, rstd_tok])
                    t_add = R.op("dve", lambda h, os_=os_, hs=hs: h.tensor_tensor(out=hs.ap, in0=hs.ap, in1=os_.ap, op=ALU.add), deps=[t_n, ld1])
                    os_.free = t_add
                    st = R.dma(lambda h, hs=hs, c=c: h.dma_start(out=hdst_d[c * 128:(c + 1) * 128, :], in_=hs.ap), hs.sem, deps=[t_add])
                    hs.free = st
                    if want_stats:
                        sq = sqs[c % 2]
                        t_sq = R.op("act", lambda h, hs=hs, sq=sq: h.activation(out=sq.ap, in_=hs.ap, func=AF.Square), deps=[t_add, sq.free])
                        for tt in range(NT):
                            last = R.op("pe", lambda h, tt=tt, sq=sq, c=c: h.matmul(ps[tt][:, :], lhsT=ones_f[:, :], rhs=sq.ap[:, tt * 512:(tt + 1) * 512], start=(c == 0), stop=(c == DC - 1)),
                                        deps=[t_sq, bank_free[tt], rstd_tok] if c == 0 else [t_sq], sig=(tt == NT - 1))
                        sq.free = last
                        hs.free = [st, t_sq]
                        if gnext is not None:
                            t_u = R.op("act", lambda h, hs=hs, c=c: h.activation(out=uT[:, c, :], in_=hs.ap, func=AF.Identity, scale=vecs[:, gnext + c:gnext + c + 1]), deps=[t_add])
                            hs.free = [st, t_sq, t_u]
                if want_stats:
                    t_fin = finish_rstd_after(last, t_add)
                    if gnext is not None:
                        for c in range(DC):
                            R.op("dve", lambda h, c=c: h.tensor_tensor(out=uT[:, c, :], in0=uT[:, c, :], in1=rstd[:, :], op=ALU.mult), deps=[t_fin, t_u])
                R.barrier()

            def finish_rstd_after(tok, guard):
                t2 = None
                for tt in range(NT):
                    dst = rstd[:, tt * 512:(tt + 1) * 512]
                    t1 = R.op("act", lambda h, tt=tt, dst=dst: h.activation(out=dst, in_=ps[tt][:, :], func=AF.Ln, bias=eps_t[:, 0:1], scale=1.0 / D), deps=[tok, guard])
                    bank_free[tt] = t1
                    t2 = R.op("act", lambda h, dst=dst: h.activation(out=dst, in_=dst, func=AF.Exp, scale=-0.5), deps=[t1])
                return t2

            residual_pass(h_src, hT_d, V0 + 16, True, gnext=V0 + 32)

            srcg = w_gate[l].rearrange("(c p) n -> p c n", p=128)
            srcu = w_up[l].rearrange("(c p) n -> p c n", p=128)
            srcs = []
            for s in range(DFF // 256):
                srcs.append(srcg[:, :, s * 256:(s + 1) * 256])
                srcs.append(srcu[:, :, s * 256:(s + 1) * 256])
            ws = WStream(srcs, 16, 256, 2, nwb=4)
            obs = [Slot(R, mi_bf16(i * (T // 2), T)) for i in range(4)]
            sgb = [mi_f32(2 * T + i * 512, 512) for i in range(4)]
            sg_free = [None] * 4
            ws.load(0)
            ws.load(1)
            ws.cast(0)
            ws.cast(1)
            oi = 0
            nsg = 0
            for s2 in range(DFF // 256):
                sg_, su_ = 2 * s2, 2 * s2 + 1
                wg, gtok = ws.wb[sg_ % 4].ap, ws.C[sg_]
                wu, utok = ws.wb[su_ % 4].ap, ws.C[su_]
                ws.load(sg_ + 2)
                ws.load(su_ + 2)
                ws.cast(sg_ + 2)
                ws.cast(su_ + 2)
                lastmm = None
                for ch in range(2):
                    ob = obs[oi % 4]
                    oi += 1
                    evl = None
                    for tt in range(NT):
                        sl = slice(tt * 512, (tt + 1) * 512)
                        bg = get_bank([4, 5, 6, 7])
                        bu = get_bank([0, 1, 2, 3])
                        for kc in range(DC):
                            tg = R.op("pe", lambda h, bg=bg, kc=kc, sl=sl, wg=wg, ch=ch: h.matmul(ps[bg][:, :], lhsT=wg[:, kc, ch * 128:(ch + 1) * 128], rhs=uT[:, kc, sl], start=(kc == 0), stop=(kc == DC - 1)),
                                      deps=[gtok, bank_free[bg]] if kc == 0 else [], sig=(kc == DC - 1))
                        for kc in range(DC):
                            lastmm = R.op("pe", lambda h, bu=bu, kc=kc, sl=sl, wu=wu, ch=ch: h.matmul(ps[bu][:, :], lhsT=wu[:, kc, ch * 128:(ch + 1) * 128], rhs=uT[:, kc, sl], start=(kc == 0), stop=(kc == DC - 1)),
                                          deps=[utok, bank_free[bu]] if kc == 0 else [], sig=(kc == DC - 1))
                        sg = nsg % 4
                        nsg += 1
                        t_s = R.op("act", lambda h, bg=bg, sg=sg: h.activation(out=sgb[sg], in_=ps[bg][:, :], func=AF.Silu), deps=[tg, sg_free[sg]])
                        bank_free[bg] = t_s
                        evl = R.op("dve", lambda h, bu=bu, sg=sg, ob=ob, sl=sl: h.tensor_tensor(out=ob.ap[:, sl], in0=ps[bu][:, :], in1=sgb[sg], op=ALU.mult), deps=[lastmm, t_s, ob.free])
                        bank_free[bu] = evl
                        sg_free[sg] = evl
                    r0 = s2 * 256 + ch * 128
                    ob.free = R.dma(lambda h, ob=ob, r0=r0: h.dma_start(out=fT_d[r0:r0 + 128, :], in_=ob.ap), ob.sem, deps=[evl])
                ws.wb[sg_ % 4].free = lastmm
                ws.wb[su_ % 4].free = lastmm
            R.barrier()

            fT = None
            for hf in range(NHALF):
                t0 = hf * HALF
                nA = (16 * T) // HALF
                def fT_ap(j, lo, hi, nA=nA):
                    if j < nA:
                        return A[:, j * HALF + lo:j * HALF + hi]
                    return XA[:, (j - nA) * HALF + lo:(j - nA) * HALF + hi]
                fsl = Slot(R, None)
                fl = []
                for j in range(FC):
                    fl.append(R.dma(lambda h, j=j, t0=t0: h.dma_start(out=fT_ap(j, 0, HALF), in_=fT_d[j * 128:(j + 1) * 128, t0:t0 + HALF]), fsl.sem if j % 2 == 0 else sem_misc[3]))
                ws = WStream([w_down[l, s_].rearrange("p (j c) -> p j c", j=FC) for s_ in range(DC)], FC, 128, 2, cast_eng="act")
                obs = [Slot(R, mi_f32(i * HALF, HALF)) for i in range(2)]
                sqs = [Slot(R, mi_f32(2 * T + i * 512, 512)) for i in range(2)]
                ws.start()
                nsq = 0
                ssq_st = {"last": None}
                pend = []
                for s in range(DC):
                    wb, wtok = ws.begin(s)
                    ob = obs[s % 2]
                    evl = None
                    lastmm = None
                    for t2 in range(HT):
                        b = get_bank([4, 5, 6, 7])
                        for j in range(FC):
                            lastmm = R.op("pe", lambda h, b=b, j=j, t2=t2, wb=wb: h.matmul(ps[b][:, :], lhsT=wb[:, j, :], rhs=fT_ap(j, t2 * 512, (t2 + 1) * 512), start=(j == 0), stop=(j == FC - 1)),
                                          deps=([wtok, bank_free[b]] + fl) if j == 0 else [], sig=(j == FC - 1))
                        for fn_ in pend:
                            fn_()
                        pend = []
                        sl = slice(t2 * 512, (t2 + 1) * 512)
                        evl = R.op("act", lambda h, b=b, ob=ob, sl=sl: h.activation(out=ob.ap[:, sl], in_=ps[b][:, :], func=AF.Copy), deps=[lastmm, ob.free])
                        bank_free[b] = evl
                        sq = sqs[nsq % 2]
                        nsq += 1
                        tq = R.op("dve", lambda h, sq=sq, ob=ob, sl=sl: h.tensor_tensor(out=sq.ap, in0=ob.ap[:, sl], in1=ob.ap[:, sl], op=ALU.mult), deps=[evl, sq.free])
                        first = (s == 0)
                        lastc = (s == DC - 1)
                        bs = hf * HT + t2

                        def ssq_(bs=bs, sq=sq, first=first, lastc=lastc, tq=tq):
                            t_ = R.op("pe", lambda h: h.matmul(ps[bs][:, :], lhsT=ones_f[:, :], rhs=sq.ap, start=first, stop=lastc, skip_group_check=True), deps=[tq, bank_free[bs]] if first else [tq])
                            sq.free = t_
                            ssq_st["last"] = t_
                        pend.append(ssq_)
                    ob.free = R.dma(lambda h, ob=ob, s=s, t0=t0: h.dma_start(out=oT_d[s * 128:(s + 1) * 128, t0:t0 + HALF], in_=ob.ap), ob.sem, deps=[evl, tq])
                    ws.end(s, lastmm)
                for fn_ in pend:
                    fn_()
                finish_rstd(ssq_st["last"], D, [hf * HT + t2 for t2 in range(HT)], t0)
                R.barrier()

            residual_pass(hT_d, h_dst, V0 + 48, not last_layer, gnext=None if last_layer else V0 + VL)

        for l in range(NL):
            layer(l, xT if l == 0 else hT_d)

        R.replay(nc)
    return nc


def _fm(v, n):
    return np.ascontiguousarray(v.reshape(n, 128).T)


def _pack_vecs(inp, layers):
    cols = []
    for l in layers:
        cols += [_fm(inp["g_pre_mix"][l], 16), _fm(inp["g_post_mix"][l], 16), _fm(inp["g_pre_ffn"][l], 16), _fm(inp["g_post_ffn"][l], 16),
                 _fm(inp["g_attn_grp"][l], 8), _fm(inp["g_conv_grp"][l], 4), _fm(inp["g_lru_grp"][l], 4),
                 _fm(inp["dw_conv_b"][l], 4), _fm(inp["conv_ln_g"][l], 4), _fm(inp["conv_ln_b"][l], 4),
                 _fm(inp["lru_conv_b"][l], 4), _fm(inp["lru_b_a"][l], 4), _fm(inp["lru_b_i"][l], 4), _fm(inp["lru_lambda"][l], 4)]
        dw = inp["dw_conv_w"][l]
        cols.append(np.ascontiguousarray(dw.reshape(31, 4, 128).transpose(2, 1, 0).reshape(128, 124)))
        lw = inp["lru_conv_w"][l]
        cols.append(np.ascontiguousarray(lw.reshape(4, 4, 128).transpose(2, 1, 0).reshape(128, 16)))
    return np.ascontiguousarray(np.concatenate(cols, axis=1).astype(np.float32))


def _pack_lruw(inp, layers):
    out = []
    for l in layers:
        wa = inp["lru_w_a"][l]
        wi = inp["lru_w_i"][l]
        both = np.concatenate([wa, wi], axis=0)
        out.append(both.transpose(1, 0, 2).reshape(128, 8 * 128))
    return np.ascontiguousarray(np.concatenate(out, axis=1).astype(np.float32))


def _consts():
    k = np.arange(128)
    ident = np.eye(128, dtype=np.float32)
    tri = (k[:, None] >= k[None, :]).astype(np.float32)
    fix = (k[:, None] < k[None, :]).astype(np.float32)
    c = np.arange(896)
    mb = np.where(k[:, None] >= (c[None, :] - 384), NEG, 0.0).astype(np.float32)
    return np.ascontiguousarray(np.concatenate([ident, tri, fix, mb], axis=1))


_NC_CACHE = {}


def run_layers(hT_list, inp, layers, T, dbg=False):
    key = (T, len(layers), dbg)
    if key not in _NC_CACHE:
        _NC_CACHE[key] = build(T, len(layers), dbg)
    nc = _NC_CACHE[key]
    shared = {
        "w_in": np.ascontiguousarray(inp["w_in"][layers]),
        "w_out": np.ascontiguousarray(inp["w_out"][layers]),
        "w_gate": np.ascontiguousarray(inp["w_gate"][layers]),
        "w_up": np.ascontiguousarray(inp["w_up"][layers]),
        "w_down": np.ascontiguousarray(inp["w_down"][layers].reshape(len(layers), FC, 128, DC, 128).transpose(0, 3, 2, 1, 4)).reshape(len(layers), DC, 128, FC * 128),
        "vecs": _pack_vecs(inp, layers),
        "lruw": _pack_lruw(inp, layers),
        "cst": _consts(),
    }
    in_maps = [dict(shared, xT=np.ascontiguousarray(h)) for h in hT_list]
    res = run_bass_kernel_spmd(nc, in_maps, core_ids=list(range(len(in_maps))))
    return res.results


FUSED = True


def kernel(**inputs):
    inp = {k: np.asarray(v) for k, v in inputs.items()}
    x = inp["x"]
    B, S, _ = x.shape
    NCORE = B
    hT = [np.ascontiguousarray(x[b % B].T) for b in range(NCORE)]
    if FUSED:
        res = run_layers(hT, inp, list(range(4)), S)
    else:
        for l in range(4):
            res = run_layers(hT, inp, [l], S)
            hT = [res[b]["outT"] for b in range(NCORE)]
    out = np.stack([np.ascontiguousarray(res[b]["outT"].T) for b in range(B)], axis=0)
    return out.astype(np.float32)
```

```python
import numpy as np
import ml_dtypes
import concourse.bass as bass
import concourse.mybir as mybir
from concourse.bass_utils import run_bass_kernel_spmd
from contextlib import ExitStack

F32 = mybir.dt.float32
BF16 = mybir.dt.bfloat16
AF = mybir.ActivationFunctionType
ALU = mybir.AluOpType

D = 2048
DC = 16
ATT = 1024
NH = 8
CCH = 512
DFF = 5632
FC = 44
INC = 5120
EPS = 1e-6
VL = 248
NEG = -30000.0


class Rec:
    ENG = ("pe", "act", "dve", "pool", "sp")

    def __init__(self):
        self.q = {e: [] for e in self.ENG}
        self.cnt = {e: 0 for e in self.ENG}
        self.dma_cnt = []
        self.last_dma = {}
        self.bar = {e: [] for e in self.ENG}

    def new_dma_sem(self):
        c = getattr(self, "sem_cursor", len(self.dma_cnt))
        if c >= len(self.dma_cnt):
            self.dma_cnt.append(0)
        self.sem_cursor = c + 1
        return c

    def _deps(self, eng, deps):
        d = []
        for x in deps:
            if x is None:
                continue
            if isinstance(x, list):
                d.extend([y for y in x if y is not None])
            else:
                d.append(x)
        if self.bar[eng]:
            d = d + self.bar[eng]
            self.bar[eng] = []
        return d

    def op(self, eng, fn, deps=(), sig=True):
        tok = None
        if sig:
            self.cnt[eng] += 1
            tok = ("e", eng, self.cnt[eng])
        self.q[eng].append((fn, self._deps(eng, deps), tok))
        return tok

    def dma(self, fn, semidx, deps=(), eng="sp"):
        self.dma_cnt[semidx] += 16
        tok = ("d", semidx, self.dma_cnt[semidx])
        self.q[eng].append((fn, self._deps(eng, deps), tok))
        self.last_dma[semidx] = tok
        return tok

    def barrier(self):
        toks = [("e", e, self.cnt[e]) for e in self.ENG if self.cnt[e] > 0]
        toks += list(self.last_dma.values())
        for e in self.ENG:
            self.bar[e] = list(toks)

    def replay(self, nc):
        with ExitStack() as es:
            esem = {e: es.enter_context(nc.semaphore("sem_" + e)) for e in self.ENG}
            dsem = [es.enter_context(nc.semaphore("dsem%d" % i)) for i in range(len(self.dma_cnt))]
            block = es.enter_context(nc.Block())

            def semof(tok):
                return esem[tok[1]] if tok[0] == "e" else dsem[tok[1]]

            def run(eng, h):
                waited = {}
                for fn, deps, tok in self.q[eng]:
                    need = {}
                    for d in deps:
                        key = (d[0], d[1])
                        if d[2] > need.get(key, (None, 0))[1]:
                            need[key] = (d, d[2])
                    for key, (d, val) in need.items():
                        if waited.get(key, 0) >= val:
                            continue
                        if d[0] == "e" and d[1] == eng and eng == "pe":
                            continue
                        h.wait_ge(semof(d), val)
                        waited[key] = val
                    ins = fn(h)
                    if tok is not None:
                        ins.then_inc(semof(tok), 1 if tok[0] == "e" else 16)
                if eng == "sp":
                    for t in self.last_dma.values():
                        if waited.get((t[0], t[1]), 0) < t[2]:
                            h.wait_ge(semof(t), t[2])

            @block.tensor
            def _(h):
                run("pe", h)

            @block.scalar
            def _(h):
                run("act", h)

            @block.vector
            def _(h):
                run("dve", h)

            @block.gpsimd
            def _(h):
                run("pool", h)

            @block.sync
            def _(h):
                run("sp", h)


class Slot:
    def __init__(self, R, ap):
        self.ap = ap
        self.sem = R.new_dma_sem()
        self.free = None
        self.ready = None


def build(T, NL, dbg=False):
    NT = T // 512
    NBK = T // 128
    HALF = min(1024, T)
    NHALF = T // HALF
    HT = HALF // 512
    nc = bass.Bass("TRN2", target_bir_lowering=False)
    R = Rec()
    din = lambda n, s, dt=F32: nc.dram_tensor(n, s, dt, kind="ExternalInput").ap()
    skind = "ExternalOutput" if dbg else "Internal"
    dsc = lambda n, s, dt: nc.dram_tensor(n, s, dt, kind=skind).ap()
    xT = din("xT", [D, T])
    w_in = din("w_in", [NL, D, INC])
    w_out = din("w_out", [NL, D, D])
    w_gate = din("w_gate", [NL, D, DFF])
    w_up = din("w_up", [NL, D, DFF])
    w_down = din("w_down", [NL, DC, 128, FC * 128])
    vecs_d = din("vecs", [128, NL * VL])
    lruw_d = din("lruw", [128, NL * 8 * 128])
    cst_d = din("cst", [128, 128 * 3 + 896])
    outT = nc.dram_tensor("outT", [D, T], F32, kind="ExternalOutput").ap()
    hT_d = dsc("hT_s", [D, T], F32)
    oT_d = dsc("oT_s", [D, T], F32)
    qT_d = dsc("qT_s", [ATT, T], BF16)
    kT_d = dsc("kT_s", [ATT, T], BF16)
    v_d = dsc("v_s", [T, ATT], BF16)
    glu_d = dsc("glu_s", [CCH, T], BF16)
    rx_d = dsc("rx_s", [CCH, T], BF16)
    gry_d = dsc("gry_s", [CCH, T], F32)
    yat_d = dsc("yat_s", [ATT, T], F32)
    fT_d = dsc("fT_s", [DFF, T], BF16)

    with ExitStack() as es:
        def sb(name, shape, dt):
            return es.enter_context(nc.sbuf_tensor(name, shape, dt))

        vecs = sb("vecs_sb", [128, NL * VL], F32)
        ident_f = sb("ident_f", [128, 128], F32)
        ident_b = sb("ident_b", [128, 128], BF16)
        tri_b = sb("tri_b", [128, 128], BF16)
        fix_b = sb("fix_b", [128, 128], BF16)
        mbig = sb("mbig", [128, 896], BF16)
        ones_f = sb("ones_f", [128, 128], F32)
        eps_t = sb("eps_t", [128, 1], F32)
        one_t = sb("one_t", [128, 1], F32)
        lruc = sb("lruc", [128, NL * 8], F32)
        lruw_b = sb("lruw_b", [128, 8 * 128], BF16)
        rstd = sb("rstd", [128, T], F32)
        A = sb("arenaA", [128, 16 * T], BF16)
        XA = sb("arenaX", [128, max(0, 44 * HALF - 16 * T) + 2], BF16)
        WA = sb("arenaW", [128, 66 * 256], F32)
        MI = sb("arenaM", [128, 6 * 1024], F32)
        ps = [es.enter_context(nc.psum_tensor("ps%d" % i, [128, 512], F32)) for i in range(8)]
        bank_free = [None] * 8

        uT = A[:, :].rearrange("p (c t) -> p c t", c=16)

        def wa_f32(off, n):
            return WA[:, off:off + n]

        def wa_bf16(off_f32, n_bf):
            return WA[:, off_f32:off_f32 + (n_bf + 1) // 2].bitcast(BF16)

        def mi_f32(off, n):
            return MI[:, off:off + n]

        def mi_bf16(off_f32, n_bf):
            return MI[:, off_f32:off_f32 + (n_bf + 1) // 2].bitcast(BF16)

        sem_misc = [R.new_dma_sem() for _ in range(4)]

        t_v = R.dma(lambda h: h.dma_start(out=vecs[:, :], in_=vecs_d[:, :]), sem_misc[0])
        cst = WA[:, 0:128 * 3 + 896]
        t_c = R.dma(lambda h: h.dma_start(out=cst, in_=cst_d[:, :]), sem_misc[1])
        R.op("dve", lambda h: h.tensor_copy(out=ident_b[:, :], in_=cst[:, 0:128]), deps=[t_c])
        R.op("dve", lambda h: h.tensor_copy(out=ident_f[:, :], in_=cst[:, 0:128]))
        R.op("dve", lambda h: h.tensor_copy(out=tri_b[:, :], in_=cst[:, 128:256]))
        R.op("dve", lambda h: h.tensor_copy(out=fix_b[:, :], in_=cst[:, 256:384]))
        R.op("dve", lambda h: h.tensor_copy(out=mbig[:, :], in_=cst[:, 384:384 + 896]))
        R.op("dve", lambda h: h.memset(ones_f[:, :], 1.0))
        R.op("dve", lambda h: h.memset(eps_t[:, :], EPS))
        R.op("dve", lambda h: h.memset(one_t[:, :], 1.0))
        for l in range(NL):
            lam = vecs[:, l * VL + 104:l * VL + 108]
            cc = lruc[:, l * 8:l * 8 + 4]
            c2 = lruc[:, l * 8 + 4:l * 8 + 8]
            ta_ = R.op("act", lambda h, lam=lam, cc=cc: h.activation(out=cc, in_=lam, func=AF.Exp, scale=-1.0), deps=[t_v])
            tb_ = R.op("act", lambda h, cc=cc: h.activation(out=cc, in_=cc, func=AF.Ln, bias=one_t[:, 0:1], scale=1.0),
                       deps=[("e", "dve", R.cnt["dve"]), ta_])
            t1 = R.op("act", lambda h, cc=cc, c2=c2: h.activation(out=c2, in_=cc, func=AF.Copy, scale=-16.0), deps=[tb_])
            R.op("act", lambda h, cc=cc: h.activation(out=cc, in_=cc, func=AF.Copy, scale=-8.0), deps=[t1])
        R.barrier()

        def get_bank(cands):
            b = cands[get_bank.i % len(cands)]
            get_bank.i += 1
            return b
        get_bank.i = 0

        def stats_pass(src_d, nchunks, hslots, sqslots):
            last = None
            for c in range(nchunks):
                hs = hslots[c % 2]
                ld = R.dma(lambda h, hs=hs, c=c: h.dma_start(out=hs.ap, in_=src_d[c * 128:(c + 1) * 128, :]), hs.sem, deps=[hs.free])
                sq = sqslots[c % 2]
                t_sq = R.op("act", lambda h, hs=hs, sq=sq: h.activation(out=sq.ap, in_=hs.ap, func=AF.Square), deps=[ld, sq.free])
                hs.free = t_sq
                for tt in range(NT):
                    last = R.op("pe", lambda h, tt=tt, sq=sq, c=c: h.matmul(ps[tt][:, :], lhsT=ones_f[:, :], rhs=sq.ap[:, tt * 512:(tt + 1) * 512], start=(c == 0), stop=(c == nchunks - 1)),
                                deps=[t_sq, bank_free[tt]] if True else [], sig=(tt == NT - 1))
                sq.free = last
            finish_rstd(last, nchunks * 128, range(NT), 0)

        def finish_rstd(tok, nfeat, banks, col0, width=512):
            t2 = None
            for i, b in enumerate(banks):
                dst = rstd[:, col0 + i * width:col0 + (i + 1) * width]
                t1 = R.op("act", lambda h, b=b, dst=dst: h.activation(out=dst, in_=ps[b][:, 0:width], func=AF.Ln, bias=eps_t[:, 0:1], scale=1.0 / nfeat), deps=[tok])
                bank_free[b] = t1
                t2 = R.op("act", lambda h, dst=dst: h.activation(out=dst, in_=dst, func=AF.Exp, scale=-0.5), deps=[t1])
            return t2

        def build_uT(src_d, gcol0, hslots, rstd_tok):
            toks = []
            for c in range(DC):
                hs = hslots[c % 2]
                ld = R.dma(lambda h, hs=hs, c=c: h.dma_start(out=hs.ap, in_=src_d[c * 128:(c + 1) * 128, :]), hs.sem, deps=[hs.free])
                t = R.op("dve", lambda h, hs=hs, c=c: h.scalar_tensor_tensor(out=uT[:, c, :], in0=hs.ap, scalar=vecs[:, gcol0 + c:gcol0 + c + 1], in1=rstd[:, :], op0=ALU.mult, op1=ALU.mult),
                         deps=[ld, rstd_tok])
                hs.free = t
                toks.append(t)
            return toks[-1]

        class WStream:
            def __init__(self, srcs, nrows, ncols, nstg, nwb=2, cast_eng="pool"):
                self.cast_eng = cast_eng
                self.srcs = srcs
                self.n = len(srcs)
                self.nrows, self.ncols = nrows, ncols
                sz = nrows * ncols
                self.stg = [Slot(R, wa_f32(i * sz, sz).rearrange("p (r c) -> p r c", r=nrows)) for i in range(nstg)]
                off = nstg * sz
                self.wb = [Slot(R, wa_bf16(off + i * (sz // 2), sz).rearrange("p (r c) -> p r c", r=nrows)) for i in range(nwb)]
                assert off + nwb * (sz // 2) <= 66 * 256
                self.nstg = nstg
                self.nwb = nwb
                self.L, self.C = {}, {}
                self.used = {}

            def load(self, s):
                if s >= self.n or s in self.L:
                    return
                st = self.stg[s % self.nstg]
                self.L[s] = R.dma(lambda h, st=st, s=s: h.dma_start(out=st.ap, in_=self.srcs[s]), st.sem, deps=[st.free])

            def cast(self, s):
                if s >= self.n or s in self.C:
                    return
                st = self.stg[s % self.nstg]
                wb = self.wb[s % self.nwb]
                if self.cast_eng == "act":
                    self.C[s] = R.op("act", lambda h, st=st, wb=wb: h.activation(out=wb.ap, in_=st.ap, func=AF.Copy), deps=[self.L[s], wb.free])
                else:
                    self.C[s] = R.op("pool", lambda h, st=st, wb=wb: h.tensor_copy(out=wb.ap, in_=st.ap), deps=[self.L[s], wb.free])
                st.free = self.C[s]

            def start(self):
                for s in range(min(self.nstg, self.n)):
                    self.load(s)
                self.cast(0)

            def begin(self, s):
                self.cast(s + 1)
                return self.wb[s % self.nwb].ap, self.C[s]

            def end(self, s, last_mm_tok):
                self.wb[s % self.nwb].free = last_mm_tok
                self.load(s + self.nstg)

        def colslabs(w2d, col0, ncols, width):
            src = w2d.rearrange("(c p) n -> p c n", p=128)
            return [src[:, :, c:c + width] for c in range(col0, col0 + ncols, width)]

        sem_base = len(R.dma_cnt)

        def layer(l, h_src):
            R.sem_cursor = sem_base
            V0 = l * VL
            last_layer = (l == NL - 1)
            h_dst = outT if last_layer else hT_d

            hsl = [Slot(R, mi_f32(i * T, T)) for i in range(2)]
            sqs = [Slot(R, wa_f32(i * T, T)) for i in range(2)]
            if l == 0:
                stats_pass(h_src, DC, hsl, sqs)
                rstd_tok = ("e", "act", R.cnt["act"])
                build_uT(h_src, V0 + 0, hsl, rstd_tok)
                R.barrier()

            wl = w_in[l]
            order = []
            for s in range(4):
                order.append(("q", s * 256))
            for s in range(4):
                order.append(("k", 1024 + s * 256))
            for s in range(4):
                order.append(("v", 2048 + s * 256))
            for s in range(2):
                order.append(("cg", 3584 + s * 256))
                order.append(("cv", 3072 + s * 256))
            for s in range(2):
                order.append(("rx", 4096 + s * 256))
            for s in range(2):
                order.append(("ry", 4608 + s * 256))
            srcw = wl.rearrange("(c p) n -> p c n", p=128)
            ws = WStream([srcw[:, :, c0:c0 + 256] for _, c0 in order], 16, 256, 3)
            obs = [Slot(R, mi_f32(i * T, T)) for i in range(2)]
            sig = [XA[:, i * 2 * T:(i + 1) * 2 * T].bitcast(F32) for i in range(2)]
            tmpA = mi_f32(2 * T, 512)
            tmpB = mi_f32(2 * T + 512, 512)
            vbuf = Slot(R, XA[:, 4 * T:4 * T + NBK * 256].rearrange("p (n c) -> p n c", n=NBK))
            oi = 0
            sig_tok = [None, None]
            tmp_free = None
            ws.start()
            for s, (kind, c0) in enumerate(order):
                wb, wtok = ws.begin(s)
                lastmm = None
                if kind == "v":
                    vb = vbuf
                    evs = []
                    for tk in range(NBK):
                        b = get_bank([4, 5, 6, 7])
                        for dc in range(DC):
                            lastmm = R.op("pe", lambda h, b=b, dc=dc, tk=tk, wb=wb: h.matmul(ps[b][:, 0:256], lhsT=uT[:, dc, tk * 128:(tk + 1) * 128], rhs=wb[:, dc, :], start=(dc == 0), stop=(dc == DC - 1)),
                                          deps=[wtok, bank_free[b]] if dc == 0 else [], sig=(dc == DC - 1))
                        ev = R.op("act", lambda h, b=b, tk=tk, vb=vb: h.activation(out=vb.ap[:, tk, :], in_=ps[b][:, 0:256], func=AF.Copy), deps=[lastmm, vb.free])
                        bank_free[b] = ev
                        evs.append(ev)
                    vc0 = c0 - 2048
                    vb.free = R.dma(lambda h, vb=vb, vc0=vc0: h.dma_start(out=v_d.rearrange("(n p) c -> p n c", p=128)[:, :, vc0:vc0 + 256], in_=vb.ap), vb.sem, deps=[evs[-1]])
                else:
                    for ch in range(2):
                        ob = obs[oi % 2]
                        oi += 1
                        ob_bf = ob.ap.bitcast(BF16)[:, 0:T]
                        evl = None
                        for tt in range(NT):
                            b = get_bank([4, 5, 6, 7])
                            for dc in range(DC):
                                lastmm = R.op("pe", lambda h, b=b, dc=dc, tt=tt, wb=wb, ch=ch: h.matmul(ps[b][:, :], lhsT=wb[:, dc, ch * 128:(ch + 1) * 128], rhs=uT[:, dc, tt * 512:(tt + 1) * 512], start=(dc == 0), stop=(dc == DC - 1)),
                                              deps=[wtok, bank_free[b]] if dc == 0 else [], sig=(dc == DC - 1))
                            sl = slice(tt * 512, (tt + 1) * 512)
                            P = ps[b]
                            if kind == "q":
                                evl = R.op("act", lambda h, P=P, ob_bf=ob_bf, sl=sl: h.activation(out=ob_bf[:, sl], in_=P[:, :], func=AF.Copy, scale=128.0 ** -0.5), deps=[lastmm, ob.free])
                            elif kind in ("k", "rx"):
                                evl = R.op("act", lambda h, P=P, ob_bf=ob_bf, sl=sl: h.activation(out=ob_bf[:, sl], in_=P[:, :], func=AF.Copy), deps=[lastmm, ob.free])
                            elif kind == "cg":
                                evl = R.op("act", lambda h, P=P, ch=ch, sl=sl: h.activation(out=sig[ch][:, sl], in_=P[:, :], func=AF.Sigmoid), deps=[lastmm, sig_tok[ch]])
                            elif kind == "cv":
                                evl = R.op("dve", lambda h, P=P, ch=ch, sl=sl, ob_bf=ob_bf: h.tensor_tensor(out=ob_bf[:, sl], in0=P[:, :], in1=sig[ch][:, sl], op=ALU.mult), deps=[lastmm, ob.free, sig_ready[ch]])
                                sig_tok[ch] = evl
                            elif kind == "ry":
                                t1 = R.op("act", lambda h, P=P: h.activation(out=tmpA, in_=P[:, :], func=AF.Square), deps=[lastmm, tmp_free])
                                t2 = R.op("dve", lambda h: h.tensor_scalar(out=tmpA, in0=tmpA, scalar1=0.044715, scalar2=1.0, op0=ALU.mult, op1=ALU.add), deps=[t1])
                                t3 = R.op("dve", lambda h, P=P: h.tensor_tensor(out=tmpB, in0=P[:, :], in1=tmpA, op=ALU.mult), deps=[t2])
                                t4 = R.op("act", lambda h: h.activation(out=tmpA, in_=tmpB, func=AF.Sigmoid, scale=1.5957691216), deps=[t3])
                                evl = R.op("dve", lambda h, P=P, ob=ob, sl=sl: h.tensor_tensor(out=ob.ap[:, sl], in0=P[:, :], in1=tmpA, op=ALU.mult), deps=[t4, ob.free])
                                tmp_free = evl
                            bank_free[b] = evl
                        if kind == "cg":
                            if ch == 0:
                                sig_ready = [None, None]
                            sig_ready[ch] = evl
                            oi -= 1
                            continue
                        r0 = (c0 + ch * 128)
                        if kind == "q":
                            dst = qT_d[r0:r0 + 128, :]
                        elif kind == "k":
                            dst = kT_d[r0 - 1024:r0 - 1024 + 128, :]
                        elif kind == "cv":
                            dst = glu_d[r0 - 3072:r0 - 3072 + 128, :]
                        elif kind == "rx":
                            dst = rx_d[r0 - 4096:r0 - 4096 + 128, :]
                        elif kind == "ry":
                            dst = gry_d[r0 - 4608:r0 - 4608 + 128, :]
                        src_ap = ob.ap if kind == "ry" else ob_bf
                        ob.free = R.dma(lambda h, dst=dst, src_ap=src_ap: h.dma_start(out=dst, in_=src_ap), ob.sem, deps=[evl])
                ws.end(s, lastmm)
            R.barrier()

            off = 0
            qs, ks, vs = [], [], []
            for i in range(2):
                qs.append(Slot(R, wa_bf16(off, T))); off += T // 2
                ks.append(Slot(R, wa_bf16(off, T))); off += T // 2
                vs.append(Slot(R, wa_bf16(off, NBK * 128).rearrange("p (n d) -> p n d", n=NBK))); off += NBK * 64
            ebuf = [wa_f32(off + i * 512, 512) for i in range(4)]; off += 4 * 512
            ecbuf = [wa_f32(off + i * 512, 512) for i in range(4)]; off += 4 * 512
            spbuf = [wa_bf16(off + i * 256, 512) for i in range(4)]; off += 4 * 256
            wbuf = [wa_bf16(off + i * 256, 512) for i in range(4)]; off += 4 * 256
            ybuf = [Slot(R, wa_f32(off + i * T, T)) for i in range(2)]; off += 2 * T
            assert off <= 66 * 256, off
            jobs = []
            for hh in range(NH):
                for c in range(NT):
                    nb = 4 * (c + 1)
                    for i in range(nb - 1, -1, -1):
                        jobs.append((hh, c, i, i == nb - 1, i == 0, (i - 4 * c) if i >= 4 * c else -1))
            NJ = len(jobs)
            Ztok, Etok, Ltok, Ttok, Xtok, Mtok, Otok = {}, {}, {}, {}, {}, {}, {}
            head_ld = {}
            PZ = [0, 1]
            PP = [2, 3]
            PO = [4, 5]
            stream_of = {}
            sidx = -1
            for n, jb in enumerate(jobs):
                if jb[3]:
                    sidx += 1
                stream_of[n] = sidx
            y_evac = {}

            def load_head(hh):
                if hh >= NH or hh in head_ld:
                    return
                i = hh % 2
                t1 = R.dma(lambda h, i=i, hh=hh: h.dma_start(out=qs[i].ap, in_=qT_d[hh * 128:(hh + 1) * 128, :]), qs[i].sem, deps=[qs[i].free])
                t2 = R.dma(lambda h, i=i, hh=hh: h.dma_start(out=ks[i].ap, in_=kT_d[hh * 128:(hh + 1) * 128, :]), ks[i].sem, deps=[ks[i].free])
                t3 = R.dma(lambda h, i=i, hh=hh: h.dma_start(out=vs[i].ap, in_=v_d.rearrange("(n p) c -> p n c", p=128)[:, :, hh * 128:(hh + 1) * 128]), vs[i].sem, deps=[vs[i].free])
                head_ld[hh] = [t1, t2, t3]

            def stageZ(n):
                if n >= NJ:
                    return
                hh, c, i, first, last, dg = jobs[n]
                hi = hh % 2
                b = PZ[n % 2]
                deps = [bank_free[b]] + head_ld[hh][0:2]
                tk = R.op("pe", lambda h, b=b, hi=hi, i=i, c=c: h.matmul(ps[b][:, :], lhsT=ks[hi].ap[:, i * 128:(i + 1) * 128], rhs=qs[hi].ap[:, c * 512:(c + 1) * 512], start=True, stop=(dg < 0)),
                          deps=deps, sig=(dg < 0))
                if dg >= 0:
                    o0 = 384 - 128 * dg
                    tk = R.op("pe", lambda h, b=b, o0=o0: h.matmul(ps[b][:, :], lhsT=ident_b[:, :], rhs=mbig[:, o0:o0 + 512], start=False, stop=True))
                Ztok[n] = tk
                eb, spb = ebuf[n % 4], spbuf[n % 4]
                Etok[n] = R.op("act", lambda h, b=b, eb=eb: h.activation(out=eb, in_=ps[b][:, :], func=AF.Exp), deps=[tk, Mtok.get(n - 4)])
                bank_free[b] = Etok[n]
                Ltok[n] = R.op("act", lambda h, eb=eb, spb=spb: h.activation(out=spb, in_=eb, func=AF.Ln, bias=one_t[:, 0:1], scale=1.0), deps=[Etok[n], Ttok.get(n - 4), Ftok.get(n - 4)])

            Ftok = {}

            def stageT(n):
                if n >= NJ:
                    return
                hh, c, i, first, last, dg = jobs[n]
                b = PP[stream_of[n] % 2]
                spb = spbuf[n % 4]
                Ttok[n] = R.op("pe", lambda h, b=b, spb=spb, first=first: h.matmul(ps[b][:, :], lhsT=tri_b[:, :], rhs=spb, start=first, stop=True, skip_group_check=True),
                               deps=[Ltok[n], Xtok.get(n - 1), bank_free[b] if first else None])

            def stageF(n):
                hh, c, i, first, last, dg = jobs[n]
                if last:
                    return
                b = PP[stream_of[n] % 2]
                spb = spbuf[n % 4]
                Ftok[n] = R.op("pe", lambda h, b=b, spb=spb: h.matmul(ps[b][:, :], lhsT=fix_b[:, :], rhs=spb, start=False, stop=True, skip_group_check=True),
                               deps=[Xtok[n]])

            def stageX(n):
                if n >= NJ:
                    return
                hh, c, i, first, last, dg = jobs[n]
                b = PP[stream_of[n] % 2]
                ecb = ecbuf[n % 4]
                Xtok[n] = R.op("act", lambda h, b=b, ecb=ecb: h.activation(out=ecb, in_=ps[b][:, :], func=AF.Exp, scale=-1.0), deps=[Ttok[n], Mtok.get(n - 4)])
                if last:
                    bank_free[b] = Xtok[n]

            def stageM(n):
                eb, ecb, wbf = ebuf[n % 4], ecbuf[n % 4], wbuf[n % 4]
                Mtok[n] = R.op("dve", lambda h, eb=eb, ecb=ecb, wbf=wbf: h.tensor_tensor(out=wbf, in0=eb, in1=ecb, op=ALU.mult), deps=[Xtok[n], Etok[n], Otok.get(n - 4)])

            def stageO(n):
                hh, c, i, first, last, dg = jobs[n]
                hi = hh % 2
                b = PO[stream_of[n] % 2]
                wbf = wbuf[n % 4]
                Otok[n] = R.op("pe", lambda h, b=b, hi=hi, i=i, wbf=wbf, first=first, last=last: h.matmul(ps[b][:, :], lhsT=vs[hi].ap[:, i, :], rhs=wbf, start=first, stop=last),
                               deps=[Mtok[n], head_ld[hh][2], bank_free[b] if first else None])
                if last:
                    yb = ybuf[hi]
                    ev = R.op("dve", lambda h, b=b, yb=yb, c=c: h.tensor_copy(out=yb.ap[:, c * 512:(c + 1) * 512], in_=ps[b][:, :]), deps=[Otok[n], yb.free if c == 0 else None])
                    bank_free[b] = ev
                    if c == NT - 1:
                        yb.free = R.dma(lambda h, yb=yb, hh=hh: h.dma_start(out=yat_d[hh * 128:(hh + 1) * 128, :], in_=yb.ap), yb.sem, deps=[ev])
                        qs[hi].free = Otok[n]
                        ks[hi].free = Otok[n]
                        vs[hi].free = Otok[n]
                        load_head(hh + 2)

            load_head(0)
            load_head(1)
            stageZ(0)
            stageZ(1)
            stageT(0)
            stageX(0)
            for n in range(NJ):
                stageZ(n + 2)
                stageF(n)
                stageT(n + 1)
                stageX(n + 1)
                stageM(n)
                stageO(n)
            R.barrier()

            mixT = uT
            off = 0
            glus = []
            for i in range(2):
                glus.append(Slot(R, wa_bf16(off, T + 32))); off += (T + 32) // 2
            diags = [wa_bf16(off + i * 31 * 64, 31 * 128).rearrange("p (k c) -> p k c", k=31) for i in range(2)]; off += 2 * 31 * 64
            ycv = wa_f32(off, 4 * T).rearrange("p (j t) -> p j t", j=4); off += 4 * T
            mu = mi_f32(0, T)
            sq2 = [wa_f32(off + i * 512, 512) for i in range(2)]; off += 1024
            assert off <= 66 * 256, off
            for i in range(2):
                glus[i].free = R.op("dve", lambda h, i=i: h.memset(glus[i].ap[:, 0:32], 0.0))
            diag_free = [None, None]
            last_cv = None
            dts_of = {}
            ld_of = {}

            def conv_prep(j):
                if j >= 4:
                    return
                gs_ = glus[j % 2]
                ld_of[j] = R.dma(lambda h, gs_=gs_, j=j: h.dma_start(out=gs_.ap[:, 32:32 + T], in_=glu_d[j * 128:(j + 1) * 128, :]), gs_.sem, deps=[gs_.free])
                dg_ = diags[j % 2]
                t_ = None
                for k in range(31):
                    wc = vecs[:, V0 + 108 + j * 31 + k:V0 + 108 + j * 31 + k + 1]
                    t_ = R.op("act", lambda h, k=k, wc=wc, dg_=dg_: h.activation(out=dg_[:, k, :], in_=ident_f[:, :], func=AF.Identity, scale=wc), deps=[diag_free[j % 2]] if k == 0 else [], sig=(k == 30))
                dts_of[j] = t_

            conv_prep(0)
            for j in range(4):
                diag = diags[j % 2]
                gs = glus[j % 2]
                ld, dts = ld_of[j], dts_of[j]
                lm = None
                for tt in range(NT):
                    b = get_bank([0, 1, 2, 3])
                    for k in range(31):
                        s0 = 32 + tt * 512 - 30 + k
                        lm = R.op("pe", lambda h, b=b, k=k, s0=s0, gs=gs, diag=diag: h.matmul(ps[b][:, :], lhsT=diag[:, k, :], rhs=gs.ap[:, s0:s0 + 512], start=(k == 0), stop=(k == 30)),
                                  deps=[ld, dts, bank_free[b]] if k == 0 else [], sig=(k == 30))
                    if tt == 0:
                        conv_prep(j + 1)
                    ev = R.op("act", lambda h, b=b, j=j, tt=tt: h.activation(out=ycv[:, j, tt * 512:(tt + 1) * 512], in_=ps[b][:, :], func=AF.Identity, bias=vecs[:, V0 + 80 + j:V0 + 81 + j], scale=1.0), deps=[lm])
                    bank_free[b] = ev
                    last_cv = ev
                gs.free = lm
                diag_free[j % 2] = lm
            for tt in range(NT):
                sl = slice(tt * 512, (tt + 1) * 512)
                b1 = get_bank([4, 5, 6, 7])
                b2 = get_bank([0, 1, 2, 3])
                t_s1 = t_s2 = None
                for j in range(4):
                    t_s1 = R.op("pe", lambda h, b1=b1, j=j, sl=sl: h.matmul(ps[b1][:, :], lhsT=ones_f[:, :], rhs=ycv[:, j, sl], start=(j == 0), stop=(j == 3)), deps=[last_cv, bank_free[b1]] if j == 0 else [], sig=(j == 3))
                for j in range(4):
                    sq = sq2[j % 2]
                    tq = R.op("dve", lambda h, sq=sq, j=j, sl=sl: h.tensor_tensor(out=sq, in0=ycv[:, j, sl], in1=ycv[:, j, sl], op=ALU.mult), deps=[last_cv, t_s2, ("e", "pe", R.cnt["pe"])])
                    t_s2 = R.op("pe", lambda h, b2=b2, j=j, sq=sq: h.matmul(ps[b2][:, :], lhsT=ones_f[:, :], rhs=sq, start=(j == 0), stop=(j == 3)), deps=[tq, bank_free[b2]] if j == 0 else [tq])
                t_mu = R.op("act", lambda h, b1=b1, sl=sl: h.activation(out=mu[:, sl], in_=ps[b1][:, :], func=AF.Copy, scale=1.0 / 512), deps=[t_s1])
                bank_free[b1] = t_mu
                t_m2 = R.op("dve", lambda h, sl=sl: h.tensor_tensor(out=sq2[0], in0=mu[:, sl], in1=mu[:, sl], op=ALU.mult), deps=[t_mu, t_s2])
                t_var = R.op("dve", lambda h, b2=b2, sl=sl: h.scalar_tensor_tensor(out=rstd[:, sl], in0=ps[b2][:, :], scalar=1.0 / 512, in1=sq2[0], op0=ALU.mult, op1=ALU.subtract), deps=[t_m2])
                bank_free[b2] = t_var
                t_l = R.op("act", lambda h, sl=sl: h.activation(out=rstd[:, sl], in_=rstd[:, sl], func=AF.Ln, bias=eps_t[:, 0:1], scale=1.0), deps=[t_var])
                t_r = R.op("act", lambda h, sl=sl: h.activation(out=rstd[:, sl], in_=rstd[:, sl], func=AF.Exp, scale=-0.5), deps=[t_l])
                b3 = get_bank([0, 1, 2, 3])
                t_g = None
                for j in range(4):
                    ta = R.op("dve", lambda h, j=j, sl=sl: h.tensor_tensor(out=ycv[:, j, sl], in0=ycv[:, j, sl], in1=mu[:, sl], op=ALU.subtract), deps=[t_r, t_s1, t_s2])
                    tb = R.op("dve", lambda h, j=j, sl=sl: h.tensor_tensor(out=ycv[:, j, sl], in0=ycv[:, j, sl], in1=rstd[:, sl], op=ALU.mult), deps=[ta])
                    tc = R.op("act", lambda h, j=j, sl=sl: h.activation(out=ycv[:, j, sl], in_=ycv[:, j, sl], func=AF.Silu, bias=vecs[:, V0 + 88 + j:V0 + 89 + j], scale=vecs[:, V0 + 84 + j:V0 + 85 + j]), deps=[tb])
                    sq = sq2[j % 2]
                    tq = R.op("dve", lambda h, sq=sq, j=j, sl=sl: h.tensor_tensor(out=sq, in0=ycv[:, j, sl], in1=ycv[:, j, sl], op=ALU.mult), deps=[tc, t_g])
                    t_g = R.op("pe", lambda h, b3=b3, j=j, sq=sq: h.matmul(ps[b3][:, :], lhsT=ones_f[:, :], rhs=sq, start=(j == 0), stop=(j == 3)), deps=[tq, bank_free[b3]] if j == 0 else [tq])
                t_l = R.op("act", lambda h, b3=b3, sl=sl: h.activation(out=rstd[:, sl], in_=ps[b3][:, :], func=AF.Ln, bias=eps_t[:, 0:1], scale=1.0 / 512), deps=[t_g])
                bank_free[b3] = t_l
                t_r = R.op("act", lambda h, sl=sl: h.activation(out=rstd[:, sl], in_=rstd[:, sl], func=AF.Exp, scale=-0.5), deps=[t_l])
                for j in range(4):
                    R.op("dve", lambda h, j=j, sl=sl: h.scalar_tensor_tensor(out=mixT[:, 8 + j, sl], in0=ycv[:, j, sl], scalar=vecs[:, V0 + 72 + j:V0 + 73 + j], in1=rstd[:, sl], op0=ALU.mult, op1=ALU.mult), deps=[t_r])
            R.barrier()

            off = 0
            rxs = []
            for i in range(2):
                rxs.append(Slot(R, wa_bf16(off, T + 4))); off += (T + 4) // 2
            dg4 = wa_bf16(off, 4 * 128).rearrange("p (k c) -> p k c", k=4); off += 4 * 64
            xr = wa_f32(off, T); off += T
            xrb = wa_bf16(off, T); off += T // 2
            ga = wa_f32(off, T); off += T
            gi = wa_f32(off, T); off += T
            a2 = wa_f32(off, T); off += T
            ylr = A[:, 0:8 * T].bitcast(F32).rearrange("p (j t) -> p j t", j=4)
            lruw_f = XA[:, 0:2048].bitcast(F32)
            grs = [Slot(R, wa_f32(off + i * T, T)) for i in range(2)]; off += 2 * T
            assert off <= 66 * 256, off
            t_lw = R.dma(lambda h: h.dma_start(out=lruw_f[:, :], in_=lruw_d[:, l * 1024:(l + 1) * 1024]), sem_misc[2])
            t_lwb = R.op("dve", lambda h: h.tensor_copy(out=lruw_b[:, :], in_=lruw_f[:, :]), deps=[t_lw])
            for i in range(2):
                rxs[i].free = R.op("dve", lambda h, i=i: h.memset(rxs[i].ap[:, 0:4], 0.0))
            dfree = None
            grp_last = None
            for j in range(4):
                rs = rxs[j % 2]
                ld = R.dma(lambda h, rs=rs, j=j: h.dma_start(out=rs.ap[:, 4:4 + T], in_=rx_d[j * 128:(j + 1) * 128, :]), rs.sem, deps=[rs.free])
                gr = grs[j % 2]
                ldg = R.dma(lambda h, gr=gr, j=j: h.dma_start(out=gr.ap, in_=gry_d[j * 128:(j + 1) * 128, :]), gr.sem, deps=[gr.free])
                dts = None
                for k in range(4):
                    wc = vecs[:, V0 + 232 + j * 4 + k:V0 + 232 + j * 4 + k + 1]
                    dts = R.op("act", lambda h, k=k, wc=wc: h.activation(out=dg4[:, k, :], in_=ident_f[:, :], func=AF.Identity, scale=wc), deps=[dfree] if k == 0 else [], sig=(k == 3))
                lm = None
                prev = grp_last
                for tt in range(NT):
                    sl = slice(tt * 512, (tt + 1) * 512)
                    b = get_bank([0, 1, 2, 3])
                    for k in range(4):
                        s0 = 4 + tt * 512 - 3 + k
                        lm = R.op("pe", lambda h, b=b, k=k, s0=s0, rs=rs: h.matmul(ps[b][:, :], lhsT=dg4[:, k, :], rhs=rs.ap[:, s0:s0 + 512], start=(k == 0), stop=(k == 3)),
                                  deps=[ld, dts, bank_free[b]] if k == 0 else [], sig=(k == 3))
                    t_x = R.op("act", lambda h, b=b, sl=sl, j=j: h.activation(out=xr[:, sl], in_=ps[b][:, :], func=AF.Identity, bias=vecs[:, V0 + 92 + j:V0 + 93 + j], scale=1.0), deps=[lm, prev])
                    bank_free[b] = t_x
                    t_xb = R.op("dve", lambda h, sl=sl: h.tensor_copy(out=xrb[:, sl], in_=xr[:, sl]), deps=[t_x, prev])
                    ba = get_bank([4, 5, 6, 7])
                    bi = get_bank([4, 5, 6, 7])
                    t_ga = R.op("pe", lambda h, ba=ba, sl=sl, j=j: h.matmul(ps[ba][:, :], lhsT=lruw_b[:, j * 128:(j + 1) * 128], rhs=xrb[:, sl], start=True, stop=True), deps=[t_xb, t_lwb, bank_free[ba]])
                    t_gi = R.op("pe", lambda h, bi=bi, sl=sl, j=j: h.matmul(ps[bi][:, :], lhsT=lruw_b[:, (4 + j) * 128:(5 + j) * 128], rhs=xrb[:, sl], start=True, stop=True), deps=[t_xb, bank_free[bi]])
                    t_r = R.op("act", lambda h, ba=ba, sl=sl, j=j: h.activation(out=ga[:, sl], in_=ps[ba][:, :], func=AF.Sigmoid, bias=vecs[:, V0 + 96 + j:V0 + 97 + j], scale=1.0), deps=[t_ga, prev])
                    bank_free[ba] = t_r
                    t_i = R.op("act", lambda h, bi=bi, sl=sl, j=j: h.activation(out=gi[:, sl], in_=ps[bi][:, :], func=AF.Sigmoid, bias=vecs[:, V0 + 100 + j:V0 + 101 + j], scale=1.0), deps=[t_gi, prev])
                    bank_free[bi] = t_i
                rs.free = lm
                dfree = lm
                t_a2 = R.op("act", lambda h, j=j: h.activation(out=a2[:, :], in_=ga[:, :], func=AF.Exp, scale=lruc[:, l * 8 + 4 + j:l * 8 + 5 + j]), deps=[t_r, t_i, prev])
                t_a = R.op("act", lambda h, j=j: h.activation(out=ga[:, :], in_=ga[:, :], func=AF.Exp, scale=lruc[:, l * 8 + j:l * 8 + j + 1]), deps=[t_a2])
                t_om = R.op("dve", lambda h: h.tensor_scalar(out=a2[:, :], in0=a2[:, :], scalar1=-1.0, scalar2=1.0, op0=ALU.mult, op1=ALU.add), deps=[t_a2])
                t_om = R.op("dve", lambda h: h.tensor_scalar(out=a2[:, :], in0=a2[:, :], scalar1=1e-30, scalar2=None, op0=ALU.max), deps=[t_om])
                t_ln = R.op("act", lambda h: h.activation(out=a2[:, :], in_=a2[:, :], func=AF.Ln), deps=[t_om, t_a])
                t_sq = R.op("act", lambda h: h.activation(out=a2[:, :], in_=a2[:, :], func=AF.Exp, scale=0.5), deps=[t_ln])
                t_b1 = R.op("dve", lambda h: h.tensor_tensor(out=gi[:, :], in0=gi[:, :], in1=xr[:, :], op=ALU.mult), deps=[t_i, t_x])
                t_b2 = R.op("dve", lambda h: h.tensor_tensor(out=gi[:, :], in0=gi[:, :], in1=a2[:, :], op=ALU.mult), deps=[t_b1, t_sq])
                t_h = R.op("dve", lambda h, j=j: h.tensor_tensor_scan(out=ylr[:, j, :], data0=ga[:, :], data1=gi[:, :], initial=0.0, op0=ALU.mult, op1=ALU.add), deps=[t_b2, t_a])
                t_y = R.op("dve", lambda h, j=j, gr=gr: h.tensor_tensor(out=ylr[:, j, :], in0=ylr[:, j, :], in1=gr.ap, op=ALU.mult), deps=[t_h, ldg])
                gr.free = t_y
                grp_last = t_y
            for tt in range(NT):
                sl = slice(tt * 512, (tt + 1) * 512)
                b3 = get_bank([0, 1, 2, 3])
                t_g = None
                for j in range(4):
                    sqb = (xr if j % 2 == 0 else a2)[:, 0:512]
                    tq = R.op("dve", lambda h, sqb=sqb, j=j, sl=sl: h.tensor_tensor(out=sqb, in0=ylr[:, j, sl], in1=ylr[:, j, sl], op=ALU.mult), deps=[grp_last, t_g])
                    t_g = R.op("pe", lambda h, b3=b3, j=j, sqb=sqb: h.matmul(ps[b3][:, :], lhsT=ones_f[:, :], rhs=sqb, start=(j == 0), stop=(j == 3)), deps=[tq, bank_free[b3]] if j == 0 else [tq])
                t_l = R.op("act", lambda h, b3=b3, sl=sl: h.activation(out=rstd[:, sl], in_=ps[b3][:, :], func=AF.Ln, bias=eps_t[:, 0:1], scale=1.0 / 512), deps=[t_g])
                bank_free[b3] = t_l
                t_r = R.op("act", lambda h, sl=sl: h.activation(out=rstd[:, sl], in_=rstd[:, sl], func=AF.Exp, scale=-0.5), deps=[t_l])
                for j in range(4):
                    R.op("dve", lambda h, j=j, sl=sl: h.scalar_tensor_tensor(out=mixT[:, 12 + j, sl], in0=ylr[:, j, sl], scalar=vecs[:, V0 + 76 + j:V0 + 77 + j], in1=rstd[:, sl], op0=ALU.mult, op1=ALU.mult), deps=[t_r])
            R.barrier()

            hsl = [Slot(R, mi_f32(i * T, T)) for i in range(2)]
            sqs = [Slot(R, wa_f32(i * T, T)) for i in range(2)]
            stats_pass(yat_d, NH, hsl, sqs)
            rstd_tok = ("e", "act", R.cnt["act"])
            for c in range(NH):
                hs = hsl[c % 2]
                ld = R.dma(lambda h, hs=hs, c=c: h.dma_start(out=hs.ap, in_=yat_d[c * 128:(c + 1) * 128, :]), hs.sem, deps=[hs.free])
                hs.free = R.op("dve", lambda h, hs=hs, c=c: h.scalar_tensor_tensor(out=mixT[:, c, :], in0=hs.ap, scalar=vecs[:, V0 + 64 + c:V0 + 65 + c], in1=rstd[:, :], op0=ALU.mult, op1=ALU.mult), deps=[ld, rstd_tok])
            R.barrier()

            def proj_postnorm(ws, nk, rhs_of, tok_tiles, col_of_tile):
                pass

            def out_proj(wmat, kchunks, rhs_fn, tiles, wslabs_builder):
                pass

            srcw = w_out[l].rearrange("(c p) n -> p c n", p=128)
            ws = WStream([srcw[:, :, c0:c0 + 256] for c0 in range(0, D, 256)], 16, 256, 3)
            obs = [Slot(R, mi_f32(i * T, T)) for i in range(2)]
            sqs = [Slot(R, mi_f32(2 * T + i * 512, 512)) for i in range(2)]
            oi = 0
            ws.start()
            ssq_st = {"last": None}
            pend = []
            nsq = 0
            for s in range(8):
                wb, wtok = ws.begin(s)
                lastmm = None
                for ch in range(2):
                    ob = obs[oi % 2]
                    oi += 1
                    evl = None
                    for tt in range(NT):
                        b = get_bank([4, 5, 6, 7])
                        sl = slice(tt * 512, (tt + 1) * 512)
                        for kc in range(DC):
                            lastmm = R.op("pe", lambda h, b=b, kc=kc, sl=sl, wb=wb, ch=ch: h.matmul(ps[b][:, :], lhsT=wb[:, kc, ch * 128:(ch + 1) * 128], rhs=mixT[:, kc, sl], start=(kc == 0), stop=(kc == DC - 1)),
                                          deps=[wtok, bank_free[b]] if kc == 0 else [], sig=(kc == DC - 1))
                        for fn_ in pend:
                            fn_()
                        pend = []
                        evl = R.op("act", lambda h, b=b, ob=ob, sl=sl: h.activation(out=ob.ap[:, sl], in_=ps[b][:, :], func=AF.Copy), deps=[lastmm, ob.free])
                        bank_free[b] = evl
                        sq = sqs[nsq % 2]
                        nsq += 1
                        tq = R.op("dve", lambda h, sq=sq, ob=ob, sl=sl: h.tensor_tensor(out=sq.ap, in0=ob.ap[:, sl], in1=ob.ap[:, sl], op=ALU.mult), deps=[evl, sq.free])
                        first = (s == 0 and ch == 0)
                        lastc = (s == 7 and ch == 1)

                        def ssq_(tt=tt, sq=sq, first=first, lastc=lastc, tq=tq):
                            t_ = R.op("pe", lambda h: h.matmul(ps[tt][:, :], lhsT=ones_f[:, :], rhs=sq.ap, start=first, stop=lastc, skip_group_check=True), deps=[tq, bank_free[tt]] if first else [tq])
                            sq.free = t_
                            ssq_st["last"] = t_
                        pend.append(ssq_)
                    r0 = s * 256 + ch * 128
                    ob.free = R.dma(lambda h, ob=ob, r0=r0: h.dma_start(out=oT_d[r0:r0 + 128, :], in_=ob.ap), ob.sem, deps=[evl, tq])
                ws.end(s, lastmm)
            for fn_ in pend:
                fn_()
            finish_rstd(ssq_st["last"], D, range(NT), 0)
            R.barrier()

            def residual_pass(hsrc_d, hdst_d, gcol0, want_stats, gnext=None):
                hsl = [Slot(R, mi_f32(i * T, T)) for i in range(2)]
                osl = [Slot(R, wa_f32(i * T, T)) for i in range(2)]
                sqs = [Slot(R, wa_f32(2 * T + i * T, T)) for i in range(2)]
                nrm = wa_f32(4 * T, T)
                rstd_tok = ("e", "act", R.cnt["act"])
                last = None
                t_add_prev = None
                for c in range(DC):
                    hs, os_ = hsl[c % 2], osl[c % 2]
                    ld1 = R.dma(lambda h, hs=hs, c=c: h.dma_start(out=hs.ap, in_=hsrc_d[c * 128:(c + 1) * 128, :]), hs.sem, deps=[hs.free])
                    ld2 = R.dma(lambda h, os_=os_, c=c: h.dma_start(out=os_.ap, in_=oT_d[c * 128:(c + 1) * 128, :]), os_.sem, deps=[os_.free])
                    t_n = R.op("dve", lambda h, os_=os_, c=c: h.scalar_tensor_tensor(out=os_.ap, in0=os_.ap, scalar=vecs[:, gcol0 + c:gcol0 + c + 1], in1=rstd[:, :], op0=ALU.mult, op1=ALU.mult), deps=[ld2, rstd_tok])
                    t_add = R.op("dve", lambda h, os_=os_, hs=hs: h.tensor_tensor(out=hs.ap, in0=hs.ap, in1=os_.ap, op=ALU.add), deps=[t_n, ld1])
                    os_.free = t_add
                    st = R.dma(lambda h, hs=hs, c=c: h.dma_start(out=hdst_d[c * 128:(c + 1) * 128, :], in_=hs.ap), hs.sem, deps=[t_add])
                    hs.free = st
                    if want_stats:
                        sq = sqs[c % 2]
                        t_sq = R.op("act", lambda h, hs=hs, sq=sq: h.activation(out=sq.ap, in_=hs.ap, func=AF.Square), deps=[t_add, sq.free])
                        for tt in range(NT):
                            last = R.op("pe", lambda h, tt=tt, sq=sq, c=c: h.matmul(ps[tt][:, :], lhsT=ones_f[:, :], rhs=sq.ap[:, tt * 512:(tt + 1) * 512], start=(c == 0), stop=(c == DC - 1)),
                                        deps=[t_sq, bank_free[tt], rstd_tok] if c == 0 else [t_sq], sig=(tt == NT - 1))
                        sq.free = last
                        hs.free = [st, t_sq]
                        if gnext is not None:
                            t_u = R.op("act", lambda h, hs=hs, c=c: h.activation(out=uT[:, c, :], in_=hs.ap, func=AF.Identity, scale=vecs[:, gnext + c:gnext + c + 1]), deps=[t_add])
                            hs.free = [st, t_sq, t_u]
                if want_stats:
                    t_fin = finish_rstd_after(last, t_add)
                    if gnext is not None:
                        for c in range(DC):
                            R.op("dve", lambda h, c=c: h.tensor_tensor(out=uT[:, c, :], in0=uT[:, c, :], in1=rstd[:, :], op=ALU.mult), deps=[t_fin, t_u])
                R.barrier()

            def finish_rstd_after(tok, guard):
                t2 = None
                for tt in range(NT):
                    dst = rstd[:, tt * 512:(tt + 1) * 512]
                    t1 = R.op("act", lambda h, tt=tt, dst=dst: h.activation(out=dst, in_=ps[tt][:, :], func=AF.Ln, bias=eps_t[:, 0:1], scale=1.0 / D), deps=[tok, guard])
                    bank_free[tt] = t1
                    t2 = R.op("act", lambda h, dst=dst: h.activation(out=dst, in_=dst, func=AF.Exp, scale=-0.5), deps=[t1])
                return t2

            residual_pass(h_src, hT_d, V0 + 16, True, gnext=V0 + 32)

            srcg = w_gate[l].rearrange("(c p) n -> p c n", p=128)
            srcu = w_up[l].rearrange("(c p) n -> p c n", p=128)
            srcs = []
            for s in range(DFF // 256):
                srcs.append(srcg[:, :, s * 256:(s + 1) * 256])
                srcs.append(srcu[:, :, s * 256:(s + 1) * 256])
            ws = WStream(srcs, 16, 256, 2, nwb=4)
            obs = [Slot(R, mi_bf16(i * (T // 2), T)) for i in range(4)]
            sgb = [mi_f32(2 * T + i * 512, 512) for i in range(4)]
            sg_free = [None] * 4
            ws.load(0)
            ws.load(1)
            ws.cast(0)
            ws.cast(1)
            oi = 0
            nsg = 0
            for s2 in range(DFF // 256):
                sg_, su_ = 2 * s2, 2 * s2 + 1
                wg, gtok = ws.wb[sg_ % 4].ap, ws.C[sg_]
                wu, utok = ws.wb[su_ % 4].ap, ws.C[su_]
                ws.load(sg_ + 2)
                ws.load(su_ + 2)
                ws.cast(sg_ + 2)
                ws.cast(su_ + 2)
                lastmm = None
                for ch in range(2):
                    ob = obs[oi % 4]
                    oi += 1
                    evl = None
                    for tt in range(NT):
                        sl = slice(tt * 512, (tt + 1) * 512)
                        bg = get_bank([4, 5, 6, 7])
                        bu = get_bank([0, 1, 2, 3])
                        for kc in range(DC):
                            tg = R.op("pe", lambda h, bg=bg, kc=kc, sl=sl, wg=wg, ch=ch: h.matmul(ps[bg][:, :], lhsT=wg[:, kc, ch * 128:(ch + 1) * 128], rhs=uT[:, kc, sl], start=(kc == 0), stop=(kc == DC - 1)),
                                      deps=[gtok, bank_free[bg]] if kc == 0 else [], sig=(kc == DC - 1))
                        for kc in range(DC):
                            lastmm = R.op("pe", lambda h, bu=bu, kc=kc, sl=sl, wu=wu, ch=ch: h.matmul(ps[bu][:, :], lhsT=wu[:, kc, ch * 128:(ch + 1) * 128], rhs=uT[:, kc, sl], start=(kc == 0), stop=(kc == DC - 1)),
                                          deps=[utok, bank_free[bu]] if kc == 0 else [], sig=(kc == DC - 1))
                        sg = nsg % 4
                        nsg += 1
                        t_s = R.op("act", lambda h, bg=bg, sg=sg: h.activation(out=sgb[sg], in_=ps[bg][:, :], func=AF.Silu), deps=[tg, sg_free[sg]])
                        bank_free[bg] = t_s
                        evl = R.op("dve", lambda h, bu=bu, sg=sg, ob=ob, sl=sl: h.tensor_tensor(out=ob.ap[:, sl], in0=ps[bu][:, :], in1=sgb[sg], op=ALU.mult), deps=[lastmm, t_s, ob.free])
                        bank_free[bu] = evl
                        sg_free[sg] = evl
                    r0 = s2 * 256 + ch * 128
                    ob.free = R.dma(lambda h, ob=ob, r0=r0: h.dma_start(out=fT_d[r0:r0 + 128, :], in_=ob.ap), ob.sem, deps=[evl])
                ws.wb[sg_ % 4].free = lastmm
                ws.wb[su_ % 4].free = lastmm
            R.barrier()

            fT = None
            for hf in range(NHALF):
                t0 = hf * HALF
                nA = (16 * T) // HALF
                def fT_ap(j, lo, hi, nA=nA):
                    if j < nA:
                        return A[:, j * HALF + lo:j * HALF + hi]
                    return XA[:, (j - nA) * HALF + lo:(j - nA) * HALF + hi]
                fsl = Slot(R, None)
                fl = []
                for j in range(FC):
                    fl.append(R.dma(lambda h, j=j, t0=t0: h.dma_start(out=fT_ap(j, 0, HALF), in_=fT_d[j * 128:(j + 1) * 128, t0:t0 + HALF]), fsl.sem if j % 2 == 0 else sem_misc[3]))
                ws = WStream([w_down[l, s_].rearrange("p (j c) -> p j c", j=FC) for s_ in range(DC)], FC, 128, 2, cast_eng="act")
                obs = [Slot(R, mi_f32(i * HALF, HALF)) for i in range(2)]
                sqs = [Slot(R, mi_f32(2 * T + i * 512, 512)) for i in range(2)]
                ws.start()
                nsq = 0
                ssq_st = {"last": None}
                pend = []
                for s in range(DC):
                    wb, wtok = ws.begin(s)
                    ob = obs[s % 2]
                    evl = None
                    lastmm = None
                    for t2 in range(HT):
                        b = get_bank([4, 5, 6, 7])
                        for j in range(FC):
                            lastmm = R.op("pe", lambda h, b=b, j=j, t2=t2, wb=wb: h.matmul(ps[b][:, :], lhsT=wb[:, j, :], rhs=fT_ap(j, t2 * 512, (t2 + 1) * 512), start=(j == 0), stop=(j == FC - 1)),
                                          deps=([wtok, bank_free[b]] + fl) if j == 0 else [], sig=(j == FC - 1))
                        for fn_ in pend:
                            fn_()
                        pend = []
                        sl = slice(t2 * 512, (t2 + 1) * 512)
                        evl = R.op("act", lambda h, b=b, ob=ob, sl=sl: h.activation(out=ob.ap[:, sl], in_=ps[b][:, :], func=AF.Copy), deps=[lastmm, ob.free])
                        bank_free[b] = evl
                        sq = sqs[nsq % 2]
                        nsq += 1
                        tq = R.op("dve", lambda h, sq=sq, ob=ob, sl=sl: h.tensor_tensor(out=sq.ap, in0=ob.ap[:, sl], in1=ob.ap[:, sl], op=ALU.mult), deps=[evl, sq.free])
                        first = (s == 0)
                        lastc = (s == DC - 1)
                        bs = hf * HT + t2

                        def ssq_(bs=bs, sq=sq, first=first, lastc=lastc, tq=tq):
                            t_ = R.op("pe", lambda h: h.matmul(ps[bs][:, :], lhsT=ones_f[:, :], rhs=sq.ap, start=first, stop=lastc, skip_group_check=True), deps=[tq, bank_free[bs]] if first else [tq])
                            sq.free = t_
                            ssq_st["last"] = t_
                        pend.append(ssq_)
                    ob.free = R.dma(lambda h, ob=ob, s=s, t0=t0: h.dma_start(out=oT_d[s * 128:(s + 1) * 128, t0:t0 + HALF], in_=ob.ap), ob.sem, deps=[evl, tq])
                    ws.end(s, lastmm)
                for fn_ in pend:
                    fn_()
                finish_rstd(ssq_st["last"], D, [hf * HT + t2 for t2 in range(HT)], t0)
                R.barrier()

            residual_pass(hT_d, h_dst, V0 + 48, not last_layer, gnext=None if last_layer else V0 + VL)

        for l in range(NL):
            layer(l, xT if l == 0 else hT_d)

        R.replay(nc)
    return nc


def _fm(v, n):
    return np.ascontiguousarray(v.reshape(n, 128).T)


def _pack_vecs(inp, layers):
    cols = []
    for l in layers:
        cols += [_fm(inp["g_pre_mix"][l], 16), _fm(inp["g_post_mix"][l], 16), _fm(inp["g_pre_ffn"][l], 16), _fm(inp["g_post_ffn"][l], 16),
                 _fm(inp["g_attn_grp"][l], 8), _fm(inp["g_conv_grp"][l], 4), _fm(inp["g_lru_grp"][l], 4),
                 _fm(inp["dw_conv_b"][l], 4), _fm(inp["conv_ln_g"][l], 4), _fm(inp["conv_ln_b"][l], 4),
                 _fm(inp["lru_conv_b"][l], 4), _fm(inp["lru_b_a"][l], 4), _fm(inp["lru_b_i"][l], 4), _fm(inp["lru_lambda"][l], 4)]
        dw = inp["dw_conv_w"][l]
        cols.append(np.ascontiguousarray(dw.reshape(31, 4, 128).transpose(2, 1, 0).reshape(128, 124)))
        lw = inp["lru_conv_w"][l]
        cols.append(np.ascontiguousarray(lw.reshape(4, 4, 128).transpose(2, 1, 0).reshape(128, 16)))
    return np.ascontiguousarray(np.concatenate(cols, axis=1).astype(np.float32))


def _pack_lruw(inp, layers):
    out = []
    for l in layers:
        wa = inp["lru_w_a"][l]
        wi = inp["lru_w_i"][l]
        both = np.concatenate([wa, wi], axis=0)
        out.append(both.transpose(1, 0, 2).reshape(128, 8 * 128))
    return np.ascontiguousarray(np.concatenate(out, axis=1).astype(np.float32))


def _consts():
    k = np.arange(128)
    ident = np.eye(128, dtype=np.float32)
    tri = (k[:, None] >= k[None, :]).astype(np.float32)
    fix = (k[:, None] < k[None, :]).astype(np.float32)
    c = np.arange(896)
    mb = np.where(k[:, None] >= (c[None, :] - 384), NEG, 0.0).astype(np.float32)
    return np.ascontiguousarray(np.concatenate([ident, tri, fix, mb], axis=1))


_NC_CACHE = {}


def run_layers(hT_list, inp, layers, T, dbg=False):
    key = (T, len(layers), dbg)
    if key not in _NC_CACHE:
        _NC_CACHE[key] = build(T, len(layers), dbg)
    nc = _NC_CACHE[key]
    shared = {
        "w_in": np.ascontiguousarray(inp["w_in"][layers]),
        "w_out": np.ascontiguousarray(inp["w_out"][layers]),
        "w_gate": np.ascontiguousarray(inp["w_gate"][layers]),
        "w_up": np.ascontiguousarray(inp["w_up"][layers]),
        "w_down": np.ascontiguousarray(inp["w_down"][layers].reshape(len(layers), FC, 128, DC, 128).transpose(0, 3, 2, 1, 4)).reshape(len(layers), DC, 128, FC * 128),
        "vecs": _pack_vecs(inp, layers),
        "lruw": _pack_lruw(inp, layers),
        "cst": _consts(),
    }
    in_maps = [dict(shared, xT=np.ascontiguousarray(h)) for h in hT_list]
    res = run_bass_kernel_spmd(nc, in_maps, core_ids=list(range(len(in_maps))))
    return res.results


FUSED = True


def kernel(**inputs):
    inp = {k: np.asarray(v) for k, v in inputs.items()}
    x = inp["x"]
    B, S, _ = x.shape
    NCORE = B
    hT = [np.ascontiguousarray(x[b % B].T) for b in range(NCORE)]
    if FUSED:
        res = run_layers(hT, inp, list(range(4)), S)
    else:
        for l in range(4):
            res = run_layers(hT, inp, [l], S)
            hT = [res[b]["outT"] for b in range(NCORE)]
    out = np.stack([np.ascontiguousarray(res[b]["outT"].T) for b in range(B)], axis=0)
    return out.astype(np.float32)
```
